# Optimizing a Trainium2 kernel written in Bass

```python
import math
import jax, jax.numpy as jnp
from jax import lax
import numpy as np

D_MODEL = 1024
BATCH = 8
SEQ = 4096
DEPTH = 2

N_MIXERS = 2
SSM_GROUP = 16
SSM_GROUPS = D_MODEL // SSM_GROUP
SSM_STATE = 64
DT_MIN = 1e-3
DT_MAX = 1e-1
SSM_C_STD = 0.5
SB_HEADS = 16
SB_HEAD_DIM = D_MODEL // SB_HEADS
Q_BLOCK = 128
FFN_HIDDEN = -(-8 * D_MODEL // (3 * 256)) * 256
DEEPNORM_ALPHA = (2 * DEPTH) ** 0.25
DEEPNORM_BETA = (8 * DEPTH) ** -0.25
LN_EPS = 1e-5
N_SSM_LAYERS = (DEPTH + 1) // 2
N_SB_LAYERS = DEPTH // 2

kernel_name = "interleaved_s5_stickbreaking_deepnorm"


def layer_norm(x, g, b):
    xf = x.astype(jnp.float32)
    mu = jnp.mean(xf, axis=-1, keepdims=True)
    var = jnp.mean(jnp.square(xf - mu), axis=-1, keepdims=True)
    y = (xf - mu) * lax.rsqrt(var + LN_EPS) * g.astype(jnp.float32) + b.astype(jnp.float32)
    return y.astype(x.dtype)


def _cmul(ar, ai, br, bi):
    return ar * br - ai * bi, ar * bi + ai * br


def _ssm_combine(left, right):
    alr, ali, blr, bli = left
    arr, ari, brr, bri = right
    ar, ai = _cmul(arr, ari, alr, ali)
    br, bi = _cmul(arr, ari, blr, bli)
    return ar, ai, br + brr, bi + bri


def s5_mixer(x, a_re, a_im, log_dt, b_re, b_im, c_re, c_im, d_skip, w_glu, w_out):
    bsz, seq, _ = x.shape
    f32 = jnp.float32
    u = x.astype(f32).reshape(bsz, seq, SSM_GROUPS, SSM_GROUP)
    dt = jnp.exp(log_dt.astype(f32))[:, None]
    lam_re = jnp.minimum(a_re.astype(f32), -1e-4)
    lam_im = a_im.astype(f32)
    mag = jnp.exp(lam_re * dt)
    lbar_re = mag * jnp.cos(lam_im * dt)
    lbar_im = mag * jnp.sin(lam_im * dt)
    den = lam_re * lam_re + lam_im * lam_im
    num_re = lbar_re - 1.0
    num_im = lbar_im
    coef_re = (num_re * lam_re + num_im * lam_im) / den
    coef_im = (num_im * lam_re - num_re * lam_im) / den
    bb_re, bb_im = _cmul(coef_re[..., None], coef_im[..., None],
                         b_re.astype(f32), b_im.astype(f32))
    bu_re = jnp.einsum('blgh,gph->lbgp', u, bb_re)
    bu_im = jnp.einsum('blgh,gph->lbgp', u, bb_im)
    a_seq_re = jnp.broadcast_to(lbar_re, (seq, 1, SSM_GROUPS, SSM_STATE))
    a_seq_im = jnp.broadcast_to(lbar_im, (seq, 1, SSM_GROUPS, SSM_STATE))
    _, _, s_re, s_im = lax.associative_scan(
        _ssm_combine, (a_seq_re, a_seq_im, bu_re, bu_im), axis=0)
    y = (jnp.einsum('lbgp,ghp->blgh', s_re, c_re.astype(f32))
         - jnp.einsum('lbgp,ghp->blgh', s_im, c_im.astype(f32)))
    y = y.reshape(bsz, seq, D_MODEL) + d_skip.astype(f32) * x.astype(f32)
    h = jax.nn.gelu(y).astype(x.dtype)
    val, gate = jnp.split(h @ w_glu, 2, axis=-1)
    z = val * jax.nn.sigmoid(gate)
    return (z @ w_out).astype(x.dtype)


def stick_breaking_mixer(x, w_qkv, w_o):
    bsz, seq, _ = x.shape
    f32 = jnp.float32
    q, k, v = jnp.split(x @ w_qkv, 3, axis=-1)
    to_heads = lambda t: t.reshape(bsz, seq, SB_HEADS, SB_HEAD_DIM).transpose(0, 2, 1, 3)
    q, k, v = to_heads(q), to_heads(k), to_heads(v)
    scale = SB_HEAD_DIM ** -0.5
    outs = []
    for blk in range(seq // Q_BLOCK):
        q0 = blk * Q_BLOCK
        kv_len = q0 + Q_BLOCK
        qb = q[:, :, q0:kv_len].astype(f32)
        kb = k[:, :, :kv_len].astype(f32)
        vb = v[:, :, :kv_len]
        z = jnp.einsum('bhqd,bhkd->bhqk', qb, kb) * scale
        t_idx = q0 + jnp.arange(Q_BLOCK)[:, None]
        s_idx = jnp.arange(kv_len)[None, :]
        past = s_idx < t_idx
        log_beta = jax.nn.log_sigmoid(z)
        log_1m_beta = jnp.where(past, jax.nn.log_sigmoid(-z), 0.0)
        between = lax.cumsum(log_1m_beta, axis=3, reverse=True) - log_1m_beta
        weights = jnp.where(past, jnp.exp(log_beta + between), 0.0)
        outs.append(jnp.einsum('bhqk,bhkd->bhqd', weights.astype(vb.dtype), vb))
    o = jnp.concatenate(outs, axis=2)
    o = o.transpose(0, 2, 1, 3).reshape(bsz, seq, D_MODEL)
    return (o @ w_o).astype(x.dtype)


def swiglu_ffn(x, w_gu, w_down):
    g, u = jnp.split(x @ w_gu, 2, axis=-1)
    return (jax.nn.silu(g) * u) @ w_down


def setup_inputs(seed: int = 0) -> dict:
    key = jax.random.key(seed)
    ks = jax.random.split(key, 24)
    f32 = jnp.float32
    nrm = lambda k, shape, std: jax.random.normal(k, shape, f32) * std
    G, P, H, D, F = SSM_GROUPS, SSM_STATE, SSM_GROUP, D_MODEL, FFN_HIDDEN
    x = jax.random.normal(ks[0], (BATCH, SEQ, D), f32)
    ssm_a_re = -0.5 + nrm(ks[1], (N_SSM_LAYERS, G, P), 0.01)
    ssm_a_im = (math.pi * jnp.arange(P, dtype=f32))[None, None, :] + nrm(ks[2], (N_SSM_LAYERS, G, P), 0.01)
    ssm_log_dt = jax.random.uniform(ks[3], (N_SSM_LAYERS, G), f32,
                                    minval=math.log(DT_MIN), maxval=math.log(DT_MAX))
    ssm_b_re = nrm(ks[4], (N_SSM_LAYERS, G, P, H), (2 * H) ** -0.5)
    ssm_b_im = nrm(ks[5], (N_SSM_LAYERS, G, P, H), (2 * H) ** -0.5)
    ssm_c_re = nrm(ks[6], (N_SSM_LAYERS, G, H, P), SSM_C_STD)
    ssm_c_im = nrm(ks[7], (N_SSM_LAYERS, G, H, P), SSM_C_STD)
    ssm_d = nrm(ks[8], (N_SSM_LAYERS, D), 1.0)
    ssm_w_glu = nrm(ks[9], (N_SSM_LAYERS, D, 2 * D), D ** -0.5)
    ssm_w_out = nrm(ks[10], (N_SSM_LAYERS, D, D), D ** -0.5 * DEEPNORM_BETA)
    sb_w_qkv = nrm(ks[11], (N_SB_LAYERS, D, 3 * D), D ** -0.5)
    sb_w_o = nrm(ks[12], (N_SB_LAYERS, D, D), D ** -0.5 * DEEPNORM_BETA)
    ffn_w_gu = nrm(ks[13], (DEPTH, D, 2 * F), D ** -0.5)
    ffn_w_down = nrm(ks[14], (DEPTH, F, D), F ** -0.5 * DEEPNORM_BETA)
    ln_mix_g = 1.0 + nrm(ks[15], (DEPTH, D), 0.02)
    ln_mix_b = nrm(ks[16], (DEPTH, D), 0.02)
    ln_ffn_g = 1.0 + nrm(ks[17], (DEPTH, D), 0.02)
    ln_ffn_b = nrm(ks[18], (DEPTH, D), 0.02)
    return {"x": x,
            "ssm_a_re": ssm_a_re, "ssm_a_im": ssm_a_im, "ssm_log_dt": ssm_log_dt,
            "ssm_b_re": ssm_b_re, "ssm_b_im": ssm_b_im,
            "ssm_c_re": ssm_c_re, "ssm_c_im": ssm_c_im, "ssm_d": ssm_d,
            "ssm_w_glu": ssm_w_glu, "ssm_w_out": ssm_w_out,
            "sb_w_qkv": sb_w_qkv, "sb_w_o": sb_w_o,
            "ffn_w_gu": ffn_w_gu, "ffn_w_down": ffn_w_down,
            "ln_mix_g": ln_mix_g, "ln_mix_b": ln_mix_b,
            "ln_ffn_g": ln_ffn_g, "ln_ffn_b": ln_ffn_b}


def reference(x, ssm_a_re, ssm_a_im, ssm_log_dt, ssm_b_re, ssm_b_im, ssm_c_re, ssm_c_im,
              ssm_d, ssm_w_glu, ssm_w_out, sb_w_qkv, sb_w_o, ffn_w_gu, ffn_w_down,
              ln_mix_g, ln_mix_b, ln_ffn_g, ln_ffn_b):
    h = x
    for i in range(DEPTH):
        j = i // N_MIXERS
        if i % N_MIXERS == 0:
            mix = s5_mixer(h, ssm_a_re[j], ssm_a_im[j], ssm_log_dt[j], ssm_b_re[j], ssm_b_im[j],
                           ssm_c_re[j], ssm_c_im[j], ssm_d[j], ssm_w_glu[j], ssm_w_out[j])
        else:
            mix = stick_breaking_mixer(h, sb_w_qkv[j], sb_w_o[j])
        h = layer_norm(DEEPNORM_ALPHA * h + mix, ln_mix_g[i], ln_mix_b[i])
        h = layer_norm(DEEPNORM_ALPHA * h + swiglu_ffn(h, ffn_w_gu[i], ffn_w_down[i]),
                       ln_ffn_g[i], ln_ffn_b[i])
    return h
```

```python
import math
from contextlib import ExitStack
import numpy as np
import concourse.bass as bass
import concourse.mybir as mybir
from concourse.bass_utils import run_bass_kernel_spmd

F32 = mybir.dt.float32
BF16 = mybir.dt.bfloat16
ALU = mybir.AluOpType
AF = mybir.ActivationFunctionType

D = 1024
L = 4096
FH = 2816
NG = 64
ALPHA = 4.0 ** 0.25
EPS = 1e-5
PI = math.pi
MAGIC = 12582912.0
GELU_C = 1.5957691216057308
KV = [float(k) for k in range(-7, 9)] + [float(7 - m) for m in range(8)] + [float(8 * j) for j in range(2, 9)]
L8IDX = [15] + list(range(24, 31))
NKV = len(KV)


class Sem:
    def __init__(self, h):
        self.h = h
        self.v = 0


class Buf:
    def __init__(self, name=""):
        self.name = name
        self.w = {}
        self.r = {}


class Eng:
    def __init__(self, name, eng, sem, is_pe=False):
        self.name = name
        self.eng = eng
        self.sem = sem
        self.waited = {}
        self.is_pe = is_pe

    def wait(self, s, v):
        if v <= 0:
            return
        if self.waited.get(s, 0) >= v:
            return
        self.eng.wait_ge(s.h, v)
        self.waited[s] = v


class Prog:
    def __init__(self, nc, es):
        self.nc = nc
        self.es = es
        self.nsem = 0
        self.allsems = []
        self.pe = Eng("pe", nc.tensor, self.newsem("pe"), True)
        self.act = Eng("act", nc.scalar, self.newsem("act"))
        self.dve = Eng("dve", nc.vector, self.newsem("dve"))
        self.pool = Eng("pool", nc.gpsimd, self.newsem("pool"))
        self.sp = Eng("sp", nc.sync, self.newsem("sp"))
        self.engs = [self.pe, self.act, self.dve, self.pool, self.sp]

    def newsem(self, name):
        self.nsem += 1
        s = Sem(self.es.enter_context(self.nc.semaphore("s%d_%s" % (self.nsem, name))))
        self.allsems.append(s)
        return s

    def _deps(self, E, reads, writes):
        for b in reads:
            for s, v in b.w.items():
                if E.is_pe and s is E.sem:
                    continue
                E.wait(s, v)
        for b in writes:
            for s, v in list(b.w.items()) + list(b.r.items()):
                if E.is_pe and s is E.sem:
                    continue
                E.wait(s, v)

    def op(self, E, fn, reads=(), writes=(), inc=True):
        self._deps(E, reads, writes)
        inst = fn(E.eng)
        if inc:
            E.sem.v += 1
            inst.then_inc(E.sem.h, 1)
            val = E.sem.v
        else:
            val = E.sem.v + 1
        for b in reads:
            b.r[E.sem] = max(b.r.get(E.sem, 0), val)
        for b in writes:
            b.w = {E.sem: val}
            b.r = {}
        return inst

    def dma(self, Q, out, in_, dsem, reads=(), writes=(), **kw):
        self._deps(Q, reads, writes)
        inst = Q.eng.dma_start(out=out, in_=in_, **kw)
        dsem.v += 16
        inst.then_inc(dsem.h, 16)
        for b in reads:
            b.r[dsem] = max(b.r.get(dsem, 0), dsem.v)
        for b in writes:
            b.w = {dsem: dsem.v}
            b.r = {}
        return inst

    def barrier(self):
        for E in self.engs:
            for s in self.allsems:
                if s is E.sem:
                    continue
                E.wait(s, s.v)


def build_program():
    nc = bass.Bass("TRN2", target_bir_lowering=False)
    dram = {}

    def din(name, shape):
        dram[name] = nc.dram_tensor(name, list(shape), F32, kind="ExternalInput").ap()
        return dram[name]

    x = din("x", [L, D])
    a_re = din("ssm_a_re", [NG, 64]); a_im = din("ssm_a_im", [NG, 64]); log_dt = din("ssm_log_dt", [NG])
    b_re = din("ssm_b_re", [NG, 64, 16]); b_im = din("ssm_b_im", [NG, 64, 16])
    c_re = din("ssm_c_re", [NG, 16, 64]); c_im = din("ssm_c_im", [NG, 16, 64])
    d_skip = din("ssm_d", [D])
    w_glu = din("ssm_w_glu", [D, 2 * D]); w_out = din("ssm_w_out", [D, D])
    w_qkv = din("sb_w_qkv", [D, 3 * D]); w_o = din("sb_w_o", [D, D])
    w_gu = [din("ffn_w_gu%d" % i, [D, 2 * FH]) for i in range(2)]
    w_dn = [din("ffn_w_down%d" % i, [FH, D]) for i in range(2)]
    lnp = {}
    for nm in ("ln_mix_g", "ln_mix_b", "ln_ffn_g", "ln_ffn_b"):
        for i in range(2):
            lnp[(nm, i)] = din("%s%d" % (nm, i), [D])
    y = nc.dram_tensor("y", [L, D], F32, kind="ExternalOutput").ap()

    def dscr(name, shape, dt):
        return nc.dram_tensor(name, list(shape), dt, kind="Internal").ap()

    wb_glu = dscr("wb_glu", [D, 2 * D], BF16); wb_out = dscr("wb_out", [D, D], BF16)
    wb_qkv = dscr("wb_qkv", [D, 3 * D], BF16); wb_o = dscr("wb_o", [D, D], BF16)
    wb_gu = [dscr("wb_gu%d" % i, [D, 2 * FH], BF16) for i in range(2)]
    wb_dn = [dscr("wb_dn%d" % i, [FH, D], BF16) for i in range(2)]
    HTd = dscr("HTd", [D, L], BF16)
    x2d = dscr("x2d", [L, D], F32)
    attTd = dscr("attTd", [D, L], BF16)

    top = ExitStack()
    with top:
        P = Prog(nc, top)
        pe, act, dve, pool, sp = P.pe, P.act, P.dve, P.pool, P.sp

        uniq = [0]

        def sbt(es, name, shape, dt):
            uniq[0] += 1
            return es.enter_context(nc.sbuf_tensor("%s_%d" % (name, uniq[0]), list(shape), dt))

        def pst(es, name, shape, dt):
            return es.enter_context(nc.psum_tensor(name, list(shape), dt))

        ident_f = sbt(top, "ident_f", [128, 128], F32)
        dummy = sbt(top, "dummy", [128, 8], F32)
        B_dummy = Buf("dummy")
        ident_b = sbt(top, "ident_b", [128, 128], BF16)
        B_const = Buf("const")
        P.op(pool, lambda e: e.memset(ident_f[:], 1.0), writes=[B_const])
        P.op(pool, lambda e: e.affine_select(out=ident_f[:], in_=ident_f[:], pattern=[[1, 128]], base=0,
                                             channel_multiplier=-1, compare_op=ALU.is_equal, fill=0.0),
             writes=[B_const])
        P.op(dve, lambda e: e.tensor_copy(out=ident_b[:], in_=ident_f[:]), reads=[B_const], writes=[B_const])

        NB = 8
        psf = [pst(top, "psb%d" % i, [128, 512], F32) for i in range(NB)]
        psb = [Buf("ps%d" % i) for i in range(NB)]
        ps_rr = [0]

        po_rr = [0]

        NROT = [6]

        def next_ps(long=False):
            if long:
                i = 6 + po_rr[0] % 2
                po_rr[0] += 1
            else:
                i = ps_rr[0] % NROT[0]
                ps_rr[0] += 1
            return psf[i], psb[i]

        WB = {}

        def cast_weight(key, src, dst, rows, cols):
            s = P.newsem("wc")
            b = Buf("wb_" + key)
            for r0 in range(0, rows, 128):
                for c0 in range(0, cols, 2048):
                    c1 = min(cols, c0 + 2048)
                    P.dma(pool, dst[r0:r0 + 128, c0:c1], src[r0:r0 + 128, c0:c1], s)
            b.w = {s: s.v}
            WB[key] = b


        B_HTd = Buf("HTd")
        B_x2d = Buf("x2d")
        B_attTd = Buf("attTd")
        B_y = Buf("y")
        s_o = P.newsem("so")

        sS = ExitStack()
        AJ1 = sbt(sS, "AJ1", [128, 8, 2, 32], F32); AJ2 = sbt(sS, "AJ2", [128, 8, 2, 32], F32)
        WyR = sbt(sS, "WyR", [128, 32, 128], BF16); WyI = sbt(sS, "WyI", [128, 32, 128], BF16)
        Kb = sbt(sS, "Kb", [128, NG, 128], BF16)
        WzR = sbt(sS, "WzR", [128, NG, 64], BF16); WzI = sbt(sS, "WzI", [128, NG, 64], BF16)
        with ExitStack() as es:
            s_ld = P.newsem("sld")
            are = sbt(es, "are", [128, 32], F32); aim = sbt(es, "aim", [128, 32], F32)
            dtl = sbt(es, "dtl", [128, 32], F32)
            Bre = sbt(es, "Bre", [128, 32, 16], F32); Bim = sbt(es, "Bim", [128, 32, 16], F32)
            Cre = sbt(es, "Cre", [128, 32, 16], F32); Cim = sbt(es, "Cim", [128, 32, 16], F32)
            Tc = [sbt(es, "Tc%d" % i, [128, 4, 2, 64], F32) for i in range(2)]
            dsk = sbt(es, "dsk", [128, 64], F32)
            B_par = Buf("par")
            for g2 in range(2):
                sl = slice(64 * g2, 64 * g2 + 64)
                gs = slice(32 * g2, 32 * g2 + 32)
                P.dma(sp, are[sl, :], a_re[gs, :].rearrange("g p -> p g"), s_ld, writes=[B_par], allow_slow_non_contiguous=True)
                P.dma(sp, aim[sl, :], a_im[gs, :].rearrange("g p -> p g"), s_ld, writes=[B_par], allow_slow_non_contiguous=True)
                P.dma(sp, dtl[sl, :], log_dt[gs].partition_broadcast(64), s_ld, writes=[B_par])
                P.dma(sp, Bre[sl, :, :], b_re[gs, :, :].rearrange("g p h -> p g h"), s_ld, writes=[B_par])
                P.dma(sp, Bim[sl, :, :], b_im[gs, :, :].rearrange("g p h -> p g h"), s_ld, writes=[B_par])
            for ri_, csrc in enumerate((c_re, c_im)):
                cv = csrc.rearrange("(g2 gt gp8) h p -> (gp8 h) gt g2 p", g2=2, gt=4)
                for g2 in range(2):
                    P.dma(sp, Tc[ri_][:, :, g2, :], cv[:, :, g2, :], s_ld, writes=[B_par])
            for m in range(8):
                P.dma(sp, dsk[16 * m:16 * m + 16, :], d_skip.rearrange("(g h) -> h g", h=16), s_ld, writes=[B_par],
                      allow_slow_non_contiguous=True)
            B_par.w = {s_ld: s_ld.v}
            pool.wait(s_ld, s_ld.v)
            cast_weight("glu", w_glu, wb_glu, D, 2 * D)
            cast_weight("out", w_out, wb_out, D, D)
            cast_weight("gu0", w_gu[0], wb_gu[0], D, 2 * FH)
            cast_weight("dn0", w_dn[0], wb_dn[0], FH, D)

            B_C = Buf("C")
            for ri, (T, Cd) in enumerate(((Tc[0], Cre), (Tc[1], Cim))):
                pt, pb = next_ps()
                for gt in range(4):
                    P.op(pe, lambda e, gt=gt, T=T, pt=pt: e.transpose(
                        out=pt[:, gt * 128:(gt + 1) * 128], in_=T[:, gt, :, :].rearrange("q a p -> q (a p)"),
                        identity=ident_f[:]), reads=[B_par, B_const], writes=[pb], inc=(gt == 3))
                P.op(dve, lambda e, Cd=Cd, pt=pt: e.tensor_copy(
                    out=Cd[:].rearrange("q g h -> q (g h)"), in_=pt[:, :]), reads=[pb], writes=[B_C])

            B_t = Buf("tab")

            def dv(fn, reads=(B_par,), writes=None):
                return P.op(dve, fn, reads=list(reads) + [B_t, B_C], writes=[B_t] if writes is None else writes)

            dt_ = sbt(es, "dt_", [128, 32], F32); lre = sbt(es, "lre", [128, 32], F32)
            xr = sbt(es, "xr", [128, 32], F32); xi = sbt(es, "xi", [128, 32], F32)
            P.op(act, lambda e: e.activation(out=dt_[:], in_=dtl[:], func=AF.Exp), reads=[B_par], writes=[B_t])
            dv(lambda e: e.tensor_scalar(out=lre[:], in0=are[:], scalar1=-1e-4, scalar2=None, op0=ALU.min))
            dv(lambda e: e.tensor_tensor(out=xr[:], in0=lre[:], in1=dt_[:], op=ALU.mult))
            dv(lambda e: e.tensor_tensor(out=xi[:], in0=aim[:], in1=dt_[:], op=ALU.mult))
            kv = sbt(es, "kv", [128, NKV], F32)
            for i, k in enumerate(KV):
                dv(lambda e, i=i, k=k: e.memset(kv[:, i:i + 1], k))
            T3 = [128, 32, NKV]
            ang = sbt(es, "ang", T3, F32); expo = sbt(es, "expo", T3, F32); mag = sbt(es, "mag", T3, F32)
            kk = sbt(es, "kk", T3, F32); sn = sbt(es, "sn", T3, F32); cs = sbt(es, "cs", T3, F32)
            PwR = sbt(es, "PwR", T3, F32); PwI = sbt(es, "PwI", T3, F32)
            kvb = kv[:, :].unsqueeze(1).broadcast_to(T3)
            dv(lambda e: e.tensor_tensor(out=ang[:], in0=xi[:, :].unsqueeze(2).broadcast_to(T3), in1=kvb, op=ALU.mult))
            dv(lambda e: e.tensor_tensor(out=expo[:], in0=xr[:, :].unsqueeze(2).broadcast_to(T3), in1=kvb, op=ALU.mult))
            P.op(act, lambda e: e.activation(out=mag[:], in_=expo[:], func=AF.Exp), reads=[B_t], writes=[B_t])

            def range_reduce(dst, src, shift):
                if shift != 0.0:
                    dv(lambda e: e.tensor_scalar(out=dst[:], in0=src[:], scalar1=shift, scalar2=None, op0=ALU.add))
                    s2 = dst
                else:
                    s2 = src
                dv(lambda e: e.tensor_scalar(out=kk[:], in0=s2[:], scalar1=1.0 / (2 * PI), scalar2=MAGIC, op0=ALU.mult, op1=ALU.add))
                dv(lambda e: e.tensor_scalar(out=kk[:], in0=kk[:], scalar1=MAGIC, scalar2=None, op0=ALU.subtract))
                dv(lambda e: e.scalar_tensor_tensor(out=dst[:], in0=kk[:], scalar=-2 * PI, in1=s2[:], op0=ALU.mult, op1=ALU.add))

            range_reduce(sn, ang, 0.0)
            P.op(act, lambda e: e.activation(out=sn[:], in_=sn[:], func=AF.Sin), reads=[B_t], writes=[B_t])
            range_reduce(cs, ang, PI / 2)
            P.op(act, lambda e: e.activation(out=cs[:], in_=cs[:], func=AF.Sin), reads=[B_t], writes=[B_t])
            dv(lambda e: e.tensor_tensor(out=PwR[:], in0=mag[:], in1=cs[:], op=ALU.mult))
            dv(lambda e: e.tensor_tensor(out=PwI[:], in0=mag[:], in1=sn[:], op=ALU.mult))
            t_a = sbt(es, "t_a", [128, 32], F32); t_b = sbt(es, "t_b", [128, 32], F32)
            nr = sbt(es, "nr", [128, 32], F32); den = sbt(es, "den", [128, 32], F32)
            cr = sbt(es, "cr", [128, 32], F32); ci = sbt(es, "ci", [128, 32], F32)
            LR = PwR[:, :, 8]; LI = PwI[:, :, 8]
            dv(lambda e: e.tensor_scalar(out=nr[:], in0=LR, scalar1=-1.0, scalar2=None, op0=ALU.add))
            dv(lambda e: e.tensor_tensor(out=den[:], in0=lre[:], in1=lre[:], op=ALU.mult))
            dv(lambda e: e.tensor_tensor(out=t_a[:], in0=aim[:], in1=aim[:], op=ALU.mult))
            dv(lambda e: e.tensor_tensor(out=den[:], in0=den[:], in1=t_a[:], op=ALU.add))
            dv(lambda e: e.reciprocal(out=den[:], in_=den[:]))
            dv(lambda e: e.tensor_tensor(out=t_a[:], in0=nr[:], in1=lre[:], op=ALU.mult))
            dv(lambda e: e.tensor_tensor(out=t_b[:], in0=LI, in1=aim[:], op=ALU.mult))
            dv(lambda e: e.tensor_tensor(out=t_a[:], in0=t_a[:], in1=t_b[:], op=ALU.add))
            dv(lambda e: e.tensor_tensor(out=cr[:], in0=t_a[:], in1=den[:], op=ALU.mult))
            dv(lambda e: e.tensor_tensor(out=t_a[:], in0=LI, in1=lre[:], op=ALU.mult))
            dv(lambda e: e.tensor_tensor(out=t_b[:], in0=nr[:], in1=aim[:], op=ALU.mult))
            dv(lambda e: e.tensor_tensor(out=t_a[:], in0=t_a[:], in1=t_b[:], op=ALU.subtract))
            dv(lambda e: e.tensor_tensor(out=ci[:], in0=t_a[:], in1=den[:], op=ALU.mult))
            T16 = [128, 32, 16]
            BbR = sbt(es, "BbR", T16, F32); BbI = sbt(es, "BbI", T16, F32)
            u1 = sbt(es, "u1", T16, F32)
            crb = cr[:, :].unsqueeze(2).broadcast_to(T16); cib = ci[:, :].unsqueeze(2).broadcast_to(T16)
            dv(lambda e: e.tensor_tensor(out=BbR[:], in0=Bre[:], in1=crb, op=ALU.mult))
            dv(lambda e: e.tensor_tensor(out=u1[:], in0=Bim[:], in1=cib, op=ALU.mult))
            dv(lambda e: e.tensor_tensor(out=BbR[:], in0=BbR[:], in1=u1[:], op=ALU.subtract))
            dv(lambda e: e.tensor_tensor(out=BbI[:], in0=Bim[:], in1=crb, op=ALU.mult))
            dv(lambda e: e.tensor_tensor(out=u1[:], in0=Bre[:], in1=cib, op=ALU.mult))
            dv(lambda e: e.tensor_tensor(out=BbI[:], in0=BbI[:], in1=u1[:], op=ALU.add))
            for j8, ki in enumerate(L8IDX):
                for r in range(2):
                    dv(lambda e, r=r, j8=j8, ki=ki: e.tensor_copy(out=AJ1[:, j8, r, :], in_=PwR[:, :, ki]))
                dv(lambda e, j8=j8, ki=ki: e.tensor_scalar(out=AJ2[:, j8, 0, :], in0=PwI[:, :, ki], scalar1=-1.0, scalar2=None, op0=ALU.mult))
                dv(lambda e, j8=j8, ki=ki: e.tensor_copy(out=AJ2[:, j8, 1, :], in_=PwI[:, :, ki]))

            T4 = [128, 32, 8, 16]
            E7R = sbt(es, "E7R", T4, F32); E7I = sbt(es, "E7I", T4, F32)
            F0R = sbt(es, "F0R", T4, F32); F0I = sbt(es, "F0I", T4, F32)
            w1 = sbt(es, "w1", T4, F32)

            def cmul(outR, outI, k0, XR, XI, negI=False):
                pr = PwR[:, :, k0:k0 + 8].unsqueeze(3).broadcast_to(T4)
                pi_ = PwI[:, :, k0:k0 + 8].unsqueeze(3).broadcast_to(T4)
                xr_ = XR[:, :, :].unsqueeze(2).broadcast_to(T4)
                xi_ = XI[:, :, :].unsqueeze(2).broadcast_to(T4)
                dv(lambda e: e.tensor_tensor(out=outR, in0=pr, in1=xr_, op=ALU.mult))
                dv(lambda e: e.tensor_tensor(out=w1[:], in0=pi_, in1=xi_, op=ALU.mult))
                dv(lambda e: e.tensor_tensor(out=outR, in0=outR, in1=w1[:], op=ALU.subtract))
                dv(lambda e: e.tensor_tensor(out=outI, in0=pr, in1=xi_, op=ALU.mult))
                dv(lambda e: e.tensor_tensor(out=w1[:], in0=pi_, in1=xr_, op=ALU.mult))
                if negI:
                    dv(lambda e: e.scalar_tensor_tensor(out=outI, in0=outI, scalar=-1.0, in1=w1[:], op0=ALU.mult, op1=ALU.subtract))
                else:
                    dv(lambda e: e.tensor_tensor(out=outI, in0=outI, in1=w1[:], op=ALU.add))

            cmul(E7R[:], E7I[:], 16, BbR, BbI)
            cmul(F0R[:], F0I[:], 8, Cre, Cim, negI=True)
            dv(lambda e: e.tensor_copy(out=WyR[:].rearrange("q g (l h) -> q g l h", l=8), in_=F0R[:]))
            dv(lambda e: e.tensor_copy(out=WyI[:].rearrange("q g (l h) -> q g l h", l=8), in_=F0I[:]))
            cmul(F0R[:], F0I[:], 0, Cre, Cim, negI=True)

            maskK = sbt(es, "maskK", [128, 128], F32)
            P.op(pool, lambda e: e.memset(maskK[:], 1.0), writes=[B_const])
            P.op(pool, lambda e: e.affine_select(out=maskK[:].rearrange("q (l h) -> q l h", l=8),
                                                 in_=maskK[:].rearrange("q (l h) -> q l h", l=8),
                                                 pattern=[[16, 8], [0, 16]], base=15, channel_multiplier=-1,
                                                 compare_op=ALU.is_ge, fill=0.0), writes=[B_const])
            P.op(pool, lambda e: e.memset(dummy[:, 0:1], 0.0), writes=[B_dummy])
            ktmp = sbt(es, "ktmp", [128, 4, 128], F32)
            B_K = Buf("K")
            for gb in range(NG // 4):
                pt, pb = next_ps()
                for j in range(4):
                    g = gb * 4 + j
                    g2, gp = g // 32, g % 32
                    sl = slice(64 * g2, 64 * g2 + 64)
                    P.op(pe, lambda e, j=j, sl=sl, gp=gp, pt=pt: e.matmul(
                        out=pt[:, j * 128:(j + 1) * 128], lhsT=E7R[sl, gp, :, :].rearrange("q m h -> q (m h)"),
                        rhs=F0R[sl, gp, :, :].rearrange("q m h -> q (m h)"), start=True, stop=False),
                        reads=[B_t], writes=[pb], inc=False)
                    P.op(pe, lambda e, j=j, sl=sl, gp=gp, pt=pt: e.matmul(
                        out=pt[:, j * 128:(j + 1) * 128], lhsT=E7I[sl, gp, :, :].rearrange("q m h -> q (m h)"),
                        rhs=F0I[sl, gp, :, :].rearrange("q m h -> q (m h)"), start=False, stop=True),
                        reads=[B_t], writes=[pb], inc=(j == 3))
                P.op(dve, lambda e, pt=pt: e.tensor_tensor(
                    out=ktmp[:], in0=pt[:, :].rearrange("q (j n) -> q j n", j=4),
                    in1=maskK[:, :].unsqueeze(1).broadcast_to([128, 4, 128]), op=ALU.mult),
                    reads=[pb, B_const], writes=[B_K])
                for j in range(4):
                    g = gb * 4 + j
                    P.op(dve, lambda e, j=j, g=g: e.scalar_tensor_tensor(
                        out=Kb[:, g, :], in0=ident_f[:], scalar=dsk[:, g:g + 1], in1=ktmp[:, j, :],
                        op0=ALU.mult, op1=ALU.add), reads=[B_K, B_par, B_const], writes=[B_K])
            for ri, (E7, Wz) in enumerate(((E7R, WzR), (E7I, WzI))):
                for gb in range(NG // 8):
                    pt, pb = next_ps()
                    for j in range(8):
                        g = gb * 8 + j
                        g2, gp = g // 32, g % 32
                        sl = slice(64 * g2, 64 * g2 + 64)
                        P.op(pe, lambda e, j=j, sl=sl, gp=gp, pt=pt, E7=E7: e.transpose(
                            out=pt[:, j * 64:(j + 1) * 64], in_=E7[sl, gp, :, :].rearrange("q m h -> q (m h)"),
                            identity=ident_f[sl, sl]), reads=[B_t, B_const], writes=[pb], inc=(j == 7))
                    P.op(dve, lambda e, pt=pt, Wz=Wz, gb=gb: e.tensor_copy(
                        out=Wz[:, gb * 8:(gb + 1) * 8, :].rearrange("q g p -> q (g p)"), in_=pt[:, :]),
                        reads=[pb], writes=[B_K])
            P.barrier()

        with ExitStack() as es:
            BIG = sbt(es, "BIG", [128, 8192], F32)
            Xg = sbt(es, "Xg", [128, 64, 128], BF16)
            U = sbt(es, "U", [128, 64, 128], BF16)
            carry = sbt(es, "carry", [128, 2, 32], F32)
            HTc = sbt(es, "HTc", [128, 8, 1024], BF16)
            HTs = sbt(es, "HTs", [128, 8, 1024], BF16)
            tq0 = [sbt(es, "tq0_%d" % i, [128, 512], F32) for i in range(4)]
            ysb = [sbt(es, "ysb%d" % i, [128, 512], F32) for i in range(4)]
            B_tq0 = [Buf() for _ in range(4)]; B_ysb = [Buf() for _ in range(4)]
            hT = [sbt(es, "hT%d" % i, [128, 512], BF16) for i in range(2)]
            st1 = sbt(es, "st1", [128, 2, 32], F32); st2 = sbt(es, "st2", [128, 2, 32], F32)
            sA = sbt(es, "sA", [128, 2, 32, 16], F32); sB = sbt(es, "sB", [128, 2, 32, 16], F32)
            CinT = sbt(es, "CinT", [128, 2, 32, 16], F32)
            B_cin = Buf("cin")
            B_BIG = Buf("BIG"); B_Xg = Buf("Xg"); B_U = Buf("U"); B_car = Buf("carry")
            B_HTc = Buf("HTc"); B_HTs = Buf("HTs"); B_st = Buf("st")
            B_tq = [Buf() for _ in range(4)]; B_hT = [Buf() for _ in range(2)]
            s_x = P.newsem("sx"); s_ht = P.newsem("sht")
            Xc4 = BIG[:, :].rearrange("q (m g h) -> q g m h", m=8, g=64)
            Z4 = BIG[:, :].rearrange("q (r g c) -> q r g c", r=2, g=32)
            Sb4 = Xg[:].rearrange("q (r g) c -> q r g c", r=2)
            P.op(dve, lambda e: e.memset(carry[:], 0.0), writes=[B_car])
            for seg in range(4):
                P.dma(sp, BIG[:, :], x[seg * 1024:(seg + 1) * 1024, :].rearrange("(c m) d -> c (m d)", m=8), s_x,
                      writes=[B_BIG])
                if seg == 1:
                    pool.wait(s_x, s_x.v)
                    cast_weight("qkv", w_qkv, wb_qkv, D, 3 * D)
                    cast_weight("o", w_o, wb_o, D, D)
                    cast_weight("gu1", w_gu[1], wb_gu[1], D, 2 * FH)
                    cast_weight("dn1", w_dn[1], wb_dn[1], FH, D)
                for half in range(2):
                    E = act if half == 0 else dve
                    gs = slice(32 * half, 32 * half + 32)
                    if half == 0:
                        P.op(act, lambda e, gs=gs: e.activation(
                            out=Xg[:, gs, :].rearrange("q g (m h) -> q g m h", m=8), in_=Xc4[:, gs, :, :], func=AF.Copy),
                            reads=[B_BIG], writes=[B_Xg])
                    else:
                        P.op(dve, lambda e, gs=gs: e.tensor_copy(
                            out=Xg[:, gs, :].rearrange("q g (m h) -> q g m h", m=8), in_=Xc4[:, gs, :, :]),
                            reads=[B_BIG], writes=[B_Xg])
                for gb in range(16):
                    pt, pb = next_ps()
                    ptb = pt[:, :].bitcast(BF16)
                    for j in range(4):
                        g = gb * 4 + j
                        P.op(pe, lambda e, j=j, g=g, ptb=ptb: e.transpose(
                            out=ptb[:, j * 128:(j + 1) * 128], in_=Xg[:, g, :], identity=ident_b[:]),
                            reads=[B_Xg, B_const], writes=[pb], inc=(j == 3))
                    E = act if gb % 2 == 0 else dve
                    if gb % 2 == 0:
                        P.op(act, lambda e, gb=gb, ptb=ptb: e.activation(
                            out=U[:, gb * 4:(gb + 1) * 4, :].rearrange("q g c -> q (g c)"), in_=ptb[:, 0:512], func=AF.Copy),
                            reads=[pb], writes=[B_U])
                    else:
                        P.op(dve, lambda e, gb=gb, ptb=ptb: e.tensor_copy(
                            out=U[:, gb * 4:(gb + 1) * 4, :].rearrange("q g c -> q (g c)"), in_=ptb[:, 0:512]),
                            reads=[pb], writes=[B_U])
                for gpb in range(8):
                    for ri, Wz in enumerate((WzR, WzI)):
                        pt, pb = next_ps()
                        for j in range(4):
                            gp = gpb * 4 + j
                            for g2 in range(2):
                                g = g2 * 32 + gp
                                P.op(pe, lambda e, j=j, g2=g2, g=g, pt=pt, Wz=Wz: e.matmul(
                                    out=pt[64 * g2:64 * g2 + 64, j * 128:(j + 1) * 128], lhsT=Wz[:, g, :], rhs=U[:, g, :],
                                    start=True, stop=True), reads=[B_U, B_K], writes=[pb], inc=(j == 3 and g2 == 1))
                        if ri == 0:
                            P.op(act, lambda e, pt=pt, gpb=gpb, ri=ri: e.activation(
                                out=Z4[:, ri, gpb * 4:(gpb + 1) * 4, :].rearrange("q g c -> q (g c)"), in_=pt[:, :], func=AF.Copy),
                                reads=[pb, B_Xg], writes=[B_BIG])
                        else:
                            P.op(dve, lambda e, pt=pt, gpb=gpb, ri=ri: e.tensor_copy(
                                out=Z4[:, ri, gpb * 4:(gpb + 1) * 4, :].rearrange("q g c -> q (g c)"), in_=pt[:, :]),
                                reads=[pb, B_Xg], writes=[B_BIG])
                P.op(dve, lambda e: e.tensor_copy(out=Sb4[:, :, :, 0], in_=carry[:]), reads=[B_car, B_Xg, B_BIG], writes=[B_Xg])
                T4s = [128, 2, 32, 16]

                def cstep(dst, prev, prev_sw, j8, add_to, big):
                    a1 = AJ1[:, j8, :, :]
                    a2 = AJ2[:, j8, :, :]
                    ta, tb = (sA, sB) if big else (st1, st2)
                    if big:
                        a1 = a1.unsqueeze(3).broadcast_to(T4s)
                        a2 = a2.unsqueeze(3).broadcast_to(T4s)
                    P.op(dve, lambda e: e.tensor_tensor(out=ta[:], in0=a1, in1=prev, op=ALU.mult),
                         reads=[B_BIG, B_car, B_t, B_cin], writes=[B_st])
                    P.op(dve, lambda e: e.tensor_tensor(out=tb[:], in0=a2, in1=prev_sw, op=ALU.mult),
                         reads=[B_BIG, B_car, B_st, B_cin], writes=[B_st])
                    P.op(dve, lambda e: e.tensor_tensor(out=ta[:], in0=ta[:], in1=tb[:], op=ALU.add),
                         reads=[B_st], writes=[B_st])
                    return ta

                for j in range(1, 8):
                    ta = cstep(None, Z4[:, :, :, j - 1:128:8], Z4[:, ::-1, :, j - 1:128:8], 0, None, True)
                    P.op(dve, lambda e, j=j, ta=ta: e.tensor_tensor(out=Z4[:, :, :, j:128:8], in0=Z4[:, :, :, j:128:8], in1=ta[:], op=ALU.add),
                         reads=[B_st], writes=[B_BIG])
                P.op(dve, lambda e: e.tensor_copy(out=CinT[:, :, :, 0], in_=carry[:]), reads=[B_car], writes=[B_cin])
                for b in range(15):
                    ta = cstep(None, CinT[:, :, :, b], CinT[:, ::-1, :, b], 7, None, False)
                    P.op(dve, lambda e, b=b, ta=ta: e.tensor_tensor(out=CinT[:, :, :, b + 1], in0=Z4[:, :, :, b * 8 + 7], in1=ta[:], op=ALU.add),
                         reads=[B_st, B_BIG], writes=[B_cin])
                for j in range(8):
                    ta = cstep(None, CinT[:, :, :, :], CinT[:, ::-1, :, :], j, None, True)
                    P.op(dve, lambda e, j=j, ta=ta: e.tensor_tensor(out=Z4[:, :, :, j:128:8], in0=Z4[:, :, :, j:128:8], in1=ta[:], op=ALU.add),
                         reads=[B_st], writes=[B_BIG])
                P.op(dve, lambda e: e.tensor_copy(out=carry[:], in_=Z4[:, :, :, 127]), reads=[B_BIG], writes=[B_car])
                P.op(act, lambda e: e.activation(out=Sb4[:, :, :, 1:128], in_=Z4[:, :, :, 0:127], func=AF.Copy),
                     reads=[B_BIG], writes=[B_Xg])
                NQ = 4
                Ysl = {}

                def G0(gb):
                    pt, pb = next_ps()
                    Ysl[gb] = (pt, pb)
                    for j in range(4):
                        g = gb * 4 + j
                        g2, gp = g // 32, g % 32
                        sl = slice(64 * g2, 64 * g2 + 64)
                        o = pt[:, j * 128:(j + 1) * 128]
                        P.op(pe, lambda e, o=o, g=g: e.matmul(out=o, lhsT=Kb[:, g, :], rhs=U[:, g, :], start=True, stop=False),
                             reads=[B_U, B_K], writes=[pb], inc=False)
                        P.op(pe, lambda e, o=o, sl=sl, gp=gp: e.matmul(out=o, lhsT=WyR[sl, gp, :], rhs=Sb4[sl, 0, gp, :], start=False, stop=False),
                             reads=[B_Xg, B_t], writes=[pb], inc=False)
                        P.op(pe, lambda e, o=o, sl=sl, gp=gp: e.matmul(out=o, lhsT=WyI[sl, gp, :], rhs=Sb4[sl, 1, gp, :], start=False, stop=True),
                             reads=[B_Xg, B_t], writes=[pb], inc=(j == 3))

                def G1(gb):
                    pt, pb = Ysl[gb]
                    q = gb % NQ
                    P.op(act, lambda e: e.activation(out=ysb[q][:], in_=pt[:, :], func=AF.Copy), reads=[pb], writes=[B_ysb[q]])
                    P.op(act, lambda e: e.activation(out=tq0[q][:], in_=pt[:, :], func=AF.Square), reads=[pb], writes=[B_tq0[q]])

                def G2(gb):
                    q = gb % NQ
                    P.op(dve, lambda e: e.tensor_scalar(out=tq0[q][:], in0=tq0[q][:], scalar1=0.044715, scalar2=1.0, op0=ALU.mult, op1=ALU.add),
                         reads=[B_tq0[q]], writes=[B_tq0[q]])
                    P.op(dve, lambda e: e.tensor_tensor(out=tq0[q][:], in0=ysb[q][:], in1=tq0[q][:], op=ALU.mult),
                         reads=[B_tq0[q], B_ysb[q]], writes=[B_tq0[q]])

                def G3(gb):
                    q = gb % NQ
                    P.op(act, lambda e: e.activation(out=tq0[q][:], in_=tq0[q][:], func=AF.Sigmoid, scale=GELU_C), reads=[B_tq0[q]], writes=[B_tq0[q]])

                def G4(gb):
                    q = gb % NQ
                    k = gb % 2
                    P.op(dve, lambda e: e.tensor_tensor(out=hT[k][:], in0=ysb[q][:], in1=tq0[q][:], op=ALU.mult),
                         reads=[B_tq0[q], B_ysb[q]], writes=[B_hT[k]])

                def G5(gb):
                    k = gb % 2
                    pt2, pb2 = next_ps()
                    Ysl[("t", gb)] = (pt2, pb2)
                    pt2b = pt2[:, :].bitcast(BF16)
                    for j in range(4):
                        P.op(pe, lambda e, j=j: e.transpose(
                            out=pt2b[:, j * 128:(j + 1) * 128], in_=hT[k][:, j * 128:(j + 1) * 128], identity=ident_b[:]),
                            reads=[B_hT[k], B_const], writes=[pb2], inc=(j == 3))

                def G6(gb):
                    pt2, pb2 = Ysl[("t", gb)]
                    pt2b = pt2[:, :].bitcast(BF16)
                    P.op(act, lambda e: e.activation(
                        out=HTc[:, :, gb * 64:(gb + 1) * 64].rearrange("q l (g h) -> q l g h", g=4),
                        in_=pt2b[:, 0:512].rearrange("q (g l h) -> q l g h", g=4, l=8), func=AF.Copy),
                        reads=[pb2], writes=[B_HTc])

                gst = [G0, G1, G2, G3, G4, G5, G6]
                for tick in range(16 + len(gst) - 1):
                    for si, fn in enumerate(gst):
                        gb = tick - si
                        if 0 <= gb < 16:
                            fn(gb)
                for dt_i in range(8):
                    for lh in range(2):
                        pt, pb = next_ps()
                        ptb = pt[:, :].bitcast(BF16)
                        for j in range(4):
                            l = lh * 4 + j
                            P.op(pe, lambda e, j=j, l=l, dt_i=dt_i, ptb=ptb: e.transpose(
                                out=ptb[:, j * 128:(j + 1) * 128], in_=HTc[:, l, dt_i * 128:(dt_i + 1) * 128], identity=ident_b[:]),
                                reads=[B_HTc, B_const], writes=[pb], inc=(j == 3))
                        if lh == 0:
                            P.op(act, lambda e, dt_i=dt_i, lh=lh, ptb=ptb: e.activation(
                                out=HTs[:, dt_i, lh * 512:(lh + 1) * 512], in_=ptb[:, 0:512], func=AF.Copy),
                                reads=[pb], writes=[B_HTs])
                        else:
                            P.op(dve, lambda e, dt_i=dt_i, lh=lh, ptb=ptb: e.tensor_copy(
                                out=HTs[:, dt_i, lh * 512:(lh + 1) * 512], in_=ptb[:, 0:512]),
                                reads=[pb], writes=[B_HTs])
                P.dma(sp, HTd[:, seg * 1024:(seg + 1) * 1024].rearrange("(dt q) t -> q dt t", q=128), HTs[:, :, :], s_ht,
                      reads=[B_HTs], writes=[B_HTd])
            P.barrier()
        sS.close()

        def dense_phase(layer):
            with ExitStack() as es:
                NWS = 6
                wslot = [sbt(es, "wslot%d" % i, [128, 11, 512], BF16) for i in range(NWS)]
                wsem = [P.newsem("ws%d" % i) for i in range(NWS)]
                wbuf = [Buf("wslot%d" % i) for i in range(NWS)]
                w_rr = [0]

                def wload(src_ap, nkt, wb_buf, ncols=512):
                    i = w_rr[0] % NWS
                    w_rr[0] += 1
                    Q = sp
                    P.dma(Q, wslot[i][:, 0:nkt, 0:ncols], src_ap.rearrange("(kt q) n -> q kt n", q=128), wsem[i],
                          reads=[wb_buf], writes=[wbuf[i]])
                    return wslot[i], wbuf[i]

                lnt = {k: sbt(es, "lnt_%s" % k, [128, D], F32) for k in ("mg", "mb", "fg", "fb")}
                B_ln = Buf("ln")
                s_ln = P.newsem("sln")
                xt = sbt(es, "xt", [128, 4, D], F32)
                rr_ = sbt(es, "rr_", [128, 4, D], F32)
                xin = sbt(es, "xin", [128, 4, D], F32)
                xnb = [sbt(es, "xnb%d" % i, [128, D], BF16) for i in range(2)]
                inT = sbt(es, "inT", [128, 8, 512], BF16)
                zT = sbt(es, "zT", [128, 8, 512], BF16)
                x1T = sbt(es, "x1T", [128, 8, 512], BF16)
                h2T = sbt(es, "h2T", [128, 22, 512], BF16)
                sgt = [sbt(es, "sgt%d" % i, [128, 512], F32) for i in range(2)]
                stats = [sbt(es, "stats%d" % i, [128, 2, 6], F32) for i in range(4)]
                mv = [sbt(es, "mv%d" % i, [128, 2], F32) for i in range(4)]
                rstd = [sbt(es, "rstd%d" % i, [128, 1], F32) for i in range(4)]
                nmr = [sbt(es, "nmr%d" % i, [128, 1], F32) for i in range(4)]
                B_xt = [Buf("xt%d" % i) for i in range(4)]; B_rr = [Buf("rr%d" % i) for i in range(4)]
                B_xin = [Buf("xin%d" % i) for i in range(4)]
                B_xnb = [Buf("xnb0"), Buf("xnb1")]; B_inT = Buf("inT"); B_zT = Buf("zT")
                B_x1T = Buf("x1T"); B_h2T = Buf("h2T"); B_sg = [Buf(), Buf()]; B_stat = [Buf("stat%d" % i) for i in range(4)]
                xnb_rr = [0]
                s_xt = P.newsem("sxt"); s_in = P.newsem("sin")
                sg_rr = [0]
                pending = []

                def flush_pending():
                    while pending:
                        pending.pop(0)()

                gT = sbt(es, "gT", [128, 8], F32); bT = sbt(es, "bT", [128, 8], F32)

                def load_ln(layer):
                    for k, nm in (("mg", "ln_mix_g"), ("mb", "ln_mix_b"), ("fg", "ln_ffn_g"), ("fb", "ln_ffn_b")):
                        P.dma(sp, lnt[k][:, :], lnp[(nm, layer)].partition_broadcast(128), s_ln, writes=[B_ln])
                    P.dma(sp, gT[:, :], lnp[("ln_mix_g", layer)].rearrange("(dt q) -> q dt", q=128), s_ln, writes=[B_ln],
                          allow_slow_non_contiguous=True)
                    P.dma(sp, bT[:, :], lnp[("ln_mix_b", layer)].rearrange("(dt q) -> q dt", q=128), s_ln, writes=[B_ln],
                          allow_slow_non_contiguous=True)

                def fm_gated(src_T, B_src, wsrc, wkey, nkt, n_half, func, dst_T, B_dst):
                    ncol = n_half * 128
                    for c0 in range(0, n_half, 4):
                        nt = min(4, n_half - c0)
                        sa, ba = wload(wsrc[:, c0 * 128:(c0 + nt) * 128], nkt, WB[wkey], ncols=nt * 128)
                        sb_, bb = wload(wsrc[:, ncol + c0 * 128:ncol + (c0 + nt) * 128], nkt, WB[wkey], ncols=nt * 128)
                        for j in range(nt):
                            pa, pba = next_ps()
                            pb_, pbb = next_ps()
                            for (slot, sbuf, pt, pbuf) in ((sa, ba, pa, pba), (sb_, bb, pb_, pbb)):
                                for kt in range(nkt):
                                    P.op(pe, lambda e, slot=slot, pt=pt, kt=kt, j=j: e.matmul(
                                        out=pt[:, :], lhsT=slot[:, kt, j * 128:(j + 1) * 128], rhs=src_T[:, kt, :],
                                        start=(kt == 0), stop=(kt == nkt - 1)),
                                        reads=[sbuf, B_src], writes=[pbuf], inc=(kt == nkt - 1))
                            i = sg_rr[0] % 2
                            sg_rr[0] += 1
                            if func == "glu":
                                gate_ps, gate_b, oth_ps, oth_b = pb_, pbb, pa, pba
                                f = AF.Sigmoid
                            else:
                                gate_ps, gate_b, oth_ps, oth_b = pa, pba, pb_, pbb
                                f = AF.Silu
                            P.op(act, lambda e, i=i, gate_ps=gate_ps, f=f: e.activation(out=sgt[i][:], in_=gate_ps[:, :], func=f),
                                 reads=[gate_b], writes=[B_sg[i]])
                            P.op(dve, lambda e, i=i, oth_ps=oth_ps, c0=c0, j=j: e.tensor_tensor(
                                out=dst_T[:, c0 + j, :], in0=oth_ps[:, :], in1=sgt[i][:], op=ALU.mult),
                                reads=[B_sg[i], oth_b], writes=[B_dst])
                            if pending:
                                pending.pop(0)()

                def tm_proj_resid(src_T, B_src, wsrc, wkey, nkt, res, B_res):
                    khalves = [(0, nkt)] if nkt <= 11 else [(0, 11), (11, nkt)]
                    for ch in range(2):
                        pts = [next_ps() for _ in range(4)]
                        for hi, (k0, k1) in enumerate(khalves):
                            slot, sbuf = wload(wsrc[k0 * 128:k1 * 128, ch * 512:(ch + 1) * 512], k1 - k0, WB[wkey])
                            for st in range(4):
                                pt, pbuf = pts[st]
                                for kt in range(k0, k1):
                                    last = (kt == nkt - 1)
                                    P.op(pe, lambda e, slot=slot, pt=pt, kt=kt, k0=k0, st=st, last=last: e.matmul(
                                        out=pt[:, :], lhsT=src_T[:, kt, st * 128:(st + 1) * 128], rhs=slot[:, kt - k0, :],
                                        start=(kt == 0), stop=last),
                                        reads=[sbuf, B_src], writes=[pbuf], inc=(kt == k1 - 1))
                        flush_pending()
                        for st in range(4):
                            pt, pbuf = pts[st]
                            P.op(dve, lambda e, st=st, ch=ch, pt=pt: e.scalar_tensor_tensor(
                                out=rr_[:, st, ch * 512:(ch + 1) * 512], in0=res[:, st, ch * 512:(ch + 1) * 512], scalar=ALPHA,
                                in1=pt[:, :], op0=ALU.mult, op1=ALU.add), reads=[pbuf, B_res[st]], writes=[B_rr[st]])

                def layer_norm(gk, bk, dstT, B_dstT, store_rows=None, store_buf=None, store_sem=None):
                    R4 = range(4)
                    for st in R4:
                        for ch in range(2):
                            P.op(dve, lambda e, st=st, ch=ch: e.bn_stats(out=stats[st][:, ch, :], in_=rr_[:, st, ch * 512:(ch + 1) * 512]),
                                 reads=[B_rr[st]], writes=[B_stat[st]])
                    for st in R4:
                        P.op(dve, lambda e, st=st: e.bn_aggr(out=mv[st][:, :], in_=stats[st][:, :, :].rearrange("q a b -> q (a b)")),
                             reads=[B_stat[st]], writes=[B_stat[st]])
                    for st in R4:
                        P.op(dve, lambda e, st=st: e.tensor_scalar(out=rstd[st][:], in0=mv[st][:, 1:2], scalar1=EPS, scalar2=None, op0=ALU.add),
                             reads=[B_stat[st]], writes=[B_stat[st]])
                    for st in R4:
                        P.op(act, lambda e, st=st: e.activation(out=rstd[st][:], in_=rstd[st][:], func=AF.Sqrt), reads=[B_stat[st]], writes=[B_stat[st]])
                    for st in R4:
                        P.op(dve, lambda e, st=st: e.reciprocal(out=rstd[st][:], in_=rstd[st][:]), reads=[B_stat[st]], writes=[B_stat[st]])
                    for st in R4:
                        P.op(dve, lambda e, st=st: e.scalar_tensor_tensor(out=nmr[st][:], in0=mv[st][:, 0:1], scalar=-1.0, in1=rstd[st][:], op0=ALU.mult, op1=ALU.mult),
                             reads=[B_stat[st]], writes=[B_stat[st]])
                    if dstT is not None:
                        tps = {}
                        for st in R4:
                            xi_ = st % 2
                            P.op(act, lambda e, st=st, xi_=xi_: e.activation(out=xnb[xi_][:, :], in_=rr_[:, st, :], func=AF.Identity,
                                                                            scale=rstd[st][:, 0:1], bias=nmr[st][:, 0:1]),
                                 reads=[B_stat[st], B_rr[st]], writes=[B_xnb[xi_]])
                            for hh in range(2):
                                pt, pbuf = next_ps()
                                tps[(st, hh)] = (pt, pbuf)
                                ptb = pt[:, :].bitcast(BF16)
                                for j in range(4):
                                    dtile = hh * 4 + j
                                    P.op(pe, lambda e, j=j, dtile=dtile, ptb=ptb, xi_=xi_: e.transpose(
                                        out=ptb[:, j * 128:(j + 1) * 128], in_=xnb[xi_][:, dtile * 128:(dtile + 1) * 128], identity=ident_b[:]),
                                        reads=[B_xnb[xi_], B_const], writes=[pbuf], inc=(j == 3))
                            if st >= 1:
                                evac_T(st - 1, tps, dstT, B_dstT)
                        evac_T(3, tps, dstT, B_dstT)
                    def tail(st):
                        P.op(act, lambda e: e.activation(out=rr_[:, st, :], in_=rr_[:, st, :], func=AF.Identity, scale=rstd[st][:, 0:1], bias=nmr[st][:, 0:1]),
                             reads=[B_stat[st], B_rr[st]], writes=[B_rr[st]])
                        P.op(dve, lambda e: e.tensor_tensor(out=rr_[:, st, :], in0=rr_[:, st, :], in1=lnt[gk][:, :], op=ALU.mult),
                             reads=[B_rr[st], B_ln], writes=[B_rr[st]])
                        P.op(dve, lambda e: e.tensor_tensor(out=xt[:, st, :], in0=rr_[:, st, :], in1=lnt[bk][:, :], op=ALU.add),
                             reads=[B_rr[st], B_ln], writes=[B_xt[st]])
                        if store_rows is not None:
                            P.dma(pool, store_rows(st), xt[:, st, :], store_sem, reads=[B_xt[st]], writes=[store_buf])
                    for st in R4:
                        pending.append(lambda st=st: tail(st))

                def evac_T(st, tps, dstT, B_dstT):
                    for hh in range(2):
                        pt, pbuf = tps[(st, hh)]
                        ptb = pt[:, :].bitcast(BF16)
                        for j in range(4):
                            dtile = hh * 4 + j
                            P.op(dve, lambda e, j=j, dtile=dtile, ptb=ptb, st=st: e.tensor_scalar(
                                out=dstT[:, dtile, st * 128:(st + 1) * 128], in0=ptb[:, j * 128:(j + 1) * 128],
                                scalar1=gT[:, dtile:dtile + 1], scalar2=bT[:, dtile:dtile + 1], op0=ALU.mult, op1=ALU.add),
                                reads=[pbuf, B_ln], writes=[B_dstT])

                def ffn(layer):
                    fm_gated(x1T, B_x1T, wb_gu[layer], "gu%d" % layer, 8, 22, "swiglu", h2T, B_h2T)
                    tm_proj_resid(h2T, B_h2T, wb_dn[layer], "dn%d" % layer, 22, xt, B_xt)


                load_ln(layer)

                def rows(base, grp, st):
                    if layer == 0:
                        seg, lq = grp // 2, grp % 2
                        r0 = seg * 1024 + lq * 4 + st
                        return base[r0:r0 + 1017:8, :]
                    tt = grp * 4 + st
                    return base[tt * 128:(tt + 1) * 128, :]

                def load_inputs(grp):
                    src_fm = HTd if layer == 0 else attTd
                    B_srcfm = B_HTd if layer == 0 else B_attTd
                    P.dma(sp, inT[:, :, :], src_fm[:, grp * 512:(grp + 1) * 512].rearrange("(dt q) t -> q dt t", q=128), s_in,
                          reads=[B_srcfm], writes=[B_inT])
                    for st in range(4):
                        if layer == 0:
                            P.dma(sp, xin[:, st, :], rows(x, grp, st), s_xt, writes=[B_xin[st]])
                        else:
                            P.dma(sp, xin[:, st, :], rows(x2d, grp, st), s_xt, reads=[B_x2d], writes=[B_xin[st]])
                    for st in range(4):
                        B_xin[st].w = {s_xt: s_xt.v}

                load_inputs(0)
                for grp in range(8):
                    if layer == 0:
                        fm_gated(inT, B_inT, wb_glu, "glu", 8, 8, "glu", zT, B_zT)
                        tm_proj_resid(zT, B_zT, wb_out, "out", 8, xin, B_xin)
                    else:
                        tm_proj_resid(inT, B_inT, wb_o, "o", 8, xin, B_xin)
                    if grp + 1 < 8:
                        load_inputs(grp + 1)
                    layer_norm("mg", "mb", x1T, B_x1T)
                    ffn(layer)
                    if layer == 0:
                        layer_norm("fg", "fb", None, None, store_rows=lambda st, grp=grp: rows(x2d, grp, st), store_buf=B_x2d, store_sem=s_o)
                    else:
                        layer_norm("fg", "fb", None, None, store_rows=lambda st, grp=grp: rows(y, grp, st), store_buf=B_y, store_sem=s_o)
                flush_pending()
                P.barrier()

        dense_phase(0)

        with ExitStack() as ea:
            x2T = sbt(ea, "x2T", [128, 8, L], BF16)
            attT = [sbt(ea, "attT%d" % i, [128, L], BF16) for i in range(2)]
            B_x2T = Buf("x2T"); B_attT = [Buf("attT0"), Buf("attT1")]
            rr_ = sbt(ea, "rrA", [128, 2, D], F32); xnbA = [sbt(ea, "xnbA%d" % i, [128, D], BF16) for i in range(2)]
            B_rrs = [Buf("rrA0"), Buf("rrA1")]; B_xnbA = [Buf("xnbA0"), Buf("xnbA1")]
            s_xtA = [P.newsem("sxtA0"), P.newsem("sxtA1")]; s_at = P.newsem("sat")
            aw = [sbt(ea, "aw%d" % i, [128, 8, 128], BF16) for i in range(3)]
            awsem = [P.newsem("aw%d" % i) for i in range(3)]
            awbuf = [Buf("aw%d" % i) for i in range(3)]
            aw_rr = [0]

            def wload(src_ap, nkt, wb_buf, ncols=128):
                i = aw_rr[0] % 3
                aw_rr[0] += 1
                P.dma(sp, aw[i][:, 0:nkt, 0:ncols], src_ap.rearrange("(kt q) n -> q kt n", q=128), awsem[i],
                      reads=[wb_buf], writes=[awbuf[i]])
                return aw[i], awbuf[i]
            def X0(tt):
                st = tt % 2
                P.dma(sp, rr_[:, st, :], x2d[tt * 128:(tt + 1) * 128, :], s_xtA[st], reads=[B_x2d], writes=[B_rrs[st]])

            def X1(tt):
                st = tt % 2
                P.op(act, lambda e: e.activation(out=xnbA[st][:, :], in_=rr_[:, st, :], func=AF.Copy), reads=[B_rrs[st]], writes=[B_xnbA[st]])

            xps = {}

            def X2(tt):
                st = tt % 2
                for hh in range(2):
                    pt, pbuf = next_ps()
                    xps[(tt, hh)] = (pt, pbuf)
                    ptb = pt[:, :].bitcast(BF16)
                    for j in range(4):
                        dtile = hh * 4 + j
                        P.op(pe, lambda e, j=j, dtile=dtile, ptb=ptb: e.transpose(
                            out=ptb[:, j * 128:(j + 1) * 128], in_=xnbA[st][:, dtile * 128:(dtile + 1) * 128], identity=ident_b[:]),
                            reads=[B_xnbA[st], B_const], writes=[pbuf], inc=(j == 3))

            def X3(tt):
                for hh in range(2):
                    pt, pbuf = xps[(tt, hh)]
                    ptb = pt[:, :].bitcast(BF16)
                    E = dve if hh == 0 else act
                    if hh == 0:
                        P.op(dve, lambda e, hh=hh, ptb=ptb: e.tensor_copy(
                            out=x2T[:, hh * 4:(hh + 1) * 4, tt * 128:(tt + 1) * 128],
                            in_=ptb[:, 0:512].rearrange("q (j t) -> q j t", j=4)), reads=[pbuf], writes=[B_x2T])
                    else:
                        P.op(act, lambda e, hh=hh, ptb=ptb: e.activation(
                            out=x2T[:, hh * 4:(hh + 1) * 4, tt * 128:(tt + 1) * 128],
                            in_=ptb[:, 0:512].rearrange("q (j t) -> q j t", j=4), func=AF.Copy), reads=[pbuf], writes=[B_x2T])

            xst = [X0, X1, X2, X3]
            for tick in range(32 + len(xst) - 1):
                for si, fn in enumerate(xst):
                    tt = tick - si
                    if 0 <= tt < 32:
                        fn(tt)
            QT2 = [sbt(ea, "QT%d" % i, [128, L], BF16) for i in range(2)]
            KT2 = [sbt(ea, "KT%d" % i, [128, L], BF16) for i in range(2)]
            V2 = [sbt(ea, "V%d" % i, [128, 32, 128], BF16) for i in range(2)]
            B_QT2 = [Buf("QT0"), Buf("QT1")]; B_KT2 = [Buf("KT0"), Buf("KT1")]; B_V2 = [Buf("V0"), Buf("V1")]
            NBUF = 4
            SUBE = pool
            sig = [sbt(ea, "sig%d" % i, [128, 512], F32) for i in range(NBUF)]
            NIPX = 2
            ipx = [sbt(ea, "ipx%d" % i, [128, L + 4], F32) for i in range(NIPX)]
            wq = [sbt(ea, "wq%d" % i, [128, 512], BF16) for i in range(NBUF)]
            wTs = [sbt(ea, "wTs%d" % i, [128, 512], BF16) for i in range(NBUF)]
            B_sig = [Buf() for _ in range(NBUF)]; B_ipxc = [[Buf() for _ in range(10)] for _ in range(NIPX)]
            B_wq = [Buf() for _ in range(NBUF)]; B_wTs = [Buf() for _ in range(NBUF)]
            mlow = sbt(ea, "mlow", [128, 128], F32); mup = sbt(ea, "mup", [128, 128], F32)
            P.op(pool, lambda e: e.memset(mlow[:], 1.0), writes=[B_const])
            P.op(pool, lambda e: e.affine_select(out=mlow[:], in_=mlow[:], pattern=[[-1, 128]], base=0,
                                                 channel_multiplier=1, compare_op=ALU.is_gt, fill=0.0), writes=[B_const])
            P.op(pool, lambda e: e.memset(mup[:], 0.0), writes=[B_const])
            P.op(pool, lambda e: e.affine_select(out=mup[:], in_=mup[:], pattern=[[-1, 128]], base=0,
                                                 channel_multiplier=1, compare_op=ALU.is_gt, fill=1.0), writes=[B_const])
            negm = sbt(ea, "negm", [128, 128], BF16)
            P.op(pool, lambda e: e.memset(dummy[:, 0:1], 0.0), writes=[B_dummy])
            P.op(dve, lambda e: e.tensor_scalar(out=negm[:], in0=mup[:], scalar1=-30000.0, scalar2=None, op0=ALU.mult),
                 reads=[B_const], writes=[B_const])
            NROT[0] = 5
            bgp, bgb = psf[5], psb[5]

            def proj_gen(hp):
                pb_ = hp % 2
                QTd, KTd, Vd = QT2[pb_], KT2[pb_], V2[pb_]
                sq, bq = wload(wb_qkv[:, hp * 128:(hp + 1) * 128], 8, WB["qkv"], ncols=128)
                sk, bk_ = wload(wb_qkv[:, D + hp * 128:D + (hp + 1) * 128], 8, WB["qkv"], ncols=128)
                sv, bv = wload(wb_qkv[:, 2 * D + hp * 128:2 * D + (hp + 1) * 128], 8, WB["qkv"], ncols=128)
                for tb in range(8):
                    for (slot, sbuf, dstT, B_dstT, scl) in ((sq, bq, QTd, B_QT2[pb_], 0.125), (sk, bk_, KTd, B_KT2[pb_], 1.0)):
                        for kt in range(8):
                            P.op(pe, lambda e, slot=slot, kt=kt, tb=tb: e.matmul(
                                out=bgp[:, :], lhsT=slot[:, kt, 0:128], rhs=x2T[:, kt, tb * 512:(tb + 1) * 512],
                                start=(kt == 0), stop=(kt == 7)), reads=[sbuf, B_x2T], writes=[bgb], inc=(kt == 7))
                        yield
                        P.op(act, lambda e, dstT=dstT, tb=tb, scl=scl: e.activation(
                            out=dstT[:, tb * 512:(tb + 1) * 512], in_=bgp[:, :], func=AF.Copy, scale=scl),
                            reads=[bgb], writes=[B_dstT])
                        yield
                    for j in range(4):
                        kt_i = tb * 4 + j
                        for kt in range(8):
                            P.op(pe, lambda e, kt=kt, kt_i=kt_i: e.matmul(
                                out=bgp[:, 0:128], lhsT=x2T[:, kt, kt_i * 128:(kt_i + 1) * 128], rhs=sv[:, kt, 0:128],
                                start=(kt == 0), stop=(kt == 7)), reads=[bv, B_x2T], writes=[bgb], inc=(kt == 7))
                        yield
                        P.op(act, lambda e, kt_i=kt_i: e.activation(
                            out=Vd[:, kt_i, :], in_=bgp[:, 0:128], func=AF.Copy),
                            reads=[bgb], writes=[B_V2[pb_]])
                        yield

            for _ in proj_gen(0):
                pass
            for hp in range(8):
                QT, KT, V = QT2[hp % 2], KT2[hp % 2], V2[hp % 2]
                B_QT, B_KT, B_V = B_QT2[hp % 2], B_KT2[hp % 2], B_V2[hp % 2]
                bg = proj_gen(hp + 1) if hp + 1 < 8 else iter(())
                tasks = []
                for h2 in range(2):
                    for qt in range(32):
                        hi = 128 * (qt + 1)
                        chunks = []
                        while hi > 0:
                            lo = ((hi - 1) // 512) * 512
                            chunks.append((lo, hi))
                            hi = lo
                        poh = {}
                        for ci_, (lo, hi) in enumerate(chunks):
                            tasks.append(dict(h2=h2, qt=qt, ci=ci_, lo=lo, hi=hi, n=len(chunks), poh=poh, bi=(h2 * 32 + qt) % NIPX))

                def S0(k, T):
                    hs = slice(64 * T["h2"], 64 * T["h2"] + 64)
                    W = T["hi"] - T["lo"]
                    pz, pzb = next_ps()
                    T["pz"], T["pzb"] = pz, pzb
                    qt, lo, hi = T["qt"], T["lo"], T["hi"]
                    if T["ci"] == 0:
                        P.op(pe, lambda e: e.matmul(out=pz[:, 0:W], lhsT=QT[hs, qt * 128:(qt + 1) * 128], rhs=KT[hs, lo:hi], start=True, stop=False),
                             reads=[B_QT, B_KT], writes=[pzb], inc=False)
                        P.op(pe, lambda e: e.matmul(out=pz[:, W - 128:W], lhsT=ident_b[:], rhs=negm[:], start=False, stop=True),
                             reads=[B_const], writes=[pzb])
                    else:
                        P.op(pe, lambda e: e.matmul(out=pz[:, 0:W], lhsT=QT[hs, qt * 128:(qt + 1) * 128], rhs=KT[hs, lo:hi], start=True, stop=True),
                             reads=[B_QT, B_KT], writes=[pzb])

                def S1(k, T):
                    i = k % NBUF
                    W = T["hi"] - T["lo"]
                    pz, pzb = T["pz"], T["pzb"]
                    P.op(act, lambda e: e.activation(out=sig[i][:, 0:W], in_=pz[:, 0:W], func=AF.Sigmoid, scale=-1.0),
                         reads=[pzb], writes=[B_sig[i]])

                def S2(k, T):
                    i = k % NBUF
                    b = T["bi"]
                    lo, hi = T["lo"], T["hi"]
                    W = hi - lo
                    tk = B_ipxc[b][lo // 512]
                    tc = B_ipxc[b][hi // 512]
                    if T["ci"] == 0:
                        P.op(dve, lambda e: e.memset(ipx[b][:, hi:hi + 1], 1.0), writes=[tc])
                    P.op(dve, lambda e: e.tensor_tensor_scan(
                        out=ipx[b][:, lo:hi][:, ::-1], data0=sig[i][:, 0:W][:, ::-1], data1=sig[i][:, 0:W][:, ::-1],
                        initial=ipx[b][:, hi:hi + 1], op0=ALU.mult, op1=ALU.bypass), reads=[B_sig[i], tc], writes=[tk])
                    P.op(pool if (k % 8) != 7 else dve, lambda e: e.tensor_tensor(out=wq[i][:, 0:W], in0=ipx[b][:, lo + 1:hi + 1], in1=ipx[b][:, lo:hi], op=ALU.subtract),
                         reads=[tk, tc], writes=[B_wq[i]])

                def S3(k, T):
                    i = k % NBUF
                    W = T["hi"] - T["lo"]
                    nb = W // 128
                    pw, pwb = next_ps()
                    T["pw"], T["pwb"] = pw, pwb
                    pwb16 = pw[:, :].bitcast(BF16)
                    for j in range(nb):
                        P.op(pe, lambda e, j=j: e.transpose(
                            out=pwb16[:, j * 128:(j + 1) * 128], in_=wq[i][:, j * 128:(j + 1) * 128], identity=ident_b[:]),
                            reads=[B_wq[i], B_const], writes=[pwb], inc=(j == nb - 1))

                def S4(k, T):
                    i = k % NBUF
                    W = T["hi"] - T["lo"]
                    pwb16 = T["pw"][:, :].bitcast(BF16)
                    P.op(act, lambda e: e.activation(out=wTs[i][:, 0:W], in_=pwb16[:, 0:W], func=AF.Copy),
                         reads=[T["pwb"]], writes=[B_wTs[i]])

                def S5(k, T):
                    i = k % NBUF
                    hs = slice(64 * T["h2"], 64 * T["h2"] + 64)
                    W = T["hi"] - T["lo"]
                    nb = W // 128
                    if T["ci"] == 0:
                        T["poh"]["po"] = next_ps(long=True)
                    po, pob = T["poh"]["po"]
                    for j in range(nb):
                        ktile = T["lo"] // 128 + j
                        first = (T["ci"] == 0 and j == 0)
                        last = (T["ci"] == T["n"] - 1 and j == nb - 1)
                        P.op(pe, lambda e, j=j, ktile=ktile, first=first, last=last: e.matmul(
                            out=po[hs, 0:128], lhsT=V[:, ktile, hs], rhs=wTs[i][:, j * 128:(j + 1) * 128],
                            start=first, stop=last), reads=[B_wTs[i], B_V], writes=[pob], inc=(j == nb - 1))
                    if T["ci"] == T["n"] - 1:
                        qt = T["qt"]
                        P.op(act, lambda e: e.activation(
                            out=attT[hp % 2][hs, qt * 128:(qt + 1) * 128], in_=po[hs, 0:128], func=AF.Copy),
                            reads=[pob], writes=[B_attT[hp % 2]])

                stages = [S0, S1, S2, S3, S4, S5]
                offs = [0, 1, 3, 5, 6, 8]
                for tick in range(len(tasks) + offs[-1]):
                    for si, fn in enumerate(stages):
                        k = tick - offs[si]
                        if 0 <= k < len(tasks):
                            fn(k, tasks[k])
                    if tick % 3 == 1:
                        next(bg, None)
                for _ in bg:
                    pass
                P.dma(pool, attTd[hp * 128:(hp + 1) * 128, :], attT[hp % 2][:, :], s_at, reads=[B_attT[hp % 2]], writes=[B_attTd])
            P.barrier()


        NROT[0] = 6
        dense_phase(1)
        sp.wait(s_o, s_o.v)
    return nc


_NC = None


def kernel(**inputs):
    global _NC
    if _NC is None:
        _NC = build_program()
    nc = _NC
    f = lambda a: np.ascontiguousarray(np.asarray(a, dtype=np.float32))
    shared = {
        "ssm_a_re": f(inputs["ssm_a_re"][0]), "ssm_a_im": f(inputs["ssm_a_im"][0]),
        "ssm_log_dt": f(inputs["ssm_log_dt"][0]),
        "ssm_b_re": f(inputs["ssm_b_re"][0]), "ssm_b_im": f(inputs["ssm_b_im"][0]),
        "ssm_c_re": f(inputs["ssm_c_re"][0]), "ssm_c_im": f(inputs["ssm_c_im"][0]),
        "ssm_d": f(inputs["ssm_d"][0]),
        "ssm_w_glu": f(inputs["ssm_w_glu"][0]), "ssm_w_out": f(inputs["ssm_w_out"][0]),
        "sb_w_qkv": f(inputs["sb_w_qkv"][0]), "sb_w_o": f(inputs["sb_w_o"][0]),
    }
    for i in range(2):
        shared["ffn_w_gu%d" % i] = f(inputs["ffn_w_gu"][i])
        shared["ffn_w_down%d" % i] = f(inputs["ffn_w_down"][i])
        for nm in ("ln_mix_g", "ln_mix_b", "ln_ffn_g", "ln_ffn_b"):
            shared["%s%d" % (nm, i)] = f(inputs[nm][i])
    xs = np.asarray(inputs["x"], dtype=np.float32)
    in_maps = []
    for b in range(8):
        m = dict(shared)
        m["x"] = np.ascontiguousarray(xs[b])
        in_maps.append(m)
    res = run_bass_kernel_spmd(nc, in_maps, core_ids=list(range(8)))
    return np.stack([np.asarray(r["y"], dtype=np.float32) for r in res.results], axis=0)
```

```python
import math
from contextlib import ExitStack
import numpy as np
import concourse.bass as bass
import concourse.mybir as mybir
from concourse.bass_utils import run_bass_kernel_spmd

F32 = mybir.dt.float32
BF16 = mybir.dt.bfloat16
ALU = mybir.AluOpType
AF = mybir.ActivationFunctionType

D = 1024
L = 4096
FH = 2816
NG = 64
ALPHA = 4.0 ** 0.25
EPS = 1e-5
PI = math.pi
MAGIC = 12582912.0
GELU_C = 1.5957691216057308
KV = [float(k) for k in range(-7, 9)] + [float(7 - m) for m in range(8)] + [float(8 * j) for j in range(2, 9)]
L8IDX = [15] + list(range(24, 31))
NKV = len(KV)


class Sem:
    def __init__(self, h):
        self.h = h
        self.v = 0


class Buf:
    def __init__(self, name=""):
        self.name = name
        self.w = {}
        self.r = {}


class Eng:
    def __init__(self, name, eng, sem, is_pe=False):
        self.name = name
        self.eng = eng
        self.sem = sem
        self.waited = {}
        self.is_pe = is_pe

    def wait(self, s, v):
        if v <= 0:
            return
        if self.waited.get(s, 0) >= v:
            return
        self.eng.wait_ge(s.h, v)
        self.waited[s] = v


class Prog:
    def __init__(self, nc, es):
        self.nc = nc
        self.es = es
        self.nsem = 0
        self.allsems = []
        self.pe = Eng("pe", nc.tensor, self.newsem("pe"), True)
        self.act = Eng("act", nc.scalar, self.newsem("act"))
        self.dve = Eng("dve", nc.vector, self.newsem("dve"))
        self.pool = Eng("pool", nc.gpsimd, self.newsem("pool"))
        self.sp = Eng("sp", nc.sync, self.newsem("sp"))
        self.engs = [self.pe, self.act, self.dve, self.pool, self.sp]

    def newsem(self, name):
        self.nsem += 1
        s = Sem(self.es.enter_context(self.nc.semaphore("s%d_%s" % (self.nsem, name))))
        self.allsems.append(s)
        return s

    def _deps(self, E, reads, writes):
        for b in reads:
            for s, v in b.w.items():
                if E.is_pe and s is E.sem:
                    continue
                E.wait(s, v)
        for b in writes:
            for s, v in list(b.w.items()) + list(b.r.items()):
                if E.is_pe and s is E.sem:
                    continue
                E.wait(s, v)

    def op(self, E, fn, reads=(), writes=(), inc=True):
        self._deps(E, reads, writes)
        inst = fn(E.eng)
        if inc:
            E.sem.v += 1
            inst.then_inc(E.sem.h, 1)
            val = E.sem.v
        else:
            val = E.sem.v + 1
        for b in reads:
            b.r[E.sem] = max(b.r.get(E.sem, 0), val)
        for b in writes:
            b.w = {E.sem: val}
            b.r = {}
        return inst

    def dma(self, Q, out, in_, dsem, reads=(), writes=(), **kw):
        self._deps(Q, reads, writes)
        inst = Q.eng.dma_start(out=out, in_=in_, **kw)
        dsem.v += 16
        inst.then_inc(dsem.h, 16)
        for b in reads:
            b.r[dsem] = max(b.r.get(dsem, 0), dsem.v)
        for b in writes:
            b.w = {dsem: dsem.v}
            b.r = {}
        return inst

    def barrier(self):
        for E in self.engs:
            for s in self.allsems:
                if s is E.sem:
                    continue
                E.wait(s, s.v)


def build_program():
    nc = bass.Bass("TRN2", target_bir_lowering=False)
    dram = {}

    def din(name, shape):
        dram[name] = nc.dram_tensor(name, list(shape), F32, kind="ExternalInput").ap()
        return dram[name]

    x = din("x", [L, D])
    a_re = din("ssm_a_re", [NG, 64]); a_im = din("ssm_a_im", [NG, 64]); log_dt = din("ssm_log_dt", [NG])
    b_re = din("ssm_b_re", [NG, 64, 16]); b_im = din("ssm_b_im", [NG, 64, 16])
    c_re = din("ssm_c_re", [NG, 16, 64]); c_im = din("ssm_c_im", [NG, 16, 64])
    d_skip = din("ssm_d", [D])
    w_glu = din("ssm_w_glu", [D, 2 * D]); w_out = din("ssm_w_out", [D, D])
    w_qkv = din("sb_w_qkv", [D, 3 * D]); w_o = din("sb_w_o", [D, D])
    w_gu = [din("ffn_w_gu%d" % i, [D, 2 * FH]) for i in range(2)]
    w_dn = [din("ffn_w_down%d" % i, [FH, D]) for i in range(2)]
    lnp = {}
    for nm in ("ln_mix_g", "ln_mix_b", "ln_ffn_g", "ln_ffn_b"):
        for i in range(2):
            lnp[(nm, i)] = din("%s%d" % (nm, i), [D])
    y = nc.dram_tensor("y", [L, D], F32, kind="ExternalOutput").ap()

    def dscr(name, shape, dt):
        return nc.dram_tensor(name, list(shape), dt, kind="Internal").ap()

    wb_glu = dscr("wb_glu", [D, 2 * D], BF16); wb_out = dscr("wb_out", [D, D], BF16)
    wb_qkv = dscr("wb_qkv", [D, 3 * D], BF16); wb_o = dscr("wb_o", [D, D], BF16)
    wb_gu = [dscr("wb_gu%d" % i, [D, 2 * FH], BF16) for i in range(2)]
    wb_dn = [dscr("wb_dn%d" % i, [FH, D], BF16) for i in range(2)]
    HTd = dscr("HTd", [D, L], BF16)
    x2d = dscr("x2d", [L, D], F32)
    attTd = dscr("attTd", [D, L], BF16)

    top = ExitStack()
    with top:
        P = Prog(nc, top)
        pe, act, dve, pool, sp = P.pe, P.act, P.dve, P.pool, P.sp

        uniq = [0]

        def sbt(es, name, shape, dt):
            uniq[0] += 1
            return es.enter_context(nc.sbuf_tensor("%s_%d" % (name, uniq[0]), list(shape), dt))

        def pst(es, name, shape, dt):
            return es.enter_context(nc.psum_tensor(name, list(shape), dt))

        ident_f = sbt(top, "ident_f", [128, 128], F32)
        dummy = sbt(top, "dummy", [128, 8], F32)
        B_dummy = Buf("dummy")
        ident_b = sbt(top, "ident_b", [128, 128], BF16)
        B_const = Buf("const")
        P.op(pool, lambda e: e.memset(ident_f[:], 1.0), writes=[B_const])
        P.op(pool, lambda e: e.affine_select(out=ident_f[:], in_=ident_f[:], pattern=[[1, 128]], base=0,
                                             channel_multiplier=-1, compare_op=ALU.is_equal, fill=0.0),
             writes=[B_const])
        P.op(dve, lambda e: e.tensor_copy(out=ident_b[:], in_=ident_f[:]), reads=[B_const], writes=[B_const])

        NB = 8
        psf = [pst(top, "psb%d" % i, [128, 512], F32) for i in range(NB)]
        psb = [Buf("ps%d" % i) for i in range(NB)]
        ps_rr = [0]

        po_rr = [0]

        NROT = [6]

        def next_ps(long=False):
            if long:
                i = 6 + po_rr[0] % 2
                po_rr[0] += 1
            else:
                i = ps_rr[0] % NROT[0]
                ps_rr[0] += 1
            return psf[i], psb[i]

        WB = {}

        def cast_weight(key, src, dst, rows, cols):
            s = P.newsem("wc")
            b = Buf("wb_" + key)
            for r0 in range(0, rows, 128):
                for c0 in range(0, cols, 2048):
                    c1 = min(cols, c0 + 2048)
                    P.dma(pool, dst[r0:r0 + 128, c0:c1], src[r0:r0 + 128, c0:c1], s)
            b.w = {s: s.v}
            WB[key] = b


        B_HTd = Buf("HTd")
        B_x2d = Buf("x2d")
        B_attTd = Buf("attTd")
        B_y = Buf("y")
        s_o = P.newsem("so")

        sS = ExitStack()
        AJ1 = sbt(sS, "AJ1", [128, 8, 2, 32], F32); AJ2 = sbt(sS, "AJ2", [128, 8, 2, 32], F32)
        WyR = sbt(sS, "WyR", [128, 32, 128], BF16); WyI = sbt(sS, "WyI", [128, 32, 128], BF16)
        Kb = sbt(sS, "Kb", [128, NG, 128], BF16)
        WzR = sbt(sS, "WzR", [128, NG, 64], BF16); WzI = sbt(sS, "WzI", [128, NG, 64], BF16)
        with ExitStack() as es:
            s_ld = P.newsem("sld")
            are = sbt(es, "are", [128, 32], F32); aim = sbt(es, "aim", [128, 32], F32)
            dtl = sbt(es, "dtl", [128, 32], F32)
            Bre = sbt(es, "Bre", [128, 32, 16], F32); Bim = sbt(es, "Bim", [128, 32, 16], F32)
            Cre = sbt(es, "Cre", [128, 32, 16], F32); Cim = sbt(es, "Cim", [128, 32, 16], F32)
            Tc = [sbt(es, "Tc%d" % i, [128, 4, 2, 64], F32) for i in range(2)]
            dsk = sbt(es, "dsk", [128, 64], F32)
            B_par = Buf("par")
            for g2 in range(2):
                sl = slice(64 * g2, 64 * g2 + 64)
                gs = slice(32 * g2, 32 * g2 + 32)
                P.dma(sp, are[sl, :], a_re[gs, :].rearrange("g p -> p g"), s_ld, writes=[B_par], allow_slow_non_contiguous=True)
                P.dma(sp, aim[sl, :], a_im[gs, :].rearrange("g p -> p g"), s_ld, writes=[B_par], allow_slow_non_contiguous=True)
                P.dma(sp, dtl[sl, :], log_dt[gs].partition_broadcast(64), s_ld, writes=[B_par])
                P.dma(sp, Bre[sl, :, :], b_re[gs, :, :].rearrange("g p h -> p g h"), s_ld, writes=[B_par])
                P.dma(sp, Bim[sl, :, :], b_im[gs, :, :].rearrange("g p h -> p g h"), s_ld, writes=[B_par])
            for ri_, csrc in enumerate((c_re, c_im)):
                cv = csrc.rearrange("(g2 gt gp8) h p -> (gp8 h) gt g2 p", g2=2, gt=4)
                for g2 in range(2):
                    P.dma(sp, Tc[ri_][:, :, g2, :], cv[:, :, g2, :], s_ld, writes=[B_par])
            for m in range(8):
                P.dma(sp, dsk[16 * m:16 * m + 16, :], d_skip.rearrange("(g h) -> h g", h=16), s_ld, writes=[B_par],
                      allow_slow_non_contiguous=True)
            B_par.w = {s_ld: s_ld.v}
            pool.wait(s_ld, s_ld.v)
            cast_weight("glu", w_glu, wb_glu, D, 2 * D)
            cast_weight("out", w_out, wb_out, D, D)
            cast_weight("gu0", w_gu[0], wb_gu[0], D, 2 * FH)
            cast_weight("dn0", w_dn[0], wb_dn[0], FH, D)

            B_C = Buf("C")
            for ri, (T, Cd) in enumerate(((Tc[0], Cre), (Tc[1], Cim))):
                pt, pb = next_ps()
                for gt in range(4):
                    P.op(pe, lambda e, gt=gt, T=T, pt=pt: e.transpose(
                        out=pt[:, gt * 128:(gt + 1) * 128], in_=T[:, gt, :, :].rearrange("q a p -> q (a p)"),
                        identity=ident_f[:]), reads=[B_par, B_const], writes=[pb], inc=(gt == 3))
                P.op(dve, lambda e, Cd=Cd, pt=pt: e.tensor_copy(
                    out=Cd[:].rearrange("q g h -> q (g h)"), in_=pt[:, :]), reads=[pb], writes=[B_C])

            B_t = Buf("tab")

            def dv(fn, reads=(B_par,), writes=None):
                return P.op(dve, fn, reads=list(reads) + [B_t, B_C], writes=[B_t] if writes is None else writes)

            dt_ = sbt(es, "dt_", [128, 32], F32); lre = sbt(es, "lre", [128, 32], F32)
            xr = sbt(es, "xr", [128, 32], F32); xi = sbt(es, "xi", [128, 32], F32)
            P.op(act, lambda e: e.activation(out=dt_[:], in_=dtl[:], func=AF.Exp), reads=[B_par], writes=[B_t])
            dv(lambda e: e.tensor_scalar(out=lre[:], in0=are[:], scalar1=-1e-4, scalar2=None, op0=ALU.min))
            dv(lambda e: e.tensor_tensor(out=xr[:], in0=lre[:], in1=dt_[:], op=ALU.mult))
            dv(lambda e: e.tensor_tensor(out=xi[:], in0=aim[:], in1=dt_[:], op=ALU.mult))
            kv = sbt(es, "kv", [128, NKV], F32)
            for i, k in enumerate(KV):
                dv(lambda e, i=i, k=k: e.memset(kv[:, i:i + 1], k))
            T3 = [128, 32, NKV]
            ang = sbt(es, "ang", T3, F32); expo = sbt(es, "expo", T3, F32); mag = sbt(es, "mag", T3, F32)
            kk = sbt(es, "kk", T3, F32); sn = sbt(es, "sn", T3, F32); cs = sbt(es, "cs", T3, F32)
            PwR = sbt(es, "PwR", T3, F32); PwI = sbt(es, "PwI", T3, F32)
            kvb = kv[:, :].unsqueeze(1).broadcast_to(T3)
            dv(lambda e: e.tensor_tensor(out=ang[:], in0=xi[:, :].unsqueeze(2).broadcast_to(T3), in1=kvb, op=ALU.mult))
            dv(lambda e: e.tensor_tensor(out=expo[:], in0=xr[:, :].unsqueeze(2).broadcast_to(T3), in1=kvb, op=ALU.mult))
            P.op(act, lambda e: e.activation(out=mag[:], in_=expo[:], func=AF.Exp), reads=[B_t], writes=[B_t])

            def range_reduce(dst, src, shift):
                if shift != 0.0:
                    dv(lambda e: e.tensor_scalar(out=dst[:], in0=src[:], scalar1=shift, scalar2=None, op0=ALU.add))
                    s2 = dst
                else:
                    s2 = src
                dv(lambda e: e.tensor_scalar(out=kk[:], in0=s2[:], scalar1=1.0 / (2 * PI), scalar2=MAGIC, op0=ALU.mult, op1=ALU.add))
                dv(lambda e: e.tensor_scalar(out=kk[:], in0=kk[:], scalar1=MAGIC, scalar2=None, op0=ALU.subtract))
                dv(lambda e: e.scalar_tensor_tensor(out=dst[:], in0=kk[:], scalar=-2 * PI, in1=s2[:], op0=ALU.mult, op1=ALU.add))

            range_reduce(sn, ang, 0.0)
            P.op(act, lambda e: e.activation(out=sn[:], in_=sn[:], func=AF.Sin), reads=[B_t], writes=[B_t])
            range_reduce(cs, ang, PI / 2)
            P.op(act, lambda e: e.activation(out=cs[:], in_=cs[:], func=AF.Sin), reads=[B_t], writes=[B_t])
            dv(lambda e: e.tensor_tensor(out=PwR[:], in0=mag[:], in1=cs[:], op=ALU.mult))
            dv(lambda e: e.tensor_tensor(out=PwI[:], in0=mag[:], in1=sn[:], op=ALU.mult))
            t_a = sbt(es, "t_a", [128, 32], F32); t_b = sbt(es, "t_b", [128, 32], F32)
            nr = sbt(es, "nr", [128, 32], F32); den = sbt(es, "den", [128, 32], F32)
            cr = sbt(es, "cr", [128, 32], F32); ci = sbt(es, "ci", [128, 32], F32)
            LR = PwR[:, :, 8]; LI = PwI[:, :, 8]
            dv(lambda e: e.tensor_scalar(out=nr[:], in0=LR, scalar1=-1.0, scalar2=None, op0=ALU.add))
            dv(lambda e: e.tensor_tensor(out=den[:], in0=lre[:], in1=lre[:], op=ALU.mult))
            dv(lambda e: e.tensor_tensor(out=t_a[:], in0=aim[:], in1=aim[:], op=ALU.mult))
            dv(lambda e: e.tensor_tensor(out=den[:], in0=den[:], in1=t_a[:], op=ALU.add))
            dv(lambda e: e.reciprocal(out=den[:], in_=den[:]))
            dv(lambda e: e.tensor_tensor(out=t_a[:], in0=nr[:], in1=lre[:], op=ALU.mult))
            dv(lambda e: e.tensor_tensor(out=t_b[:], in0=LI, in1=aim[:], op=ALU.mult))
            dv(lambda e: e.tensor_tensor(out=t_a[:], in0=t_a[:], in1=t_b[:], op=ALU.add))
            dv(lambda e: e.tensor_tensor(out=cr[:], in0=t_a[:], in1=den[:], op=ALU.mult))
            dv(lambda e: e.tensor_tensor(out=t_a[:], in0=LI, in1=lre[:], op=ALU.mult))
            dv(lambda e: e.tensor_tensor(out=t_b[:], in0=nr[:], in1=aim[:], op=ALU.mult))
            dv(lambda e: e.tensor_tensor(out=t_a[:], in0=t_a[:], in1=t_b[:], op=ALU.subtract))
            dv(lambda e: e.tensor_tensor(out=ci[:], in0=t_a[:], in1=den[:], op=ALU.mult))
            T16 = [128, 32, 16]
            BbR = sbt(es, "BbR", T16, F32); BbI = sbt(es, "BbI", T16, F32)
            u1 = sbt(es, "u1", T16, F32)
            crb = cr[:, :].unsqueeze(2).broadcast_to(T16); cib = ci[:, :].unsqueeze(2).broadcast_to(T16)
            dv(lambda e: e.tensor_tensor(out=BbR[:], in0=Bre[:], in1=crb, op=ALU.mult))
            dv(lambda e: e.tensor_tensor(out=u1[:], in0=Bim[:], in1=cib, op=ALU.mult))
            dv(lambda e: e.tensor_tensor(out=BbR[:], in0=BbR[:], in1=u1[:], op=ALU.subtract))
            dv(lambda e: e.tensor_tensor(out=BbI[:], in0=Bim[:], in1=crb, op=ALU.mult))
            dv(lambda e: e.tensor_tensor(out=u1[:], in0=Bre[:], in1=cib, op=ALU.mult))
            dv(lambda e: e.tensor_tensor(out=BbI[:], in0=BbI[:], in1=u1[:], op=ALU.add))
            for j8, ki in enumerate(L8IDX):
                for r in range(2):
                    dv(lambda e, r=r, j8=j8, ki=ki: e.tensor_copy(out=AJ1[:, j8, r, :], in_=PwR[:, :, ki]))
                dv(lambda e, j8=j8, ki=ki: e.tensor_scalar(out=AJ2[:, j8, 0, :], in0=PwI[:, :, ki], scalar1=-1.0, scalar2=None, op0=ALU.mult))
                dv(lambda e, j8=j8, ki=ki: e.tensor_copy(out=AJ2[:, j8, 1, :], in_=PwI[:, :, ki]))

            T4 = [128, 32, 8, 16]
            E7R = sbt(es, "E7R", T4, F32); E7I = sbt(es, "E7I", T4, F32)
            F0R = sbt(es, "F0R", T4, F32); F0I = sbt(es, "F0I", T4, F32)
            w1 = sbt(es, "w1", T4, F32)

            def cmul(outR, outI, k0, XR, XI, negI=False):
                pr = PwR[:, :, k0:k0 + 8].unsqueeze(3).broadcast_to(T4)
                pi_ = PwI[:, :, k0:k0 + 8].unsqueeze(3).broadcast_to(T4)
                xr_ = XR[:, :, :].unsqueeze(2).broadcast_to(T4)
                xi_ = XI[:, :, :].unsqueeze(2).broadcast_to(T4)
                dv(lambda e: e.tensor_tensor(out=outR, in0=pr, in1=xr_, op=ALU.mult))
                dv(lambda e: e.tensor_tensor(out=w1[:], in0=pi_, in1=xi_, op=ALU.mult))
                dv(lambda e: e.tensor_tensor(out=outR, in0=outR, in1=w1[:], op=ALU.subtract))
                dv(lambda e: e.tensor_tensor(out=outI, in0=pr, in1=xi_, op=ALU.mult))
                dv(lambda e: e.tensor_tensor(out=w1[:], in0=pi_, in1=xr_, op=ALU.mult))
                if negI:
                    dv(lambda e: e.scalar_tensor_tensor(out=outI, in0=outI, scalar=-1.0, in1=w1[:], op0=ALU.mult, op1=ALU.subtract))
                else:
                    dv(lambda e: e.tensor_tensor(out=outI, in0=outI, in1=w1[:], op=ALU.add))

            cmul(E7R[:], E7I[:], 16, BbR, BbI)
            cmul(F0R[:], F0I[:], 8, Cre, Cim, negI=True)
            dv(lambda e: e.tensor_copy(out=WyR[:].rearrange("q g (l h) -> q g l h", l=8), in_=F0R[:]))
            dv(lambda e: e.tensor_copy(out=WyI[:].rearrange("q g (l h) -> q g l h", l=8), in_=F0I[:]))
            cmul(F0R[:], F0I[:], 0, Cre, Cim, negI=True)

            maskK = sbt(es, "maskK", [128, 128], F32)
            P.op(pool, lambda e: e.memset(maskK[:], 1.0), writes=[B_const])
            P.op(pool, lambda e: e.affine_select(out=maskK[:].rearrange("q (l h) -> q l h", l=8),
                                                 in_=maskK[:].rearrange("q (l h) -> q l h", l=8),
                                                 pattern=[[16, 8], [0, 16]], base=15, channel_multiplier=-1,
                                                 compare_op=ALU.is_ge, fill=0.0), writes=[B_const])
            P.op(pool, lambda e: e.memset(dummy[:, 0:1], 0.0), writes=[B_dummy])
            ktmp = sbt(es, "ktmp", [128, 4, 128], F32)
            B_K = Buf("K")
            for gb in range(NG // 4):
                pt, pb = next_ps()
                for j in range(4):
                    g = gb * 4 + j
                    g2, gp = g // 32, g % 32
                    sl = slice(64 * g2, 64 * g2 + 64)
                    P.op(pe, lambda e, j=j, sl=sl, gp=gp, pt=pt: e.matmul(
                        out=pt[:, j * 128:(j + 1) * 128], lhsT=E7R[sl, gp, :, :].rearrange("q m h -> q (m h)"),
                        rhs=F0R[sl, gp, :, :].rearrange("q m h -> q (m h)"), start=True, stop=False),
                        reads=[B_t], writes=[pb], inc=False)
                    P.op(pe, lambda e, j=j, sl=sl, gp=gp, pt=pt: e.matmul(
                        out=pt[:, j * 128:(j + 1) * 128], lhsT=E7I[sl, gp, :, :].rearrange("q m h -> q (m h)"),
                        rhs=F0I[sl, gp, :, :].rearrange("q m h -> q (m h)"), start=False, stop=True),
                        reads=[B_t], writes=[pb], inc=(j == 3))
                P.op(dve, lambda e, pt=pt: e.tensor_tensor(
                    out=ktmp[:], in0=pt[:, :].rearrange("q (j n) -> q j n", j=4),
                    in1=maskK[:, :].unsqueeze(1).broadcast_to([128, 4, 128]), op=ALU.mult),
                    reads=[pb, B_const], writes=[B_K])
                for j in range(4):
                    g = gb * 4 + j
                    P.op(dve, lambda e, j=j, g=g: e.scalar_tensor_tensor(
                        out=Kb[:, g, :], in0=ident_f[:], scalar=dsk[:, g:g + 1], in1=ktmp[:, j, :],
                        op0=ALU.mult, op1=ALU.add), reads=[B_K, B_par, B_const], writes=[B_K])
            for ri, (E7, Wz) in enumerate(((E7R, WzR), (E7I, WzI))):
                for gb in range(NG // 8):
                    pt, pb = next_ps()
                    for j in range(8):
                        g = gb * 8 + j
                        g2, gp = g // 32, g % 32
                        sl = slice(64 * g2, 64 * g2 + 64)
                        P.op(pe, lambda e, j=j, sl=sl, gp=gp, pt=pt, E7=E7: e.transpose(
                            out=pt[:, j * 64:(j + 1) * 64], in_=E7[sl, gp, :, :].rearrange("q m h -> q (m h)"),
                            identity=ident_f[sl, sl]), reads=[B_t, B_const], writes=[pb], inc=(j == 7))
                    P.op(dve, lambda e, pt=pt, Wz=Wz, gb=gb: e.tensor_copy(
                        out=Wz[:, gb * 8:(gb + 1) * 8, :].rearrange("q g p -> q (g p)"), in_=pt[:, :]),
                        reads=[pb], writes=[B_K])
            P.barrier()

        with ExitStack() as es:
            BIG = sbt(es, "BIG", [128, 8192], F32)
            Xg = sbt(es, "Xg", [128, 64, 128], BF16)
            U = sbt(es, "U", [128, 64, 128], BF16)
            carry = sbt(es, "carry", [128, 2, 32], F32)
            HTc = sbt(es, "HTc", [128, 8, 1024], BF16)
            HTs = sbt(es, "HTs", [128, 8, 1024], BF16)
            tq0 = [sbt(es, "tq0_%d" % i, [128, 512], F32) for i in range(4)]
            ysb = [sbt(es, "ysb%d" % i, [128, 512], F32) for i in range(4)]
            B_tq0 = [Buf() for _ in range(4)]; B_ysb = [Buf() for _ in range(4)]
            hT = [sbt(es, "hT%d" % i, [128, 512], BF16) for i in range(2)]
            st1 = sbt(es, "st1", [128, 2, 32], F32); st2 = sbt(es, "st2", [128, 2, 32], F32)
            sA = sbt(es, "sA", [128, 2, 32, 16], F32); sB = sbt(es, "sB", [128, 2, 32, 16], F32)
            CinT = sbt(es, "CinT", [128, 2, 32, 16], F32)
            B_cin = Buf("cin")
            B_BIG = Buf("BIG"); B_Xg = Buf("Xg"); B_U = Buf("U"); B_car = Buf("carry")
            B_HTc = Buf("HTc"); B_HTs = Buf("HTs"); B_st = Buf("st")
            B_tq = [Buf() for _ in range(4)]; B_hT = [Buf() for _ in range(2)]
            s_x = P.newsem("sx"); s_ht = P.newsem("sht")
            Xc4 = BIG[:, :].rearrange("q (m g h) -> q g m h", m=8, g=64)
            Z4 = BIG[:, :].rearrange("q (r g c) -> q r g c", r=2, g=32)
            Sb4 = Xg[:].rearrange("q (r g) c -> q r g c", r=2)
            P.op(dve, lambda e: e.memset(carry[:], 0.0), writes=[B_car])
            for seg in range(4):
                P.dma(sp, BIG[:, :], x[seg * 1024:(seg + 1) * 1024, :].rearrange("(c m) d -> c (m d)", m=8), s_x,
                      writes=[B_BIG])
                if seg == 1:
                    pool.wait(s_x, s_x.v)
                    cast_weight("qkv", w_qkv, wb_qkv, D, 3 * D)
                    cast_weight("o", w_o, wb_o, D, D)
                    cast_weight("gu1", w_gu[1], wb_gu[1], D, 2 * FH)
                    cast_weight("dn1", w_dn[1], wb_dn[1], FH, D)
                for half in range(2):
                    E = act if half == 0 else dve
                    gs = slice(32 * half, 32 * half + 32)
                    if half == 0:
                        P.op(act, lambda e, gs=gs: e.activation(
                            out=Xg[:, gs, :].rearrange("q g (m h) -> q g m h", m=8), in_=Xc4[:, gs, :, :], func=AF.Copy),
                            reads=[B_BIG], writes=[B_Xg])
                    else:
                        P.op(dve, lambda e, gs=gs: e.tensor_copy(
                            out=Xg[:, gs, :].rearrange("q g (m h) -> q g m h", m=8), in_=Xc4[:, gs, :, :]),
                            reads=[B_BIG], writes=[B_Xg])
                for gb in range(16):
                    pt, pb = next_ps()
                    ptb = pt[:, :].bitcast(BF16)
                    for j in range(4):
                        g = gb * 4 + j
                        P.op(pe, lambda e, j=j, g=g, ptb=ptb: e.transpose(
                            out=ptb[:, j * 128:(j + 1) * 128], in_=Xg[:, g, :], identity=ident_b[:]),
                            reads=[B_Xg, B_const], writes=[pb], inc=(j == 3))
                    E = act if gb % 2 == 0 else dve
                    if gb % 2 == 0:
                        P.op(act, lambda e, gb=gb, ptb=ptb: e.activation(
                            out=U[:, gb * 4:(gb + 1) * 4, :].rearrange("q g c -> q (g c)"), in_=ptb[:, 0:512], func=AF.Copy),
                            reads=[pb], writes=[B_U])
                    else:
                        P.op(dve, lambda e, gb=gb, ptb=ptb: e.tensor_copy(
                            out=U[:, gb * 4:(gb + 1) * 4, :].rearrange("q g c -> q (g c)"), in_=ptb[:, 0:512]),
                            reads=[pb], writes=[B_U])
                for gpb in range(8):
                    for ri, Wz in enumerate((WzR, WzI)):
                        pt, pb = next_ps()
                        for j in range(4):
                            gp = gpb * 4 + j
                            for g2 in range(2):
                                g = g2 * 32 + gp
                                P.op(pe, lambda e, j=j, g2=g2, g=g, pt=pt, Wz=Wz: e.matmul(
                                    out=pt[64 * g2:64 * g2 + 64, j * 128:(j + 1) * 128], lhsT=Wz[:, g, :], rhs=U[:, g, :],
                                    start=True, stop=True), reads=[B_U, B_K], writes=[pb], inc=(j == 3 and g2 == 1))
                        if ri == 0:
                            P.op(act, lambda e, pt=pt, gpb=gpb, ri=ri: e.activation(
                                out=Z4[:, ri, gpb * 4:(gpb + 1) * 4, :].rearrange("q g c -> q (g c)"), in_=pt[:, :], func=AF.Copy),
                                reads=[pb, B_Xg], writes=[B_BIG])
                        else:
                            P.op(dve, lambda e, pt=pt, gpb=gpb, ri=ri: e.tensor_copy(
                                out=Z4[:, ri, gpb * 4:(gpb + 1) * 4, :].rearrange("q g c -> q (g c)"), in_=pt[:, :]),
                                reads=[pb, B_Xg], writes=[B_BIG])
                P.op(dve, lambda e: e.tensor_copy(out=Sb4[:, :, :, 0], in_=carry[:]), reads=[B_car, B_Xg, B_BIG], writes=[B_Xg])
                T4s = [128, 2, 32, 16]

                def cstep(dst, prev, prev_sw, j8, add_to, big):
                    a1 = AJ1[:, j8, :, :]
                    a2 = AJ2[:, j8, :, :]
                    ta, tb = (sA, sB) if big else (st1, st2)
                    if big:
                        a1 = a1.unsqueeze(3).broadcast_to(T4s)
                        a2 = a2.unsqueeze(3).broadcast_to(T4s)
                    P.op(dve, lambda e: e.tensor_tensor(out=ta[:], in0=a1, in1=prev, op=ALU.mult),
                         reads=[B_BIG, B_car, B_t, B_cin], writes=[B_st])
                    P.op(dve, lambda e: e.tensor_tensor(out=tb[:], in0=a2, in1=prev_sw, op=ALU.mult),
                         reads=[B_BIG, B_car, B_st, B_cin], writes=[B_st])
                    P.op(dve, lambda e: e.tensor_tensor(out=ta[:], in0=ta[:], in1=tb[:], op=ALU.add),
                         reads=[B_st], writes=[B_st])
                    return ta

                for j in range(1, 8):
                    ta = cstep(None, Z4[:, :, :, j - 1:128:8], Z4[:, ::-1, :, j - 1:128:8], 0, None, True)
                    P.op(dve, lambda e, j=j, ta=ta: e.tensor_tensor(out=Z4[:, :, :, j:128:8], in0=Z4[:, :, :, j:128:8], in1=ta[:], op=ALU.add),
                         reads=[B_st], writes=[B_BIG])
                P.op(dve, lambda e: e.tensor_copy(out=CinT[:, :, :, 0], in_=carry[:]), reads=[B_car], writes=[B_cin])
                for b in range(15):
                    ta = cstep(None, CinT[:, :, :, b], CinT[:, ::-1, :, b], 7, None, False)
                    P.op(dve, lambda e, b=b, ta=ta: e.tensor_tensor(out=CinT[:, :, :, b + 1], in0=Z4[:, :, :, b * 8 + 7], in1=ta[:], op=ALU.add),
                         reads=[B_st, B_BIG], writes=[B_cin])
                for j in range(8):
                    ta = cstep(None, CinT[:, :, :, :], CinT[:, ::-1, :, :], j, None, True)
                    P.op(dve, lambda e, j=j, ta=ta: e.tensor_tensor(out=Z4[:, :, :, j:128:8], in0=Z4[:, :, :, j:128:8], in1=ta[:], op=ALU.add),
                         reads=[B_st], writes=[B_BIG])
                P.op(dve, lambda e: e.tensor_copy(out=carry[:], in_=Z4[:, :, :, 127]), reads=[B_BIG], writes=[B_car])
                P.op(act, lambda e: e.activation(out=Sb4[:, :, :, 1:128], in_=Z4[:, :, :, 0:127], func=AF.Copy),
                     reads=[B_BIG], writes=[B_Xg])
                NQ = 4
                Ysl = {}

                def G0(gb):
                    pt, pb = next_ps()
                    Ysl[gb] = (pt, pb)
                    for j in range(4):
                        g = gb * 4 + j
                        g2, gp = g // 32, g % 32
                        sl = slice(64 * g2, 64 * g2 + 64)
                        o = pt[:, j * 128:(j + 1) * 128]
                        P.op(pe, lambda e, o=o, g=g: e.matmul(out=o, lhsT=Kb[:, g, :], rhs=U[:, g, :], start=True, stop=False),
                             reads=[B_U, B_K], writes=[pb], inc=False)
                        P.op(pe, lambda e, o=o, sl=sl, gp=gp: e.matmul(out=o, lhsT=WyR[sl, gp, :], rhs=Sb4[sl, 0, gp, :], start=False, stop=False),
                             reads=[B_Xg, B_t], writes=[pb], inc=False)
                        P.op(pe, lambda e, o=o, sl=sl, gp=gp: e.matmul(out=o, lhsT=WyI[sl, gp, :], rhs=Sb4[sl, 1, gp, :], start=False, stop=True),
                             reads=[B_Xg, B_t], writes=[pb], inc=(j == 3))

                def G1(gb):
                    pt, pb = Ysl[gb]
                    q = gb % NQ
                    P.op(act, lambda e: e.activation(out=ysb[q][:], in_=pt[:, :], func=AF.Copy), reads=[pb], writes=[B_ysb[q]])
                    P.op(act, lambda e: e.activation(out=tq0[q][:], in_=pt[:, :], func=AF.Square), reads=[pb], writes=[B_tq0[q]])

                def G2(gb):
                    q = gb % NQ
                    P.op(dve, lambda e: e.tensor_scalar(out=tq0[q][:], in0=tq0[q][:], scalar1=0.044715, scalar2=1.0, op0=ALU.mult, op1=ALU.add),
                         reads=[B_tq0[q]], writes=[B_tq0[q]])
                    P.op(dve, lambda e: e.tensor_tensor(out=tq0[q][:], in0=ysb[q][:], in1=tq0[q][:], op=ALU.mult),
                         reads=[B_tq0[q], B_ysb[q]], writes=[B_tq0[q]])

                def G3(gb):
                    q = gb % NQ
                    P.op(act, lambda e: e.activation(out=tq0[q][:], in_=tq0[q][:], func=AF.Sigmoid, scale=GELU_C), reads=[B_tq0[q]], writes=[B_tq0[q]])

                def G4(gb):
                    q = gb % NQ
                    k = gb % 2
                    P.op(dve, lambda e: e.tensor_tensor(out=hT[k][:], in0=ysb[q][:], in1=tq0[q][:], op=ALU.mult),
                         reads=[B_tq0[q], B_ysb[q]], writes=[B_hT[k]])

                def G5(gb):
                    k = gb % 2
                    pt2, pb2 = next_ps()
                    Ysl[("t", gb)] = (pt2, pb2)
                    pt2b = pt2[:, :].bitcast(BF16)
                    for j in range(4):
                        P.op(pe, lambda e, j=j: e.transpose(
                            out=pt2b[:, j * 128:(j + 1) * 128], in_=hT[k][:, j * 128:(j + 1) * 128], identity=ident_b[:]),
                            reads=[B_hT[k], B_const], writes=[pb2], inc=(j == 3))

                def G6(gb):
                    pt2, pb2 = Ysl[("t", gb)]
                    pt2b = pt2[:, :].bitcast(BF16)
                    P.op(act, lambda e: e.activation(
                        out=HTc[:, :, gb * 64:(gb + 1) * 64].rearrange("q l (g h) -> q l g h", g=4),
                        in_=pt2b[:, 0:512].rearrange("q (g l h) -> q l g h", g=4, l=8), func=AF.Copy),
                        reads=[pb2], writes=[B_HTc])

                gst = [G0, G1, G2, G3, G4, G5, G6]
                for tick in range(16 + len(gst) - 1):
                    for si, fn in enumerate(gst):
                        gb = tick - si
                        if 0 <= gb < 16:
                            fn(gb)
                for dt_i in range(8):
                    for lh in range(2):
                        pt, pb = next_ps()
                        ptb = pt[:, :].bitcast(BF16)
                        for j in range(4):
                            l = lh * 4 + j
                            P.op(pe, lambda e, j=j, l=l, dt_i=dt_i, ptb=ptb: e.transpose(
                                out=ptb[:, j * 128:(j + 1) * 128], in_=HTc[:, l, dt_i * 128:(dt_i + 1) * 128], identity=ident_b[:]),
                                reads=[B_HTc, B_const], writes=[pb], inc=(j == 3))
                        if lh == 0:
                            P.op(act, lambda e, dt_i=dt_i, lh=lh, ptb=ptb: e.activation(
                                out=HTs[:, dt_i, lh * 512:(lh + 1) * 512], in_=ptb[:, 0:512], func=AF.Copy),
                                reads=[pb], writes=[B_HTs])
                        else:
                            P.op(dve, lambda e, dt_i=dt_i, lh=lh, ptb=ptb: e.tensor_copy(
                                out=HTs[:, dt_i, lh * 512:(lh + 1) * 512], in_=ptb[:, 0:512]),
                                reads=[pb], writes=[B_HTs])
                P.dma(sp, HTd[:, seg * 1024:(seg + 1) * 1024].rearrange("(dt q) t -> q dt t", q=128), HTs[:, :, :], s_ht,
                      reads=[B_HTs], writes=[B_HTd])
            P.barrier()
        sS.close()

        def dense_phase(layer):
            with ExitStack() as es:
                NWS = 4
                wslot = [sbt(es, "wslot%d" % i, [128, 11, 512], BF16) for i in range(NWS)]
                wsem = [P.newsem("ws%d" % i) for i in range(NWS)]
                wbuf = [Buf("wslot%d" % i) for i in range(NWS)]
                w_rr = [0]

                def wload(src_ap, nkt, wb_buf, ncols=512):
                    i = w_rr[0] % NWS
                    w_rr[0] += 1
                    Q = sp
                    P.dma(Q, wslot[i][:, 0:nkt, 0:ncols], src_ap.rearrange("(kt q) n -> q kt n", q=128), wsem[i],
                          reads=[wb_buf], writes=[wbuf[i]])
                    return wslot[i], wbuf[i]

                lnt = {k: sbt(es, "lnt_%s" % k, [128, D], F32) for k in ("mg", "mb", "fg", "fb")}
                B_ln = Buf("ln")
                s_ln = P.newsem("sln")
                xt = sbt(es, "xt", [128, 4, D], F32)
                rr_ = sbt(es, "rr_", [128, 4, D], F32)
                xin = sbt(es, "xin", [128, 4, D], F32)
                xnb = [sbt(es, "xnb%d" % i, [128, D], BF16) for i in range(2)]
                inT = sbt(es, "inT", [128, 8, 512], BF16)
                zT = sbt(es, "zT", [128, 8, 512], BF16)
                x1T = sbt(es, "x1T", [128, 8, 512], BF16)
                h2T = sbt(es, "h2T", [128, 22, 512], BF16)
                sgt = [sbt(es, "sgt%d" % i, [128, 512], F32) for i in range(2)]
                stats = [sbt(es, "stats%d" % i, [128, 2, 6], F32) for i in range(4)]
                mv = [sbt(es, "mv%d" % i, [128, 2], F32) for i in range(4)]
                rstd = [sbt(es, "rstd%d" % i, [128, 1], F32) for i in range(4)]
                nmr = [sbt(es, "nmr%d" % i, [128, 1], F32) for i in range(4)]
                B_xt = [Buf("xt%d" % i) for i in range(4)]; B_rr = [Buf("rr%d" % i) for i in range(4)]
                B_xin = [Buf("xin%d" % i) for i in range(4)]
                B_xnb = [Buf("xnb0"), Buf("xnb1")]; B_inT = Buf("inT"); B_zT = Buf("zT")
                B_x1T = Buf("x1T"); B_h2T = Buf("h2T"); B_sg = [Buf(), Buf()]; B_stat = [Buf("stat%d" % i) for i in range(4)]
                xnb_rr = [0]
                s_xt = P.newsem("sxt"); s_in = P.newsem("sin")
                sg_rr = [0]
                pending = []

                def flush_pending():
                    while pending:
                        pending.pop(0)()

                gT = sbt(es, "gT", [128, 8], F32); bT = sbt(es, "bT", [128, 8], F32)

                def load_ln(layer):
                    for k, nm in (("mg", "ln_mix_g"), ("mb", "ln_mix_b"), ("fg", "ln_ffn_g"), ("fb", "ln_ffn_b")):
                        P.dma(sp, lnt[k][:, :], lnp[(nm, layer)].partition_broadcast(128), s_ln, writes=[B_ln])
                    P.dma(sp, gT[:, :], lnp[("ln_mix_g", layer)].rearrange("(dt q) -> q dt", q=128), s_ln, writes=[B_ln],
                          allow_slow_non_contiguous=True)
                    P.dma(sp, bT[:, :], lnp[("ln_mix_b", layer)].rearrange("(dt q) -> q dt", q=128), s_ln, writes=[B_ln],
                          allow_slow_non_contiguous=True)

                def fm_gated(src_T, B_src, wsrc, wkey, nkt, n_half, func, dst_T, B_dst):
                    ncol = n_half * 128
                    for c0 in range(0, n_half, 4):
                        nt = min(4, n_half - c0)
                        sa, ba = wload(wsrc[:, c0 * 128:(c0 + nt) * 128], nkt, WB[wkey], ncols=nt * 128)
                        sb_, bb = wload(wsrc[:, ncol + c0 * 128:ncol + (c0 + nt) * 128], nkt, WB[wkey], ncols=nt * 128)
                        for j in range(nt):
                            pa, pba = next_ps()
                            pb_, pbb = next_ps()
                            for (slot, sbuf, pt, pbuf) in ((sa, ba, pa, pba), (sb_, bb, pb_, pbb)):
                                for kt in range(nkt):
                                    P.op(pe, lambda e, slot=slot, pt=pt, kt=kt, j=j: e.matmul(
                                        out=pt[:, :], lhsT=slot[:, kt, j * 128:(j + 1) * 128], rhs=src_T[:, kt, :],
                                        start=(kt == 0), stop=(kt == nkt - 1)),
                                        reads=[sbuf, B_src], writes=[pbuf], inc=(kt == nkt - 1))
                            i = sg_rr[0] % 2
                            sg_rr[0] += 1
                            if func == "glu":
                                gate_ps, gate_b, oth_ps, oth_b = pb_, pbb, pa, pba
                                f = AF.Sigmoid
                            else:
                                gate_ps, gate_b, oth_ps, oth_b = pa, pba, pb_, pbb
                                f = AF.Silu
                            P.op(act, lambda e, i=i, gate_ps=gate_ps, f=f: e.activation(out=sgt[i][:], in_=gate_ps[:, :], func=f),
                                 reads=[gate_b], writes=[B_sg[i]])
                            P.op(dve, lambda e, i=i, oth_ps=oth_ps, c0=c0, j=j: e.tensor_tensor(
                                out=dst_T[:, c0 + j, :], in0=oth_ps[:, :], in1=sgt[i][:], op=ALU.mult),
                                reads=[B_sg[i], oth_b], writes=[B_dst])
                            if pending:
                                pending.pop(0)()

                def tm_proj_resid(src_T, B_src, wsrc, wkey, nkt, res, B_res):
                    khalves = [(0, nkt)] if nkt <= 11 else [(0, 11), (11, nkt)]
                    for ch in range(2):
                        pts = [next_ps() for _ in range(4)]
                        for hi, (k0, k1) in enumerate(khalves):
                            slot, sbuf = wload(wsrc[k0 * 128:k1 * 128, ch * 512:(ch + 1) * 512], k1 - k0, WB[wkey])
                            for st in range(4):
                                pt, pbuf = pts[st]
                                for kt in range(k0, k1):
                                    last = (kt == nkt - 1)
                                    P.op(pe, lambda e, slot=slot, pt=pt, kt=kt, k0=k0, st=st, last=last: e.matmul(
                                        out=pt[:, :], lhsT=src_T[:, kt, st * 128:(st + 1) * 128], rhs=slot[:, kt - k0, :],
                                        start=(kt == 0), stop=last),
                                        reads=[sbuf, B_src], writes=[pbuf], inc=(kt == k1 - 1))
                        flush_pending()
                        for st in range(4):
                            pt, pbuf = pts[st]
                            P.op(dve, lambda e, st=st, ch=ch, pt=pt: e.scalar_tensor_tensor(
                                out=rr_[:, st, ch * 512:(ch + 1) * 512], in0=res[:, st, ch * 512:(ch + 1) * 512], scalar=ALPHA,
                                in1=pt[:, :], op0=ALU.mult, op1=ALU.add), reads=[pbuf, B_res[st]], writes=[B_rr[st]])

                def layer_norm(gk, bk, dstT, B_dstT, store_rows=None, store_buf=None, store_sem=None):
                    R4 = range(4)
                    for st in R4:
                        for ch in range(2):
                            P.op(dve, lambda e, st=st, ch=ch: e.bn_stats(out=stats[st][:, ch, :], in_=rr_[:, st, ch * 512:(ch + 1) * 512]),
                                 reads=[B_rr[st]], writes=[B_stat[st]])
                    for st in R4:
                        P.op(dve, lambda e, st=st: e.bn_aggr(out=mv[st][:, :], in_=stats[st][:, :, :].rearrange("q a b -> q (a b)")),
                             reads=[B_stat[st]], writes=[B_stat[st]])
                    for st in R4:
                        P.op(dve, lambda e, st=st: e.tensor_scalar(out=rstd[st][:], in0=mv[st][:, 1:2], scalar1=EPS, scalar2=None, op0=ALU.add),
                             reads=[B_stat[st]], writes=[B_stat[st]])
                    for st in R4:
                        P.op(act, lambda e, st=st: e.activation(out=rstd[st][:], in_=rstd[st][:], func=AF.Sqrt), reads=[B_stat[st]], writes=[B_stat[st]])
                    for st in R4:
                        P.op(dve, lambda e, st=st: e.reciprocal(out=rstd[st][:], in_=rstd[st][:]), reads=[B_stat[st]], writes=[B_stat[st]])
                    for st in R4:
                        P.op(dve, lambda e, st=st: e.scalar_tensor_tensor(out=nmr[st][:], in0=mv[st][:, 0:1], scalar=-1.0, in1=rstd[st][:], op0=ALU.mult, op1=ALU.mult),
                             reads=[B_stat[st]], writes=[B_stat[st]])
                    if dstT is not None:
                        tps = {}
                        for st in R4:
                            xi_ = st % 2
                            P.op(act, lambda e, st=st, xi_=xi_: e.activation(out=xnb[xi_][:, :], in_=rr_[:, st, :], func=AF.Identity,
                                                                            scale=rstd[st][:, 0:1], bias=nmr[st][:, 0:1]),
                                 reads=[B_stat[st], B_rr[st]], writes=[B_xnb[xi_]])
                            for hh in range(2):
                                pt, pbuf = next_ps()
                                tps[(st, hh)] = (pt, pbuf)
                                ptb = pt[:, :].bitcast(BF16)
                                for j in range(4):
                                    dtile = hh * 4 + j
                                    P.op(pe, lambda e, j=j, dtile=dtile, ptb=ptb, xi_=xi_: e.transpose(
                                        out=ptb[:, j * 128:(j + 1) * 128], in_=xnb[xi_][:, dtile * 128:(dtile + 1) * 128], identity=ident_b[:]),
                                        reads=[B_xnb[xi_], B_const], writes=[pbuf], inc=(j == 3))
                            if st >= 1:
                                evac_T(st - 1, tps, dstT, B_dstT)
                        evac_T(3, tps, dstT, B_dstT)
                    def tail(st):
                        P.op(act, lambda e: e.activation(out=rr_[:, st, :], in_=rr_[:, st, :], func=AF.Identity, scale=rstd[st][:, 0:1], bias=nmr[st][:, 0:1]),
                             reads=[B_stat[st], B_rr[st]], writes=[B_rr[st]])
                        P.op(dve, lambda e: e.tensor_tensor(out=rr_[:, st, :], in0=rr_[:, st, :], in1=lnt[gk][:, :], op=ALU.mult),
                             reads=[B_rr[st], B_ln], writes=[B_rr[st]])
                        P.op(dve, lambda e: e.tensor_tensor(out=xt[:, st, :], in0=rr_[:, st, :], in1=lnt[bk][:, :], op=ALU.add),
                             reads=[B_rr[st], B_ln], writes=[B_xt[st]])
                        if store_rows is not None:
                            P.dma(pool, store_rows(st), xt[:, st, :], store_sem, reads=[B_xt[st]], writes=[store_buf])
                    for st in R4:
                        pending.append(lambda st=st: tail(st))

                def evac_T(st, tps, dstT, B_dstT):
                    for hh in range(2):
                        pt, pbuf = tps[(st, hh)]
                        ptb = pt[:, :].bitcast(BF16)
                        for j in range(4):
                            dtile = hh * 4 + j
                            P.op(dve, lambda e, j=j, dtile=dtile, ptb=ptb, st=st: e.tensor_scalar(
                                out=dstT[:, dtile, st * 128:(st + 1) * 128], in0=ptb[:, j * 128:(j + 1) * 128],
                                scalar1=gT[:, dtile:dtile + 1], scalar2=bT[:, dtile:dtile + 1], op0=ALU.mult, op1=ALU.add),
                                reads=[pbuf, B_ln], writes=[B_dstT])

                def ffn(layer):
                    fm_gated(x1T, B_x1T, wb_gu[layer], "gu%d" % layer, 8, 22, "swiglu", h2T, B_h2T)
                    tm_proj_resid(h2T, B_h2T, wb_dn[layer], "dn%d" % layer, 22, xt, B_xt)


                load_ln(layer)

                def rows(base, grp, st):
                    if layer == 0:
                        seg, lq = grp // 2, grp % 2
                        r0 = seg * 1024 + lq * 4 + st
                        return base[r0:r0 + 1017:8, :]
                    tt = grp * 4 + st
                    return base[tt * 128:(tt + 1) * 128, :]

                def load_inputs(grp):
                    src_fm = HTd if layer == 0 else attTd
                    B_srcfm = B_HTd if layer == 0 else B_attTd
                    P.dma(sp, inT[:, :, :], src_fm[:, grp * 512:(grp + 1) * 512].rearrange("(dt q) t -> q dt t", q=128), s_in,
                          reads=[B_srcfm], writes=[B_inT])
                    for st in range(4):
                        if layer == 0:
                            P.dma(sp, xin[:, st, :], rows(x, grp, st), s_xt, writes=[B_xin[st]])
                        else:
                            P.dma(sp, xin[:, st, :], rows(x2d, grp, st), s_xt, reads=[B_x2d], writes=[B_xin[st]])
                    for st in range(4):
                        B_xin[st].w = {s_xt: s_xt.v}

                load_inputs(0)
                for grp in range(8):
                    if layer == 0:
                        fm_gated(inT, B_inT, wb_glu, "glu", 8, 8, "glu", zT, B_zT)
                        tm_proj_resid(zT, B_zT, wb_out, "out", 8, xin, B_xin)
                    else:
                        tm_proj_resid(inT, B_inT, wb_o, "o", 8, xin, B_xin)
                    if grp + 1 < 8:
                        load_inputs(grp + 1)
                    layer_norm("mg", "mb", x1T, B_x1T)
                    ffn(layer)
                    if layer == 0:
                        layer_norm("fg", "fb", None, None, store_rows=lambda st, grp=grp: rows(x2d, grp, st), store_buf=B_x2d, store_sem=s_o)
                    else:
                        layer_norm("fg", "fb", None, None, store_rows=lambda st, grp=grp: rows(y, grp, st), store_buf=B_y, store_sem=s_o)
                flush_pending()
                P.barrier()

        dense_phase(0)

        with ExitStack() as ea:
            x2T = sbt(ea, "x2T", [128, 8, L], BF16)
            attT = [sbt(ea, "attT%d" % i, [128, L], BF16) for i in range(2)]
            B_x2T = Buf("x2T"); B_attT = [Buf("attT0"), Buf("attT1")]
            rr_ = sbt(ea, "rrA", [128, 2, D], F32); xnbA = [sbt(ea, "xnbA%d" % i, [128, D], BF16) for i in range(2)]
            B_rrs = [Buf("rrA0"), Buf("rrA1")]; B_xnbA = [Buf("xnbA0"), Buf("xnbA1")]
            s_xtA = [P.newsem("sxtA0"), P.newsem("sxtA1")]; s_at = P.newsem("sat")
            aw = [sbt(ea, "aw%d" % i, [128, 8, 128], BF16) for i in range(3)]
            awsem = [P.newsem("aw%d" % i) for i in range(3)]
            awbuf = [Buf("aw%d" % i) for i in range(3)]
            aw_rr = [0]

            def wload(src_ap, nkt, wb_buf, ncols=128):
                i = aw_rr[0] % 3
                aw_rr[0] += 1
                P.dma(sp, aw[i][:, 0:nkt, 0:ncols], src_ap.rearrange("(kt q) n -> q kt n", q=128), awsem[i],
                      reads=[wb_buf], writes=[awbuf[i]])
                return aw[i], awbuf[i]
            def X0(tt):
                st = tt % 2
                P.dma(sp, rr_[:, st, :], x2d[tt * 128:(tt + 1) * 128, :], s_xtA[st], reads=[B_x2d], writes=[B_rrs[st]])

            def X1(tt):
                st = tt % 2
                P.op(act, lambda e: e.activation(out=xnbA[st][:, :], in_=rr_[:, st, :], func=AF.Copy), reads=[B_rrs[st]], writes=[B_xnbA[st]])

            xps = {}

            def X2(tt):
                st = tt % 2
                for hh in range(2):
                    pt, pbuf = next_ps()
                    xps[(tt, hh)] = (pt, pbuf)
                    ptb = pt[:, :].bitcast(BF16)
                    for j in range(4):
                        dtile = hh * 4 + j
                        P.op(pe, lambda e, j=j, dtile=dtile, ptb=ptb: e.transpose(
                            out=ptb[:, j * 128:(j + 1) * 128], in_=xnbA[st][:, dtile * 128:(dtile + 1) * 128], identity=ident_b[:]),
                            reads=[B_xnbA[st], B_const], writes=[pbuf], inc=(j == 3))

            def X3(tt):
                for hh in range(2):
                    pt, pbuf = xps[(tt, hh)]
                    ptb = pt[:, :].bitcast(BF16)
                    E = dve if hh == 0 else act
                    if hh == 0:
                        P.op(dve, lambda e, hh=hh, ptb=ptb: e.tensor_copy(
                            out=x2T[:, hh * 4:(hh + 1) * 4, tt * 128:(tt + 1) * 128],
                            in_=ptb[:, 0:512].rearrange("q (j t) -> q j t", j=4)), reads=[pbuf], writes=[B_x2T])
                    else:
                        P.op(act, lambda e, hh=hh, ptb=ptb: e.activation(
                            out=x2T[:, hh * 4:(hh + 1) * 4, tt * 128:(tt + 1) * 128],
                            in_=ptb[:, 0:512].rearrange("q (j t) -> q j t", j=4), func=AF.Copy), reads=[pbuf], writes=[B_x2T])

            xst = [X0, X1, X2, X3]
            for tick in range(32 + len(xst) - 1):
                for si, fn in enumerate(xst):
                    tt = tick - si
                    if 0 <= tt < 32:
                        fn(tt)
            QT2 = [sbt(ea, "QT%d" % i, [128, L], BF16) for i in range(2)]
            KT2 = [sbt(ea, "KT%d" % i, [128, L], BF16) for i in range(2)]
            V2 = [sbt(ea, "V%d" % i, [128, 32, 128], BF16) for i in range(2)]
            B_QT2 = [Buf("QT0"), Buf("QT1")]; B_KT2 = [Buf("KT0"), Buf("KT1")]; B_V2 = [Buf("V0"), Buf("V1")]
            NBUF = 4
            SUBE = pool
            sig = [sbt(ea, "sig%d" % i, [128, 512], F32) for i in range(NBUF)]
            NIPX = 2
            ipx = [sbt(ea, "ipx%d" % i, [128, L + 4], F32) for i in range(NIPX)]
            wq = [sbt(ea, "wq%d" % i, [128, 512], BF16) for i in range(NBUF)]
            wTs = [sbt(ea, "wTs%d" % i, [128, 512], BF16) for i in range(NBUF)]
            B_sig = [Buf() for _ in range(NBUF)]; B_ipxc = [[Buf() for _ in range(10)] for _ in range(NIPX)]
            B_wq = [Buf() for _ in range(NBUF)]; B_wTs = [Buf() for _ in range(NBUF)]
            mlow = sbt(ea, "mlow", [128, 128], F32); mup = sbt(ea, "mup", [128, 128], F32)
            P.op(pool, lambda e: e.memset(mlow[:], 1.0), writes=[B_const])
            P.op(pool, lambda e: e.affine_select(out=mlow[:], in_=mlow[:], pattern=[[-1, 128]], base=0,
                                                 channel_multiplier=1, compare_op=ALU.is_gt, fill=0.0), writes=[B_const])
            P.op(pool, lambda e: e.memset(mup[:], 0.0), writes=[B_const])
            P.op(pool, lambda e: e.affine_select(out=mup[:], in_=mup[:], pattern=[[-1, 128]], base=0,
                                                 channel_multiplier=1, compare_op=ALU.is_gt, fill=1.0), writes=[B_const])
            negm = sbt(ea, "negm", [128, 128], BF16)
            P.op(pool, lambda e: e.memset(dummy[:, 0:1], 0.0), writes=[B_dummy])
            P.op(dve, lambda e: e.tensor_scalar(out=negm[:], in0=mup[:], scalar1=-30000.0, scalar2=None, op0=ALU.mult),
                 reads=[B_const], writes=[B_const])
            NROT[0] = 5
            bgp, bgb = psf[5], psb[5]

            def proj_gen(hp):
                pb_ = hp % 2
                QTd, KTd, Vd = QT2[pb_], KT2[pb_], V2[pb_]
                sq, bq = wload(wb_qkv[:, hp * 128:(hp + 1) * 128], 8, WB["qkv"], ncols=128)
                sk, bk_ = wload(wb_qkv[:, D + hp * 128:D + (hp + 1) * 128], 8, WB["qkv"], ncols=128)
                sv, bv = wload(wb_qkv[:, 2 * D + hp * 128:2 * D + (hp + 1) * 128], 8, WB["qkv"], ncols=128)
                for tb in range(8):
                    for (slot, sbuf, dstT, B_dstT, scl) in ((sq, bq, QTd, B_QT2[pb_], 0.125), (sk, bk_, KTd, B_KT2[pb_], 1.0)):
                        for kt in range(8):
                            P.op(pe, lambda e, slot=slot, kt=kt, tb=tb: e.matmul(
                                out=bgp[:, :], lhsT=slot[:, kt, 0:128], rhs=x2T[:, kt, tb * 512:(tb + 1) * 512],
                                start=(kt == 0), stop=(kt == 7)), reads=[sbuf, B_x2T], writes=[bgb], inc=(kt == 3 or kt == 7))
                            if kt == 3:
                                yield
                        yield
                        P.op(act, lambda e, dstT=dstT, tb=tb, scl=scl: e.activation(
                            out=dstT[:, tb * 512:(tb + 1) * 512], in_=bgp[:, :], func=AF.Copy, scale=scl),
                            reads=[bgb], writes=[B_dstT])
                        yield
                    for j in range(4):
                        kt_i = tb * 4 + j
                        for kt in range(8):
                            P.op(pe, lambda e, kt=kt, kt_i=kt_i: e.matmul(
                                out=bgp[:, 0:128], lhsT=x2T[:, kt, kt_i * 128:(kt_i + 1) * 128], rhs=sv[:, kt, 0:128],
                                start=(kt == 0), stop=(kt == 7)), reads=[bv, B_x2T], writes=[bgb], inc=(kt == 3 or kt == 7))
                            if kt == 3:
                                yield
                        yield
                        P.op(act, lambda e, kt_i=kt_i: e.activation(
                            out=Vd[:, kt_i, :], in_=bgp[:, 0:128], func=AF.Copy),
                            reads=[bgb], writes=[B_V2[pb_]])
                        yield

            for _ in proj_gen(0):
                pass
            for hp in range(8):
                QT, KT, V = QT2[hp % 2], KT2[hp % 2], V2[hp % 2]
                B_QT, B_KT, B_V = B_QT2[hp % 2], B_KT2[hp % 2], B_V2[hp % 2]
                bg = proj_gen(hp + 1) if hp + 1 < 8 else iter(())
                tasks = []
                for h2 in range(2):
                    for qt in range(32):
                        hi = 128 * (qt + 1)
                        chunks = []
                        while hi > 0:
                            lo = ((hi - 1) // 512) * 512
                            chunks.append((lo, hi))
                            hi = lo
                        poh = {}
                        for ci_, (lo, hi) in enumerate(chunks):
                            tasks.append(dict(h2=h2, qt=qt, ci=ci_, lo=lo, hi=hi, n=len(chunks), poh=poh, bi=(h2 * 32 + qt) % NIPX))

                def S0(k, T):
                    hs = slice(64 * T["h2"], 64 * T["h2"] + 64)
                    W = T["hi"] - T["lo"]
                    pz, pzb = next_ps()
                    T["pz"], T["pzb"] = pz, pzb
                    qt, lo, hi = T["qt"], T["lo"], T["hi"]
                    if T["ci"] == 0:
                        P.op(pe, lambda e: e.matmul(out=pz[:, 0:W], lhsT=QT[hs, qt * 128:(qt + 1) * 128], rhs=KT[hs, lo:hi], start=True, stop=False),
                             reads=[B_QT, B_KT], writes=[pzb], inc=False)
                        P.op(pe, lambda e: e.matmul(out=pz[:, W - 128:W], lhsT=ident_b[:], rhs=negm[:], start=False, stop=True),
                             reads=[B_const], writes=[pzb])
                    else:
                        P.op(pe, lambda e: e.matmul(out=pz[:, 0:W], lhsT=QT[hs, qt * 128:(qt + 1) * 128], rhs=KT[hs, lo:hi], start=True, stop=True),
                             reads=[B_QT, B_KT], writes=[pzb])

                def S1(k, T):
                    i = k % NBUF
                    W = T["hi"] - T["lo"]
                    pz, pzb = T["pz"], T["pzb"]
                    P.op(act, lambda e: e.activation(out=sig[i][:, 0:W], in_=pz[:, 0:W], func=AF.Sigmoid, scale=-1.0),
                         reads=[pzb], writes=[B_sig[i]])

                def S2(k, T):
                    i = k % NBUF
                    b = T["bi"]
                    lo, hi = T["lo"], T["hi"]
                    W = hi - lo
                    tk = B_ipxc[b][lo // 512]
                    tc = B_ipxc[b][hi // 512]
                    if T["ci"] == 0:
                        P.op(dve, lambda e: e.memset(ipx[b][:, hi:hi + 1], 1.0), writes=[tc])
                    P.op(dve, lambda e: e.tensor_tensor_scan(
                        out=ipx[b][:, lo:hi][:, ::-1], data0=sig[i][:, 0:W][:, ::-1], data1=sig[i][:, 0:W][:, ::-1],
                        initial=ipx[b][:, hi:hi + 1], op0=ALU.mult, op1=ALU.bypass), reads=[B_sig[i], tc], writes=[tk])
                    P.op(pool if (k % 8) != 7 else dve, lambda e: e.tensor_tensor(out=wq[i][:, 0:W], in0=ipx[b][:, lo + 1:hi + 1], in1=ipx[b][:, lo:hi], op=ALU.subtract),
                         reads=[tk, tc], writes=[B_wq[i]])

                def S3(k, T):
                    i = k % NBUF
                    W = T["hi"] - T["lo"]
                    nb = W // 128
                    pw, pwb = next_ps()
                    T["pw"], T["pwb"] = pw, pwb
                    pwb16 = pw[:, :].bitcast(BF16)
                    for j in range(nb):
                        P.op(pe, lambda e, j=j: e.transpose(
                            out=pwb16[:, j * 128:(j + 1) * 128], in_=wq[i][:, j * 128:(j + 1) * 128], identity=ident_b[:]),
                            reads=[B_wq[i], B_const], writes=[pwb], inc=(j == nb - 1))

                def S4(k, T):
                    i = k % NBUF
                    W = T["hi"] - T["lo"]
                    pwb16 = T["pw"][:, :].bitcast(BF16)
                    P.op(act, lambda e: e.activation(out=wTs[i][:, 0:W], in_=pwb16[:, 0:W], func=AF.Copy),
                         reads=[T["pwb"]], writes=[B_wTs[i]])

                def S5(k, T):
                    i = k % NBUF
                    hs = slice(64 * T["h2"], 64 * T["h2"] + 64)
                    W = T["hi"] - T["lo"]
                    nb = W // 128
                    if T["ci"] == 0:
                        T["poh"]["po"] = next_ps(long=True)
                    po, pob = T["poh"]["po"]
                    for j in range(nb):
                        ktile = T["lo"] // 128 + j
                        first = (T["ci"] == 0 and j == 0)
                        last = (T["ci"] == T["n"] - 1 and j == nb - 1)
                        P.op(pe, lambda e, j=j, ktile=ktile, first=first, last=last: e.matmul(
                            out=po[hs, 0:128], lhsT=V[:, ktile, hs], rhs=wTs[i][:, j * 128:(j + 1) * 128],
                            start=first, stop=last), reads=[B_wTs[i], B_V], writes=[pob], inc=(j == nb - 1))
                    if T["ci"] == T["n"] - 1:
                        qt = T["qt"]
                        P.op(act, lambda e: e.activation(
                            out=attT[hp % 2][hs, qt * 128:(qt + 1) * 128], in_=po[hs, 0:128], func=AF.Copy),
                            reads=[pob], writes=[B_attT[hp % 2]])

                stages = [S0, S1, S2, S3, S4, S5]
                offs = [0, 1, 3, 5, 6, 8]
                for tick in range(len(tasks) + offs[-1]):
                    for si, fn in enumerate(stages):
                        k = tick - offs[si]
                        if 0 <= k < len(tasks):
                            fn(k, tasks[k])
                    if tick % 2 == 1:
                        next(bg, None)
                for _ in bg:
                    pass
                P.dma(pool, attTd[hp * 128:(hp + 1) * 128, :], attT[hp % 2][:, :], s_at, reads=[B_attT[hp % 2]], writes=[B_attTd])
            P.barrier()


        NROT[0] = 6
        dense_phase(1)
        sp.wait(s_o, s_o.v)
    return nc


_NC = None


def kernel(**inputs):
    global _NC
    if _NC is None:
        _NC = build_program()
    nc = _NC
    f = lambda a: np.ascontiguousarray(np.asarray(a, dtype=np.float32))
    shared = {
        "ssm_a_re": f(inputs["ssm_a_re"][0]), "ssm_a_im": f(inputs["ssm_a_im"][0]),
        "ssm_log_dt": f(inputs["ssm_log_dt"][0]),
        "ssm_b_re": f(inputs["ssm_b_re"][0]), "ssm_b_im": f(inputs["ssm_b_im"][0]),
        "ssm_c_re": f(inputs["ssm_c_re"][0]), "ssm_c_im": f(inputs["ssm_c_im"][0]),
        "ssm_d": f(inputs["ssm_d"][0]),
        "ssm_w_glu": f(inputs["ssm_w_glu"][0]), "ssm_w_out": f(inputs["ssm_w_out"][0]),
        "sb_w_qkv": f(inputs["sb_w_qkv"][0]), "sb_w_o": f(inputs["sb_w_o"][0]),
    }
    for i in range(2):
        shared["ffn_w_gu%d" % i] = f(inputs["ffn_w_gu"][i])
        shared["ffn_w_down%d" % i] = f(inputs["ffn_w_down"][i])
        for nm in ("ln_mix_g", "ln_mix_b", "ln_ffn_g", "ln_ffn_b"):
            shared["%s%d" % (nm, i)] = f(inputs[nm][i])
    xs = np.asarray(inputs["x"], dtype=np.float32)
    in_maps = []
    for b in range(8):
        m = dict(shared)
        m["x"] = np.ascontiguousarray(xs[b])
        in_maps.append(m)
    res = run_bass_kernel_spmd(nc, in_maps, core_ids=list(range(8)))
    return np.stack([np.asarray(r["y"], dtype=np.float32) for r in res.results], axis=0)
```

```python
import math
from contextlib import ExitStack
import numpy as np
import concourse.bass as bass
import concourse.mybir as mybir
from concourse.bass_utils import run_bass_kernel_spmd

F32 = mybir.dt.float32
BF16 = mybir.dt.bfloat16
ALU = mybir.AluOpType
AF = mybir.ActivationFunctionType

D = 1024
L = 4096
FH = 2816
NG = 64
ALPHA = 4.0 ** 0.25
EPS = 1e-5
PI = math.pi
MAGIC = 12582912.0
GELU_C = 1.5957691216057308
KV = [float(k) for k in range(-7, 9)] + [float(7 - m) for m in range(8)] + [float(8 * j) for j in range(2, 9)]
L8IDX = [15] + list(range(24, 31))
NKV = len(KV)


class Sem:
    def __init__(self, h):
        self.h = h
        self.v = 0


class Buf:
    def __init__(self, name=""):
        self.name = name
        self.w = {}
        self.r = {}


class Eng:
    def __init__(self, name, eng, sem, is_pe=False):
        self.name = name
        self.eng = eng
        self.sem = sem
        self.waited = {}
        self.is_pe = is_pe

    def wait(self, s, v):
        if v <= 0:
            return
        if self.waited.get(s, 0) >= v:
            return
        self.eng.wait_ge(s.h, v)
        self.waited[s] = v


class Prog:
    def __init__(self, nc, es):
        self.nc = nc
        self.es = es
        self.nsem = 0
        self.allsems = []
        self.pe = Eng("pe", nc.tensor, self.newsem("pe"), True)
        self.act = Eng("act", nc.scalar, self.newsem("act"))
        self.dve = Eng("dve", nc.vector, self.newsem("dve"))
        self.pool = Eng("pool", nc.gpsimd, self.newsem("pool"))
        self.sp = Eng("sp", nc.sync, self.newsem("sp"))
        self.engs = [self.pe, self.act, self.dve, self.pool, self.sp]

    def newsem(self, name):
        self.nsem += 1
        s = Sem(self.es.enter_context(self.nc.semaphore("s%d_%s" % (self.nsem, name))))
        self.allsems.append(s)
        return s

    def _deps(self, E, reads, writes):
        for b in reads:
            for s, v in b.w.items():
                if E.is_pe and s is E.sem:
                    continue
                E.wait(s, v)
        for b in writes:
            for s, v in list(b.w.items()) + list(b.r.items()):
                if E.is_pe and s is E.sem:
                    continue
                E.wait(s, v)

    def op(self, E, fn, reads=(), writes=(), inc=True):
        self._deps(E, reads, writes)
        inst = fn(E.eng)
        if inc:
            E.sem.v += 1
            inst.then_inc(E.sem.h, 1)
            val = E.sem.v
        else:
            val = E.sem.v + 1
        for b in reads:
            b.r[E.sem] = max(b.r.get(E.sem, 0), val)
        for b in writes:
            b.w = {E.sem: val}
            b.r = {}
        return inst

    def dma(self, Q, out, in_, dsem, reads=(), writes=(), **kw):
        self._deps(Q, reads, writes)
        inst = Q.eng.dma_start(out=out, in_=in_, **kw)
        dsem.v += 16
        inst.then_inc(dsem.h, 16)
        for b in reads:
            b.r[dsem] = max(b.r.get(dsem, 0), dsem.v)
        for b in writes:
            b.w = {dsem: dsem.v}
            b.r = {}
        return inst

    def barrier(self):
        for E in self.engs:
            for s in self.allsems:
                if s is E.sem:
                    continue
                E.wait(s, s.v)


def build_program():
    nc = bass.Bass("TRN2", target_bir_lowering=False)
    dram = {}

    def din(name, shape):
        dram[name] = nc.dram_tensor(name, list(shape), F32, kind="ExternalInput").ap()
        return dram[name]

    x = din("x", [L, D])
    a_re = din("ssm_a_re", [NG, 64]); a_im = din("ssm_a_im", [NG, 64]); log_dt = din("ssm_log_dt", [NG])
    b_re = din("ssm_b_re", [NG, 64, 16]); b_im = din("ssm_b_im", [NG, 64, 16])
    c_re = din("ssm_c_re", [NG, 16, 64]); c_im = din("ssm_c_im", [NG, 16, 64])
    d_skip = din("ssm_d", [D])
    w_glu = din("ssm_w_glu", [D, 2 * D]); w_out = din("ssm_w_out", [D, D])
    w_qkv = din("sb_w_qkv", [D, 3 * D]); w_o = din("sb_w_o", [D, D])
    w_gu = [din("ffn_w_gu%d" % i, [D, 2 * FH]) for i in range(2)]
    w_dn = [din("ffn_w_down%d" % i, [FH, D]) for i in range(2)]
    lnp = {}
    for nm in ("ln_mix_g", "ln_mix_b", "ln_ffn_g", "ln_ffn_b"):
        for i in range(2):
            lnp[(nm, i)] = din("%s%d" % (nm, i), [D])
    y = nc.dram_tensor("y", [L, D], F32, kind="ExternalOutput").ap()

    def dscr(name, shape, dt):
        return nc.dram_tensor(name, list(shape), dt, kind="Internal").ap()

    wb_glu = dscr("wb_glu", [D, 2 * D], BF16); wb_out = dscr("wb_out", [D, D], BF16)
    wb_qkv = dscr("wb_qkv", [D, 3 * D], BF16); wb_o = dscr("wb_o", [D, D], BF16)
    wb_gu = [dscr("wb_gu%d" % i, [D, 2 * FH], BF16) for i in range(2)]
    wb_dn = [dscr("wb_dn%d" % i, [FH, D], BF16) for i in range(2)]
    HTd = dscr("HTd", [D, L], BF16)
    x2d = dscr("x2d", [L, D], F32)
    attTd = dscr("attTd", [D, L], BF16)

    top = ExitStack()
    with top:
        P = Prog(nc, top)
        pe, act, dve, pool, sp = P.pe, P.act, P.dve, P.pool, P.sp

        uniq = [0]

        def sbt(es, name, shape, dt):
            uniq[0] += 1
            return es.enter_context(nc.sbuf_tensor("%s_%d" % (name, uniq[0]), list(shape), dt))

        def pst(es, name, shape, dt):
            return es.enter_context(nc.psum_tensor(name, list(shape), dt))

        ident_f = sbt(top, "ident_f", [128, 128], F32)
        dummy = sbt(top, "dummy", [128, 8], F32)
        B_dummy = Buf("dummy")
        ident_b = sbt(top, "ident_b", [128, 128], BF16)
        B_const = Buf("const")
        P.op(pool, lambda e: e.memset(ident_f[:], 1.0), writes=[B_const])
        P.op(pool, lambda e: e.affine_select(out=ident_f[:], in_=ident_f[:], pattern=[[1, 128]], base=0,
                                             channel_multiplier=-1, compare_op=ALU.is_equal, fill=0.0),
             writes=[B_const])
        P.op(dve, lambda e: e.tensor_copy(out=ident_b[:], in_=ident_f[:]), reads=[B_const], writes=[B_const])

        NB = 8
        psf = [pst(top, "psb%d" % i, [128, 512], F32) for i in range(NB)]
        psb = [Buf("ps%d" % i) for i in range(NB)]
        ps_rr = [0]

        po_rr = [0]

        NROT = [6]

        def next_ps(long=False):
            if long:
                i = 6 + po_rr[0] % 2
                po_rr[0] += 1
            else:
                i = ps_rr[0] % NROT[0]
                ps_rr[0] += 1
            return psf[i], psb[i]

        WB = {}

        def cast_weight(key, src, dst, rows, cols):
            s = P.newsem("wc")
            b = Buf("wb_" + key)
            for r0 in range(0, rows, 128):
                for c0 in range(0, cols, 2048):
                    c1 = min(cols, c0 + 2048)
                    P.dma(pool, dst[r0:r0 + 128, c0:c1], src[r0:r0 + 128, c0:c1], s)
            b.w = {s: s.v}
            WB[key] = b


        B_HTd = Buf("HTd")
        B_x2d = Buf("x2d")
        B_attTd = Buf("attTd")
        B_y = Buf("y")
        s_o = P.newsem("so")

        sS = ExitStack()
        AJ1 = sbt(sS, "AJ1", [128, 8, 2, 32], F32); AJ2 = sbt(sS, "AJ2", [128, 8, 2, 32], F32)
        WyR = sbt(sS, "WyR", [128, 32, 128], BF16); WyI = sbt(sS, "WyI", [128, 32, 128], BF16)
        Kb = sbt(sS, "Kb", [128, NG, 128], BF16)
        WzR = sbt(sS, "WzR", [128, NG, 64], BF16); WzI = sbt(sS, "WzI", [128, NG, 64], BF16)
        with ExitStack() as es:
            s_ld = P.newsem("sld")
            are = sbt(es, "are", [128, 32], F32); aim = sbt(es, "aim", [128, 32], F32)
            dtl = sbt(es, "dtl", [128, 32], F32)
            Bre = sbt(es, "Bre", [128, 32, 16], F32); Bim = sbt(es, "Bim", [128, 32, 16], F32)
            Cre = sbt(es, "Cre", [128, 32, 16], F32); Cim = sbt(es, "Cim", [128, 32, 16], F32)
            Tc = [sbt(es, "Tc%d" % i, [128, 4, 2, 64], F32) for i in range(2)]
            dsk = sbt(es, "dsk", [128, 64], F32)
            B_par = Buf("par")
            for g2 in range(2):
                sl = slice(64 * g2, 64 * g2 + 64)
                gs = slice(32 * g2, 32 * g2 + 32)
                P.dma(sp, are[sl, :], a_re[gs, :].rearrange("g p -> p g"), s_ld, writes=[B_par], allow_slow_non_contiguous=True)
                P.dma(sp, aim[sl, :], a_im[gs, :].rearrange("g p -> p g"), s_ld, writes=[B_par], allow_slow_non_contiguous=True)
                P.dma(sp, dtl[sl, :], log_dt[gs].partition_broadcast(64), s_ld, writes=[B_par])
                P.dma(sp, Bre[sl, :, :], b_re[gs, :, :].rearrange("g p h -> p g h"), s_ld, writes=[B_par])
                P.dma(sp, Bim[sl, :, :], b_im[gs, :, :].rearrange("g p h -> p g h"), s_ld, writes=[B_par])
            for ri_, csrc in enumerate((c_re, c_im)):
                cv = csrc.rearrange("(g2 gt gp8) h p -> (gp8 h) gt g2 p", g2=2, gt=4)
                for g2 in range(2):
                    P.dma(sp, Tc[ri_][:, :, g2, :], cv[:, :, g2, :], s_ld, writes=[B_par])
            for m in range(8):
                P.dma(sp, dsk[16 * m:16 * m + 16, :], d_skip.rearrange("(g h) -> h g", h=16), s_ld, writes=[B_par],
                      allow_slow_non_contiguous=True)
            B_par.w = {s_ld: s_ld.v}
            pool.wait(s_ld, s_ld.v)
            cast_weight("glu", w_glu, wb_glu, D, 2 * D)
            cast_weight("out", w_out, wb_out, D, D)
            cast_weight("gu0", w_gu[0], wb_gu[0], D, 2 * FH)
            cast_weight("dn0", w_dn[0], wb_dn[0], FH, D)

            B_C = Buf("C")
            for ri, (T, Cd) in enumerate(((Tc[0], Cre), (Tc[1], Cim))):
                pt, pb = next_ps()
                for gt in range(4):
                    P.op(pe, lambda e, gt=gt, T=T, pt=pt: e.transpose(
                        out=pt[:, gt * 128:(gt + 1) * 128], in_=T[:, gt, :, :].rearrange("q a p -> q (a p)"),
                        identity=ident_f[:]), reads=[B_par, B_const], writes=[pb], inc=(gt == 3))
                P.op(dve, lambda e, Cd=Cd, pt=pt: e.tensor_copy(
                    out=Cd[:].rearrange("q g h -> q (g h)"), in_=pt[:, :]), reads=[pb], writes=[B_C])

            B_t = Buf("tab")

            def dv(fn, reads=(B_par,), writes=None):
                return P.op(dve, fn, reads=list(reads) + [B_t, B_C], writes=[B_t] if writes is None else writes)

            dt_ = sbt(es, "dt_", [128, 32], F32); lre = sbt(es, "lre", [128, 32], F32)
            xr = sbt(es, "xr", [128, 32], F32); xi = sbt(es, "xi", [128, 32], F32)
            P.op(act, lambda e: e.activation(out=dt_[:], in_=dtl[:], func=AF.Exp), reads=[B_par], writes=[B_t])
            dv(lambda e: e.tensor_scalar(out=lre[:], in0=are[:], scalar1=-1e-4, scalar2=None, op0=ALU.min))
            dv(lambda e: e.tensor_tensor(out=xr[:], in0=lre[:], in1=dt_[:], op=ALU.mult))
            dv(lambda e: e.tensor_tensor(out=xi[:], in0=aim[:], in1=dt_[:], op=ALU.mult))
            kv = sbt(es, "kv", [128, NKV], F32)
            for i, k in enumerate(KV):
                dv(lambda e, i=i, k=k: e.memset(kv[:, i:i + 1], k))
            T3 = [128, 32, NKV]
            ang = sbt(es, "ang", T3, F32); expo = sbt(es, "expo", T3, F32); mag = sbt(es, "mag", T3, F32)
            kk = sbt(es, "kk", T3, F32); sn = sbt(es, "sn", T3, F32); cs = sbt(es, "cs", T3, F32)
            PwR = sbt(es, "PwR", T3, F32); PwI = sbt(es, "PwI", T3, F32)
            kvb = kv[:, :].unsqueeze(1).broadcast_to(T3)
            dv(lambda e: e.tensor_tensor(out=ang[:], in0=xi[:, :].unsqueeze(2).broadcast_to(T3), in1=kvb, op=ALU.mult))
            dv(lambda e: e.tensor_tensor(out=expo[:], in0=xr[:, :].unsqueeze(2).broadcast_to(T3), in1=kvb, op=ALU.mult))
            P.op(act, lambda e: e.activation(out=mag[:], in_=expo[:], func=AF.Exp), reads=[B_t], writes=[B_t])

            def range_reduce(dst, src, shift):
                if shift != 0.0:
                    dv(lambda e: e.tensor_scalar(out=dst[:], in0=src[:], scalar1=shift, scalar2=None, op0=ALU.add))
                    s2 = dst
                else:
                    s2 = src
                dv(lambda e: e.tensor_scalar(out=kk[:], in0=s2[:], scalar1=1.0 / (2 * PI), scalar2=MAGIC, op0=ALU.mult, op1=ALU.add))
                dv(lambda e: e.tensor_scalar(out=kk[:], in0=kk[:], scalar1=MAGIC, scalar2=None, op0=ALU.subtract))
                dv(lambda e: e.scalar_tensor_tensor(out=dst[:], in0=kk[:], scalar=-2 * PI, in1=s2[:], op0=ALU.mult, op1=ALU.add))

            range_reduce(sn, ang, 0.0)
            P.op(act, lambda e: e.activation(out=sn[:], in_=sn[:], func=AF.Sin), reads=[B_t], writes=[B_t])
            range_reduce(cs, ang, PI / 2)
            P.op(act, lambda e: e.activation(out=cs[:], in_=cs[:], func=AF.Sin), reads=[B_t], writes=[B_t])
            dv(lambda e: e.tensor_tensor(out=PwR[:], in0=mag[:], in1=cs[:], op=ALU.mult))
            dv(lambda e: e.tensor_tensor(out=PwI[:], in0=mag[:], in1=sn[:], op=ALU.mult))
            t_a = sbt(es, "t_a", [128, 32], F32); t_b = sbt(es, "t_b", [128, 32], F32)
            nr = sbt(es, "nr", [128, 32], F32); den = sbt(es, "den", [128, 32], F32)
            cr = sbt(es, "cr", [128, 32], F32); ci = sbt(es, "ci", [128, 32], F32)
            LR = PwR[:, :, 8]; LI = PwI[:, :, 8]
            dv(lambda e: e.tensor_scalar(out=nr[:], in0=LR, scalar1=-1.0, scalar2=None, op0=ALU.add))
            dv(lambda e: e.tensor_tensor(out=den[:], in0=lre[:], in1=lre[:], op=ALU.mult))
            dv(lambda e: e.tensor_tensor(out=t_a[:], in0=aim[:], in1=aim[:], op=ALU.mult))
            dv(lambda e: e.tensor_tensor(out=den[:], in0=den[:], in1=t_a[:], op=ALU.add))
            dv(lambda e: e.reciprocal(out=den[:], in_=den[:]))
            dv(lambda e: e.tensor_tensor(out=t_a[:], in0=nr[:], in1=lre[:], op=ALU.mult))
            dv(lambda e: e.tensor_tensor(out=t_b[:], in0=LI, in1=aim[:], op=ALU.mult))
            dv(lambda e: e.tensor_tensor(out=t_a[:], in0=t_a[:], in1=t_b[:], op=ALU.add))
            dv(lambda e: e.tensor_tensor(out=cr[:], in0=t_a[:], in1=den[:], op=ALU.mult))
            dv(lambda e: e.tensor_tensor(out=t_a[:], in0=LI, in1=lre[:], op=ALU.mult))
            dv(lambda e: e.tensor_tensor(out=t_b[:], in0=nr[:], in1=aim[:], op=ALU.mult))
            dv(lambda e: e.tensor_tensor(out=t_a[:], in0=t_a[:], in1=t_b[:], op=ALU.subtract))
            dv(lambda e: e.tensor_tensor(out=ci[:], in0=t_a[:], in1=den[:], op=ALU.mult))
            T16 = [128, 32, 16]
            BbR = sbt(es, "BbR", T16, F32); BbI = sbt(es, "BbI", T16, F32)
            u1 = sbt(es, "u1", T16, F32)
            crb = cr[:, :].unsqueeze(2).broadcast_to(T16); cib = ci[:, :].unsqueeze(2).broadcast_to(T16)
            dv(lambda e: e.tensor_tensor(out=BbR[:], in0=Bre[:], in1=crb, op=ALU.mult))
            dv(lambda e: e.tensor_tensor(out=u1[:], in0=Bim[:], in1=cib, op=ALU.mult))
            dv(lambda e: e.tensor_tensor(out=BbR[:], in0=BbR[:], in1=u1[:], op=ALU.subtract))
            dv(lambda e: e.tensor_tensor(out=BbI[:], in0=Bim[:], in1=crb, op=ALU.mult))
            dv(lambda e: e.tensor_tensor(out=u1[:], in0=Bre[:], in1=cib, op=ALU.mult))
            dv(lambda e: e.tensor_tensor(out=BbI[:], in0=BbI[:], in1=u1[:], op=ALU.add))
            for j8, ki in enumerate(L8IDX):
                for r in range(2):
                    dv(lambda e, r=r, j8=j8, ki=ki: e.tensor_copy(out=AJ1[:, j8, r, :], in_=PwR[:, :, ki]))
                dv(lambda e, j8=j8, ki=ki: e.tensor_scalar(out=AJ2[:, j8, 0, :], in0=PwI[:, :, ki], scalar1=-1.0, scalar2=None, op0=ALU.mult))
                dv(lambda e, j8=j8, ki=ki: e.tensor_copy(out=AJ2[:, j8, 1, :], in_=PwI[:, :, ki]))

            T4 = [128, 32, 8, 16]
            E7R = sbt(es, "E7R", T4, F32); E7I = sbt(es, "E7I", T4, F32)
            F0R = sbt(es, "F0R", T4, F32); F0I = sbt(es, "F0I", T4, F32)
            w1 = sbt(es, "w1", T4, F32)

            def cmul(outR, outI, k0, XR, XI, negI=False):
                pr = PwR[:, :, k0:k0 + 8].unsqueeze(3).broadcast_to(T4)
                pi_ = PwI[:, :, k0:k0 + 8].unsqueeze(3).broadcast_to(T4)
                xr_ = XR[:, :, :].unsqueeze(2).broadcast_to(T4)
                xi_ = XI[:, :, :].unsqueeze(2).broadcast_to(T4)
                dv(lambda e: e.tensor_tensor(out=outR, in0=pr, in1=xr_, op=ALU.mult))
                dv(lambda e: e.tensor_tensor(out=w1[:], in0=pi_, in1=xi_, op=ALU.mult))
                dv(lambda e: e.tensor_tensor(out=outR, in0=outR, in1=w1[:], op=ALU.subtract))
                dv(lambda e: e.tensor_tensor(out=outI, in0=pr, in1=xi_, op=ALU.mult))
                dv(lambda e: e.tensor_tensor(out=w1[:], in0=pi_, in1=xr_, op=ALU.mult))
                if negI:
                    dv(lambda e: e.scalar_tensor_tensor(out=outI, in0=outI, scalar=-1.0, in1=w1[:], op0=ALU.mult, op1=ALU.subtract))
                else:
                    dv(lambda e: e.tensor_tensor(out=outI, in0=outI, in1=w1[:], op=ALU.add))

            cmul(E7R[:], E7I[:], 16, BbR, BbI)
            cmul(F0R[:], F0I[:], 8, Cre, Cim, negI=True)
            dv(lambda e: e.tensor_copy(out=WyR[:].rearrange("q g (l h) -> q g l h", l=8), in_=F0R[:]))
            dv(lambda e: e.tensor_copy(out=WyI[:].rearrange("q g (l h) -> q g l h", l=8), in_=F0I[:]))
            cmul(F0R[:], F0I[:], 0, Cre, Cim, negI=True)

            maskK = sbt(es, "maskK", [128, 128], F32)
            P.op(pool, lambda e: e.memset(maskK[:], 1.0), writes=[B_const])
            P.op(pool, lambda e: e.affine_select(out=maskK[:].rearrange("q (l h) -> q l h", l=8),
                                                 in_=maskK[:].rearrange("q (l h) -> q l h", l=8),
                                                 pattern=[[16, 8], [0, 16]], base=15, channel_multiplier=-1,
                                                 compare_op=ALU.is_ge, fill=0.0), writes=[B_const])
            P.op(pool, lambda e: e.memset(dummy[:, 0:1], 0.0), writes=[B_dummy])
            ktmp = sbt(es, "ktmp", [128, 4, 128], F32)
            B_K = Buf("K")
            for gb in range(NG // 4):
                pt, pb = next_ps()
                for j in range(4):
                    g = gb * 4 + j
                    g2, gp = g // 32, g % 32
                    sl = slice(64 * g2, 64 * g2 + 64)
                    P.op(pe, lambda e, j=j, sl=sl, gp=gp, pt=pt: e.matmul(
                        out=pt[:, j * 128:(j + 1) * 128], lhsT=E7R[sl, gp, :, :].rearrange("q m h -> q (m h)"),
                        rhs=F0R[sl, gp, :, :].rearrange("q m h -> q (m h)"), start=True, stop=False),
                        reads=[B_t], writes=[pb], inc=False)
                    P.op(pe, lambda e, j=j, sl=sl, gp=gp, pt=pt: e.matmul(
                        out=pt[:, j * 128:(j + 1) * 128], lhsT=E7I[sl, gp, :, :].rearrange("q m h -> q (m h)"),
                        rhs=F0I[sl, gp, :, :].rearrange("q m h -> q (m h)"), start=False, stop=True),
                        reads=[B_t], writes=[pb], inc=(j == 3))
                P.op(dve, lambda e, pt=pt: e.tensor_tensor(
                    out=ktmp[:], in0=pt[:, :].rearrange("q (j n) -> q j n", j=4),
                    in1=maskK[:, :].unsqueeze(1).broadcast_to([128, 4, 128]), op=ALU.mult),
                    reads=[pb, B_const], writes=[B_K])
                for j in range(4):
                    g = gb * 4 + j
                    P.op(dve, lambda e, j=j, g=g: e.scalar_tensor_tensor(
                        out=Kb[:, g, :], in0=ident_f[:], scalar=dsk[:, g:g + 1], in1=ktmp[:, j, :],
                        op0=ALU.mult, op1=ALU.add), reads=[B_K, B_par, B_const], writes=[B_K])
            for ri, (E7, Wz) in enumerate(((E7R, WzR), (E7I, WzI))):
                for gb in range(NG // 8):
                    pt, pb = next_ps()
                    for j in range(8):
                        g = gb * 8 + j
                        g2, gp = g // 32, g % 32
                        sl = slice(64 * g2, 64 * g2 + 64)
                        P.op(pe, lambda e, j=j, sl=sl, gp=gp, pt=pt, E7=E7: e.transpose(
                            out=pt[:, j * 64:(j + 1) * 64], in_=E7[sl, gp, :, :].rearrange("q m h -> q (m h)"),
                            identity=ident_f[sl, sl]), reads=[B_t, B_const], writes=[pb], inc=(j == 7))
                    P.op(dve, lambda e, pt=pt, Wz=Wz, gb=gb: e.tensor_copy(
                        out=Wz[:, gb * 8:(gb + 1) * 8, :].rearrange("q g p -> q (g p)"), in_=pt[:, :]),
                        reads=[pb], writes=[B_K])
            P.barrier()

        with ExitStack() as es:
            BIG = sbt(es, "BIG", [128, 8192], F32)
            Xg = sbt(es, "Xg", [128, 64, 128], BF16)
            U = sbt(es, "U", [128, 64, 128], BF16)
            carry = sbt(es, "carry", [128, 2, 32], F32)
            HTc = sbt(es, "HTc", [128, 8, 1024], BF16)
            HTs = sbt(es, "HTs", [128, 8, 1024], BF16)
            tq0 = [sbt(es, "tq0_%d" % i, [128, 512], F32) for i in range(4)]
            ysb = [sbt(es, "ysb%d" % i, [128, 512], F32) for i in range(4)]
            B_tq0 = [Buf() for _ in range(4)]; B_ysb = [Buf() for _ in range(4)]
            hT = [sbt(es, "hT%d" % i, [128, 512], BF16) for i in range(2)]
            st1 = sbt(es, "st1", [128, 2, 32], F32); st2 = sbt(es, "st2", [128, 2, 32], F32)
            sA = sbt(es, "sA", [128, 2, 32, 16], F32); sB = sbt(es, "sB", [128, 2, 32, 16], F32)
            CinT = sbt(es, "CinT", [128, 2, 32, 16], F32)
            B_cin = Buf("cin")
            B_BIG = Buf("BIG"); B_Xg = Buf("Xg"); B_U = Buf("U"); B_car = Buf("carry")
            B_HTc = Buf("HTc"); B_HTs = Buf("HTs"); B_st = Buf("st")
            B_tq = [Buf() for _ in range(4)]; B_hT = [Buf() for _ in range(2)]
            s_x = P.newsem("sx"); s_ht = P.newsem("sht")
            Xc4 = BIG[:, :].rearrange("q (m g h) -> q g m h", m=8, g=64)
            Z4 = BIG[:, :].rearrange("q (r g c) -> q r g c", r=2, g=32)
            Sb4 = Xg[:].rearrange("q (r g) c -> q r g c", r=2)
            P.op(dve, lambda e: e.memset(carry[:], 0.0), writes=[B_car])
            for seg in range(4):
                P.dma(sp, BIG[:, :], x[seg * 1024:(seg + 1) * 1024, :].rearrange("(c m) d -> c (m d)", m=8), s_x,
                      writes=[B_BIG])
                if seg == 1:
                    pool.wait(s_x, s_x.v)
                    cast_weight("qkv", w_qkv, wb_qkv, D, 3 * D)
                    cast_weight("o", w_o, wb_o, D, D)
                    cast_weight("gu1", w_gu[1], wb_gu[1], D, 2 * FH)
                    cast_weight("dn1", w_dn[1], wb_dn[1], FH, D)
                for half in range(2):
                    E = act if half == 0 else dve
                    gs = slice(32 * half, 32 * half + 32)
                    if half == 0:
                        P.op(act, lambda e, gs=gs: e.activation(
                            out=Xg[:, gs, :].rearrange("q g (m h) -> q g m h", m=8), in_=Xc4[:, gs, :, :], func=AF.Copy),
                            reads=[B_BIG], writes=[B_Xg])
                    else:
                        P.op(dve, lambda e, gs=gs: e.tensor_copy(
                            out=Xg[:, gs, :].rearrange("q g (m h) -> q g m h", m=8), in_=Xc4[:, gs, :, :]),
                            reads=[B_BIG], writes=[B_Xg])
                for gb in range(16):
                    pt, pb = next_ps()
                    ptb = pt[:, :].bitcast(BF16)
                    for j in range(4):
                        g = gb * 4 + j
                        P.op(pe, lambda e, j=j, g=g, ptb=ptb: e.transpose(
                            out=ptb[:, j * 128:(j + 1) * 128], in_=Xg[:, g, :], identity=ident_b[:]),
                            reads=[B_Xg, B_const], writes=[pb], inc=(j == 3))
                    E = act if gb % 2 == 0 else dve
                    if gb % 2 == 0:
                        P.op(act, lambda e, gb=gb, ptb=ptb: e.activation(
                            out=U[:, gb * 4:(gb + 1) * 4, :].rearrange("q g c -> q (g c)"), in_=ptb[:, 0:512], func=AF.Copy),
                            reads=[pb], writes=[B_U])
                    else:
                        P.op(dve, lambda e, gb=gb, ptb=ptb: e.tensor_copy(
                            out=U[:, gb * 4:(gb + 1) * 4, :].rearrange("q g c -> q (g c)"), in_=ptb[:, 0:512]),
                            reads=[pb], writes=[B_U])
                for gpb in range(8):
                    for ri, Wz in enumerate((WzR, WzI)):
                        pt, pb = next_ps()
                        for j in range(4):
                            gp = gpb * 4 + j
                            for g2 in range(2):
                                g = g2 * 32 + gp
                                P.op(pe, lambda e, j=j, g2=g2, g=g, pt=pt, Wz=Wz: e.matmul(
                                    out=pt[64 * g2:64 * g2 + 64, j * 128:(j + 1) * 128], lhsT=Wz[:, g, :], rhs=U[:, g, :],
                                    start=True, stop=True), reads=[B_U, B_K], writes=[pb], inc=(j == 3 and g2 == 1))
                        if ri == 0:
                            P.op(act, lambda e, pt=pt, gpb=gpb, ri=ri: e.activation(
                                out=Z4[:, ri, gpb * 4:(gpb + 1) * 4, :].rearrange("q g c -> q (g c)"), in_=pt[:, :], func=AF.Copy),
                                reads=[pb, B_Xg], writes=[B_BIG])
                        else:
                            P.op(dve, lambda e, pt=pt, gpb=gpb, ri=ri: e.tensor_copy(
                                out=Z4[:, ri, gpb * 4:(gpb + 1) * 4, :].rearrange("q g c -> q (g c)"), in_=pt[:, :]),
                                reads=[pb, B_Xg], writes=[B_BIG])
                P.op(dve, lambda e: e.tensor_copy(out=Sb4[:, :, :, 0], in_=carry[:]), reads=[B_car, B_Xg, B_BIG], writes=[B_Xg])
                T4s = [128, 2, 32, 16]

                def cstep(dst, prev, prev_sw, j8, add_to, big):
                    a1 = AJ1[:, j8, :, :]
                    a2 = AJ2[:, j8, :, :]
                    ta, tb = (sA, sB) if big else (st1, st2)
                    if big:
                        a1 = a1.unsqueeze(3).broadcast_to(T4s)
                        a2 = a2.unsqueeze(3).broadcast_to(T4s)
                    P.op(dve, lambda e: e.tensor_tensor(out=ta[:], in0=a1, in1=prev, op=ALU.mult),
                         reads=[B_BIG, B_car, B_t, B_cin], writes=[B_st])
                    P.op(dve, lambda e: e.tensor_tensor(out=tb[:], in0=a2, in1=prev_sw, op=ALU.mult),
                         reads=[B_BIG, B_car, B_st, B_cin], writes=[B_st])
                    P.op(dve, lambda e: e.tensor_tensor(out=ta[:], in0=ta[:], in1=tb[:], op=ALU.add),
                         reads=[B_st], writes=[B_st])
                    return ta

                for j in range(1, 8):
                    ta = cstep(None, Z4[:, :, :, j - 1:128:8], Z4[:, ::-1, :, j - 1:128:8], 0, None, True)
                    P.op(dve, lambda e, j=j, ta=ta: e.tensor_tensor(out=Z4[:, :, :, j:128:8], in0=Z4[:, :, :, j:128:8], in1=ta[:], op=ALU.add),
                         reads=[B_st], writes=[B_BIG])
                P.op(dve, lambda e: e.tensor_copy(out=CinT[:, :, :, 0], in_=carry[:]), reads=[B_car], writes=[B_cin])
                for b in range(15):
                    ta = cstep(None, CinT[:, :, :, b], CinT[:, ::-1, :, b], 7, None, False)
                    P.op(dve, lambda e, b=b, ta=ta: e.tensor_tensor(out=CinT[:, :, :, b + 1], in0=Z4[:, :, :, b * 8 + 7], in1=ta[:], op=ALU.add),
                         reads=[B_st, B_BIG], writes=[B_cin])
                for j in range(8):
                    ta = cstep(None, CinT[:, :, :, :], CinT[:, ::-1, :, :], j, None, True)
                    P.op(dve, lambda e, j=j, ta=ta: e.tensor_tensor(out=Z4[:, :, :, j:128:8], in0=Z4[:, :, :, j:128:8], in1=ta[:], op=ALU.add),
                         reads=[B_st], writes=[B_BIG])
                P.op(dve, lambda e: e.tensor_copy(out=carry[:], in_=Z4[:, :, :, 127]), reads=[B_BIG], writes=[B_car])
                P.op(act, lambda e: e.activation(out=Sb4[:, :, :, 1:128], in_=Z4[:, :, :, 0:127], func=AF.Copy),
                     reads=[B_BIG], writes=[B_Xg])
                NQ = 4
                Ysl = {}

                def G0(gb):
                    pt, pb = next_ps()
                    Ysl[gb] = (pt, pb)
                    for j in range(4):
                        g = gb * 4 + j
                        g2, gp = g // 32, g % 32
                        sl = slice(64 * g2, 64 * g2 + 64)
                        o = pt[:, j * 128:(j + 1) * 128]
                        P.op(pe, lambda e, o=o, g=g: e.matmul(out=o, lhsT=Kb[:, g, :], rhs=U[:, g, :], start=True, stop=False),
                             reads=[B_U, B_K], writes=[pb], inc=False)
                        P.op(pe, lambda e, o=o, sl=sl, gp=gp: e.matmul(out=o, lhsT=WyR[sl, gp, :], rhs=Sb4[sl, 0, gp, :], start=False, stop=False),
                             reads=[B_Xg, B_t], writes=[pb], inc=False)
                        P.op(pe, lambda e, o=o, sl=sl, gp=gp: e.matmul(out=o, lhsT=WyI[sl, gp, :], rhs=Sb4[sl, 1, gp, :], start=False, stop=True),
                             reads=[B_Xg, B_t], writes=[pb], inc=(j == 3))

                def G1(gb):
                    pt, pb = Ysl[gb]
                    q = gb % NQ
                    P.op(act, lambda e: e.activation(out=ysb[q][:], in_=pt[:, :], func=AF.Copy), reads=[pb], writes=[B_ysb[q]])
                    P.op(act, lambda e: e.activation(out=tq0[q][:], in_=pt[:, :], func=AF.Square), reads=[pb], writes=[B_tq0[q]])

                def G2(gb):
                    q = gb % NQ
                    P.op(dve, lambda e: e.tensor_scalar(out=tq0[q][:], in0=tq0[q][:], scalar1=0.044715, scalar2=1.0, op0=ALU.mult, op1=ALU.add),
                         reads=[B_tq0[q]], writes=[B_tq0[q]])
                    P.op(dve, lambda e: e.tensor_tensor(out=tq0[q][:], in0=ysb[q][:], in1=tq0[q][:], op=ALU.mult),
                         reads=[B_tq0[q], B_ysb[q]], writes=[B_tq0[q]])

                def G3(gb):
                    q = gb % NQ
                    P.op(act, lambda e: e.activation(out=tq0[q][:], in_=tq0[q][:], func=AF.Sigmoid, scale=GELU_C), reads=[B_tq0[q]], writes=[B_tq0[q]])

                def G4(gb):
                    q = gb % NQ
                    k = gb % 2
                    P.op(dve, lambda e: e.tensor_tensor(out=hT[k][:], in0=ysb[q][:], in1=tq0[q][:], op=ALU.mult),
                         reads=[B_tq0[q], B_ysb[q]], writes=[B_hT[k]])

                def G5(gb):
                    k = gb % 2
                    pt2, pb2 = next_ps()
                    Ysl[("t", gb)] = (pt2, pb2)
                    pt2b = pt2[:, :].bitcast(BF16)
                    for j in range(4):
                        P.op(pe, lambda e, j=j: e.transpose(
                            out=pt2b[:, j * 128:(j + 1) * 128], in_=hT[k][:, j * 128:(j + 1) * 128], identity=ident_b[:]),
                            reads=[B_hT[k], B_const], writes=[pb2], inc=(j == 3))

                def G6(gb):
                    pt2, pb2 = Ysl[("t", gb)]
                    pt2b = pt2[:, :].bitcast(BF16)
                    P.op(act, lambda e: e.activation(
                        out=HTc[:, :, gb * 64:(gb + 1) * 64].rearrange("q l (g h) -> q l g h", g=4),
                        in_=pt2b[:, 0:512].rearrange("q (g l h) -> q l g h", g=4, l=8), func=AF.Copy),
                        reads=[pb2], writes=[B_HTc])

                gst = [G0, G1, G2, G3, G4, G5, G6]
                for tick in range(16 + len(gst) - 1):
                    for si, fn in enumerate(gst):
                        gb = tick - si
                        if 0 <= gb < 16:
                            fn(gb)
                for dt_i in range(8):
                    for lh in range(2):
                        pt, pb = next_ps()
                        ptb = pt[:, :].bitcast(BF16)
                        for j in range(4):
                            l = lh * 4 + j
                            P.op(pe, lambda e, j=j, l=l, dt_i=dt_i, ptb=ptb: e.transpose(
                                out=ptb[:, j * 128:(j + 1) * 128], in_=HTc[:, l, dt_i * 128:(dt_i + 1) * 128], identity=ident_b[:]),
                                reads=[B_HTc, B_const], writes=[pb], inc=(j == 3))
                        if lh == 0:
                            P.op(act, lambda e, dt_i=dt_i, lh=lh, ptb=ptb: e.activation(
                                out=HTs[:, dt_i, lh * 512:(lh + 1) * 512], in_=ptb[:, 0:512], func=AF.Copy),
                                reads=[pb], writes=[B_HTs])
                        else:
                            P.op(dve, lambda e, dt_i=dt_i, lh=lh, ptb=ptb: e.tensor_copy(
                                out=HTs[:, dt_i, lh * 512:(lh + 1) * 512], in_=ptb[:, 0:512]),
                                reads=[pb], writes=[B_HTs])
                P.dma(sp, HTd[:, seg * 1024:(seg + 1) * 1024].rearrange("(dt q) t -> q dt t", q=128), HTs[:, :, :], s_ht,
                      reads=[B_HTs], writes=[B_HTd])
            P.barrier()
        sS.close()

        def dense_phase(layer):
            with ExitStack() as es:
                NWS = 4
                wslot = [sbt(es, "wslot%d" % i, [128, 11, 512], BF16) for i in range(NWS)]
                wsem = [P.newsem("ws%d" % i) for i in range(NWS)]
                wbuf = [Buf("wslot%d" % i) for i in range(NWS)]
                w_rr = [0]

                def wload(src_ap, nkt, wb_buf, ncols=512):
                    i = w_rr[0] % NWS
                    w_rr[0] += 1
                    Q = sp
                    P.dma(Q, wslot[i][:, 0:nkt, 0:ncols], src_ap.rearrange("(kt q) n -> q kt n", q=128), wsem[i],
                          reads=[wb_buf], writes=[wbuf[i]])
                    return wslot[i], wbuf[i]

                lnt = {k: sbt(es, "lnt_%s" % k, [128, D], F32) for k in ("mg", "mb", "fg", "fb")}
                B_ln = Buf("ln")
                s_ln = P.newsem("sln")
                xt = sbt(es, "xt", [128, 4, D], F32)
                rr_ = sbt(es, "rr_", [128, 4, D], F32)
                xin = sbt(es, "xin", [128, 4, D], F32)
                xnb = [sbt(es, "xnb%d" % i, [128, D], BF16) for i in range(2)]
                inT = sbt(es, "inT", [128, 8, 512], BF16)
                zT = sbt(es, "zT", [128, 8, 512], BF16)
                x1T = sbt(es, "x1T", [128, 8, 512], BF16)
                h2T = sbt(es, "h2T", [128, 22, 512], BF16)
                sgt = [sbt(es, "sgt%d" % i, [128, 512], F32) for i in range(2)]
                stats = [sbt(es, "stats%d" % i, [128, 2, 6], F32) for i in range(4)]
                mv = [sbt(es, "mv%d" % i, [128, 2], F32) for i in range(4)]
                rstd = [sbt(es, "rstd%d" % i, [128, 1], F32) for i in range(4)]
                nmr = [sbt(es, "nmr%d" % i, [128, 1], F32) for i in range(4)]
                B_xt = [Buf("xt%d" % i) for i in range(4)]; B_rr = [Buf("rr%d" % i) for i in range(4)]
                B_xin = [Buf("xin%d" % i) for i in range(4)]
                B_xnb = [Buf("xnb0"), Buf("xnb1")]; B_inT = Buf("inT"); B_zT = Buf("zT")
                B_x1T = Buf("x1T"); B_h2T = Buf("h2T"); B_sg = [Buf(), Buf()]; B_stat = [Buf("stat%d" % i) for i in range(4)]
                xnb_rr = [0]
                s_xt = P.newsem("sxt"); s_in = P.newsem("sin")
                sg_rr = [0]
                pending = []

                def flush_pending():
                    while pending:
                        pending.pop(0)()

                gT = sbt(es, "gT", [128, 8], F32); bT = sbt(es, "bT", [128, 8], F32)

                def load_ln(layer):
                    for k, nm in (("mg", "ln_mix_g"), ("mb", "ln_mix_b"), ("fg", "ln_ffn_g"), ("fb", "ln_ffn_b")):
                        P.dma(sp, lnt[k][:, :], lnp[(nm, layer)].partition_broadcast(128), s_ln, writes=[B_ln])
                    P.dma(sp, gT[:, :], lnp[("ln_mix_g", layer)].rearrange("(dt q) -> q dt", q=128), s_ln, writes=[B_ln],
                          allow_slow_non_contiguous=True)
                    P.dma(sp, bT[:, :], lnp[("ln_mix_b", layer)].rearrange("(dt q) -> q dt", q=128), s_ln, writes=[B_ln],
                          allow_slow_non_contiguous=True)

                def fm_gated(src_T, B_src, wsrc, wkey, nkt, n_half, func, dst_T, B_dst):
                    ncol = n_half * 128
                    for c0 in range(0, n_half, 4):
                        nt = min(4, n_half - c0)
                        sa, ba = wload(wsrc[:, c0 * 128:(c0 + nt) * 128], nkt, WB[wkey], ncols=nt * 128)
                        sb_, bb = wload(wsrc[:, ncol + c0 * 128:ncol + (c0 + nt) * 128], nkt, WB[wkey], ncols=nt * 128)
                        for j in range(nt):
                            pa, pba = next_ps()
                            pb_, pbb = next_ps()
                            for (slot, sbuf, pt, pbuf) in ((sa, ba, pa, pba), (sb_, bb, pb_, pbb)):
                                for kt in range(nkt):
                                    P.op(pe, lambda e, slot=slot, pt=pt, kt=kt, j=j: e.matmul(
                                        out=pt[:, :], lhsT=slot[:, kt, j * 128:(j + 1) * 128], rhs=src_T[:, kt, :],
                                        start=(kt == 0), stop=(kt == nkt - 1)),
                                        reads=[sbuf, B_src], writes=[pbuf], inc=(kt == nkt - 1))
                            i = sg_rr[0] % 2
                            sg_rr[0] += 1
                            if func == "glu":
                                gate_ps, gate_b, oth_ps, oth_b = pb_, pbb, pa, pba
                                f = AF.Sigmoid
                            else:
                                gate_ps, gate_b, oth_ps, oth_b = pa, pba, pb_, pbb
                                f = AF.Silu
                            P.op(act, lambda e, i=i, gate_ps=gate_ps, f=f: e.activation(out=sgt[i][:], in_=gate_ps[:, :], func=f),
                                 reads=[gate_b], writes=[B_sg[i]])
                            P.op(dve, lambda e, i=i, oth_ps=oth_ps, c0=c0, j=j: e.tensor_tensor(
                                out=dst_T[:, c0 + j, :], in0=oth_ps[:, :], in1=sgt[i][:], op=ALU.mult),
                                reads=[B_sg[i], oth_b], writes=[B_dst])
                            if pending:
                                pending.pop(0)()

                def tm_proj_resid(src_T, B_src, wsrc, wkey, nkt, res, B_res):
                    khalves = [(0, nkt)] if nkt <= 11 else [(0, 11), (11, nkt)]
                    for ch in range(2):
                        pts = [next_ps() for _ in range(4)]
                        for hi, (k0, k1) in enumerate(khalves):
                            slot, sbuf = wload(wsrc[k0 * 128:k1 * 128, ch * 512:(ch + 1) * 512], k1 - k0, WB[wkey])
                            for st in range(4):
                                pt, pbuf = pts[st]
                                for kt in range(k0, k1):
                                    last = (kt == nkt - 1)
                                    P.op(pe, lambda e, slot=slot, pt=pt, kt=kt, k0=k0, st=st, last=last: e.matmul(
                                        out=pt[:, :], lhsT=src_T[:, kt, st * 128:(st + 1) * 128], rhs=slot[:, kt - k0, :],
                                        start=(kt == 0), stop=last),
                                        reads=[sbuf, B_src], writes=[pbuf], inc=(kt == k1 - 1))
                        flush_pending()
                        for st in range(4):
                            pt, pbuf = pts[st]
                            P.op(dve, lambda e, st=st, ch=ch, pt=pt: e.scalar_tensor_tensor(
                                out=rr_[:, st, ch * 512:(ch + 1) * 512], in0=res[:, st, ch * 512:(ch + 1) * 512], scalar=ALPHA,
                                in1=pt[:, :], op0=ALU.mult, op1=ALU.add), reads=[pbuf, B_res[st]], writes=[B_rr[st]])

                def layer_norm(gk, bk, dstT, B_dstT, store_rows=None, store_buf=None, store_sem=None):
                    R4 = range(4)
                    for st in R4:
                        for ch in range(2):
                            P.op(dve, lambda e, st=st, ch=ch: e.bn_stats(out=stats[st][:, ch, :], in_=rr_[:, st, ch * 512:(ch + 1) * 512]),
                                 reads=[B_rr[st]], writes=[B_stat[st]])
                    for st in R4:
                        P.op(dve, lambda e, st=st: e.bn_aggr(out=mv[st][:, :], in_=stats[st][:, :, :].rearrange("q a b -> q (a b)")),
                             reads=[B_stat[st]], writes=[B_stat[st]])
                    for st in R4:
                        P.op(dve, lambda e, st=st: e.tensor_scalar(out=rstd[st][:], in0=mv[st][:, 1:2], scalar1=EPS, scalar2=None, op0=ALU.add),
                             reads=[B_stat[st]], writes=[B_stat[st]])
                    for st in R4:
                        P.op(act, lambda e, st=st: e.activation(out=rstd[st][:], in_=rstd[st][:], func=AF.Sqrt), reads=[B_stat[st]], writes=[B_stat[st]])
                    for st in R4:
                        P.op(dve, lambda e, st=st: e.reciprocal(out=rstd[st][:], in_=rstd[st][:]), reads=[B_stat[st]], writes=[B_stat[st]])
                    for st in R4:
                        P.op(dve, lambda e, st=st: e.scalar_tensor_tensor(out=nmr[st][:], in0=mv[st][:, 0:1], scalar=-1.0, in1=rstd[st][:], op0=ALU.mult, op1=ALU.mult),
                             reads=[B_stat[st]], writes=[B_stat[st]])
                    if dstT is not None:
                        tps = {}
                        for st in R4:
                            xi_ = st % 2
                            P.op(act, lambda e, st=st, xi_=xi_: e.activation(out=xnb[xi_][:, :], in_=rr_[:, st, :], func=AF.Identity,
                                                                            scale=rstd[st][:, 0:1], bias=nmr[st][:, 0:1]),
                                 reads=[B_stat[st], B_rr[st]], writes=[B_xnb[xi_]])
                            for hh in range(2):
                                pt, pbuf = next_ps()
                                tps[(st, hh)] = (pt, pbuf)
                                ptb = pt[:, :].bitcast(BF16)
                                for j in range(4):
                                    dtile = hh * 4 + j
                                    P.op(pe, lambda e, j=j, dtile=dtile, ptb=ptb, xi_=xi_: e.transpose(
                                        out=ptb[:, j * 128:(j + 1) * 128], in_=xnb[xi_][:, dtile * 128:(dtile + 1) * 128], identity=ident_b[:]),
                                        reads=[B_xnb[xi_], B_const], writes=[pbuf], inc=(j == 3))
                            if st >= 1:
                                evac_T(st - 1, tps, dstT, B_dstT)
                        evac_T(3, tps, dstT, B_dstT)
                    def tail(st):
                        P.op(act, lambda e: e.activation(out=rr_[:, st, :], in_=rr_[:, st, :], func=AF.Identity, scale=rstd[st][:, 0:1], bias=nmr[st][:, 0:1]),
                             reads=[B_stat[st], B_rr[st]], writes=[B_rr[st]])
                        P.op(dve, lambda e: e.tensor_tensor(out=rr_[:, st, :], in0=rr_[:, st, :], in1=lnt[gk][:, :], op=ALU.mult),
                             reads=[B_rr[st], B_ln], writes=[B_rr[st]])
                        P.op(dve, lambda e: e.tensor_tensor(out=xt[:, st, :], in0=rr_[:, st, :], in1=lnt[bk][:, :], op=ALU.add),
                             reads=[B_rr[st], B_ln], writes=[B_xt[st]])
                        if store_rows is not None:
                            P.dma(pool, store_rows(st), xt[:, st, :], store_sem, reads=[B_xt[st]], writes=[store_buf])
                    for st in R4:
                        pending.append(lambda st=st: tail(st))

                def evac_T(st, tps, dstT, B_dstT):
                    for hh in range(2):
                        pt, pbuf = tps[(st, hh)]
                        ptb = pt[:, :].bitcast(BF16)
                        for j in range(4):
                            dtile = hh * 4 + j
                            P.op(dve, lambda e, j=j, dtile=dtile, ptb=ptb, st=st: e.tensor_scalar(
                                out=dstT[:, dtile, st * 128:(st + 1) * 128], in0=ptb[:, j * 128:(j + 1) * 128],
                                scalar1=gT[:, dtile:dtile + 1], scalar2=bT[:, dtile:dtile + 1], op0=ALU.mult, op1=ALU.add),
                                reads=[pbuf, B_ln], writes=[B_dstT])

                def ffn(layer):
                    fm_gated(x1T, B_x1T, wb_gu[layer], "gu%d" % layer, 8, 22, "swiglu", h2T, B_h2T)
                    tm_proj_resid(h2T, B_h2T, wb_dn[layer], "dn%d" % layer, 22, xt, B_xt)


                load_ln(layer)

                def rows(base, grp, st):
                    if layer == 0:
                        seg, lq = grp // 2, grp % 2
                        r0 = seg * 1024 + lq * 4 + st
                        return base[r0:r0 + 1017:8, :]
                    tt = grp * 4 + st
                    return base[tt * 128:(tt + 1) * 128, :]

                def load_inputs(grp):
                    src_fm = HTd if layer == 0 else attTd
                    B_srcfm = B_HTd if layer == 0 else B_attTd
                    P.dma(sp, inT[:, :, :], src_fm[:, grp * 512:(grp + 1) * 512].rearrange("(dt q) t -> q dt t", q=128), s_in,
                          reads=[B_srcfm], writes=[B_inT])
                    for st in range(4):
                        if layer == 0:
                            P.dma(sp, xin[:, st, :], rows(x, grp, st), s_xt, writes=[B_xin[st]])
                        else:
                            P.dma(sp, xin[:, st, :], rows(x2d, grp, st), s_xt, reads=[B_x2d], writes=[B_xin[st]])
                    for st in range(4):
                        B_xin[st].w = {s_xt: s_xt.v}

                load_inputs(0)
                for grp in range(8):
                    if layer == 0:
                        fm_gated(inT, B_inT, wb_glu, "glu", 8, 8, "glu", zT, B_zT)
                        tm_proj_resid(zT, B_zT, wb_out, "out", 8, xin, B_xin)
                    else:
                        tm_proj_resid(inT, B_inT, wb_o, "o", 8, xin, B_xin)
                    if grp + 1 < 8:
                        load_inputs(grp + 1)
                    layer_norm("mg", "mb", x1T, B_x1T)
                    ffn(layer)
                    if layer == 0:
                        layer_norm("fg", "fb", None, None, store_rows=lambda st, grp=grp: rows(x2d, grp, st), store_buf=B_x2d, store_sem=s_o)
                    else:
                        layer_norm("fg", "fb", None, None, store_rows=lambda st, grp=grp: rows(y, grp, st), store_buf=B_y, store_sem=s_o)
                flush_pending()
                P.barrier()

        dense_phase(0)

        with ExitStack() as ea:
            x2T = sbt(ea, "x2T", [128, 8, L], BF16)
            attT = [sbt(ea, "attT%d" % i, [128, L], BF16) for i in range(2)]
            B_x2T = Buf("x2T"); B_attT = [Buf("attT0"), Buf("attT1")]
            rr_ = sbt(ea, "rrA", [128, 2, D], F32); xnbA = [sbt(ea, "xnbA%d" % i, [128, D], BF16) for i in range(2)]
            B_rrs = [Buf("rrA0"), Buf("rrA1")]; B_xnbA = [Buf("xnbA0"), Buf("xnbA1")]
            s_xtA = [P.newsem("sxtA0"), P.newsem("sxtA1")]; s_at = P.newsem("sat")
            aw = [sbt(ea, "aw%d" % i, [128, 8, 128], BF16) for i in range(3)]
            awsem = [P.newsem("aw%d" % i) for i in range(3)]
            awbuf = [Buf("aw%d" % i) for i in range(3)]
            aw_rr = [0]

            def wload(src_ap, nkt, wb_buf, ncols=128):
                i = aw_rr[0] % 3
                aw_rr[0] += 1
                P.dma(sp, aw[i][:, 0:nkt, 0:ncols], src_ap.rearrange("(kt q) n -> q kt n", q=128), awsem[i],
                      reads=[wb_buf], writes=[awbuf[i]])
                return aw[i], awbuf[i]
            def X0(tt):
                st = tt % 2
                P.dma(sp, rr_[:, st, :], x2d[tt * 128:(tt + 1) * 128, :], s_xtA[st], reads=[B_x2d], writes=[B_rrs[st]])

            def X1(tt):
                st = tt % 2
                P.op(act, lambda e: e.activation(out=xnbA[st][:, :], in_=rr_[:, st, :], func=AF.Copy), reads=[B_rrs[st]], writes=[B_xnbA[st]])

            xps = {}

            def X2(tt):
                st = tt % 2
                for hh in range(2):
                    pt, pbuf = next_ps()
                    xps[(tt, hh)] = (pt, pbuf)
                    ptb = pt[:, :].bitcast(BF16)
                    for j in range(4):
                        dtile = hh * 4 + j
                        P.op(pe, lambda e, j=j, dtile=dtile, ptb=ptb: e.transpose(
                            out=ptb[:, j * 128:(j + 1) * 128], in_=xnbA[st][:, dtile * 128:(dtile + 1) * 128], identity=ident_b[:]),
                            reads=[B_xnbA[st], B_const], writes=[pbuf], inc=(j == 3))

            def X3(tt):
                for hh in range(2):
                    pt, pbuf = xps[(tt, hh)]
                    ptb = pt[:, :].bitcast(BF16)
                    E = dve if hh == 0 else act
                    if hh == 0:
                        P.op(dve, lambda e, hh=hh, ptb=ptb: e.tensor_copy(
                            out=x2T[:, hh * 4:(hh + 1) * 4, tt * 128:(tt + 1) * 128],
                            in_=ptb[:, 0:512].rearrange("q (j t) -> q j t", j=4)), reads=[pbuf], writes=[B_x2T])
                    else:
                        P.op(act, lambda e, hh=hh, ptb=ptb: e.activation(
                            out=x2T[:, hh * 4:(hh + 1) * 4, tt * 128:(tt + 1) * 128],
                            in_=ptb[:, 0:512].rearrange("q (j t) -> q j t", j=4), func=AF.Copy), reads=[pbuf], writes=[B_x2T])

            xst = [X0, X1, X2, X3]
            for tick in range(32 + len(xst) - 1):
                for si, fn in enumerate(xst):
                    tt = tick - si
                    if 0 <= tt < 32:
                        fn(tt)
            QT2 = [sbt(ea, "QT%d" % i, [128, L], BF16) for i in range(2)]
            KT2 = [sbt(ea, "KT%d" % i, [128, L], BF16) for i in range(2)]
            V2 = [sbt(ea, "V%d" % i, [128, 32, 128], BF16) for i in range(2)]
            B_QT2 = [Buf("QT0"), Buf("QT1")]; B_KT2 = [Buf("KT0"), Buf("KT1")]; B_V2 = [Buf("V0"), Buf("V1")]
            NBUF = 4
            SUBE = pool
            sig = [sbt(ea, "sig%d" % i, [128, 512], F32) for i in range(NBUF)]
            NIPX = 2
            ipx = [sbt(ea, "ipx%d" % i, [128, L + 4], F32) for i in range(NIPX)]
            wq = [sbt(ea, "wq%d" % i, [128, 512], BF16) for i in range(NBUF)]
            wTs = [sbt(ea, "wTs%d" % i, [128, 512], BF16) for i in range(NBUF)]
            B_sig = [Buf() for _ in range(NBUF)]; B_ipxc = [[Buf() for _ in range(10)] for _ in range(NIPX)]
            B_wq = [Buf() for _ in range(NBUF)]; B_wTs = [Buf() for _ in range(NBUF)]
            mlow = sbt(ea, "mlow", [128, 128], F32); mup = sbt(ea, "mup", [128, 128], F32)
            P.op(pool, lambda e: e.memset(mlow[:], 1.0), writes=[B_const])
            P.op(pool, lambda e: e.affine_select(out=mlow[:], in_=mlow[:], pattern=[[-1, 128]], base=0,
                                                 channel_multiplier=1, compare_op=ALU.is_gt, fill=0.0), writes=[B_const])
            P.op(pool, lambda e: e.memset(mup[:], 0.0), writes=[B_const])
            P.op(pool, lambda e: e.affine_select(out=mup[:], in_=mup[:], pattern=[[-1, 128]], base=0,
                                                 channel_multiplier=1, compare_op=ALU.is_gt, fill=1.0), writes=[B_const])
            negm = sbt(ea, "negm", [128, 128], BF16)
            P.op(pool, lambda e: e.memset(dummy[:, 0:1], 0.0), writes=[B_dummy])
            P.op(dve, lambda e: e.tensor_scalar(out=negm[:], in0=mup[:], scalar1=-30000.0, scalar2=None, op0=ALU.mult),
                 reads=[B_const], writes=[B_const])
            NROT[0] = 5
            bgp, bgb = psf[5], psb[5]

            def proj_gen(hp):
                pb_ = hp % 2
                QTd, KTd, Vd = QT2[pb_], KT2[pb_], V2[pb_]
                sq, bq = wload(wb_qkv[:, hp * 128:(hp + 1) * 128], 8, WB["qkv"], ncols=128)
                sk, bk_ = wload(wb_qkv[:, D + hp * 128:D + (hp + 1) * 128], 8, WB["qkv"], ncols=128)
                sv, bv = wload(wb_qkv[:, 2 * D + hp * 128:2 * D + (hp + 1) * 128], 8, WB["qkv"], ncols=128)
                for tb in range(8):
                    for (slot, sbuf, dstT, B_dstT, scl) in ((sq, bq, QTd, B_QT2[pb_], 0.125), (sk, bk_, KTd, B_KT2[pb_], 1.0)):
                        for kt in range(8):
                            P.op(pe, lambda e, slot=slot, kt=kt, tb=tb: e.matmul(
                                out=bgp[:, :], lhsT=slot[:, kt, 0:128], rhs=x2T[:, kt, tb * 512:(tb + 1) * 512],
                                start=(kt == 0), stop=(kt == 7)), reads=[sbuf, B_x2T], writes=[bgb], inc=(kt == 7))
                        yield
                        P.op(act, lambda e, dstT=dstT, tb=tb, scl=scl: e.activation(
                            out=dstT[:, tb * 512:(tb + 1) * 512], in_=bgp[:, :], func=AF.Copy, scale=scl),
                            reads=[bgb], writes=[B_dstT])
                        yield
                    for j in range(4):
                        kt_i = tb * 4 + j
                        for kt in range(8):
                            P.op(pe, lambda e, kt=kt, kt_i=kt_i: e.matmul(
                                out=bgp[:, 0:128], lhsT=x2T[:, kt, kt_i * 128:(kt_i + 1) * 128], rhs=sv[:, kt, 0:128],
                                start=(kt == 0), stop=(kt == 7)), reads=[bv, B_x2T], writes=[bgb], inc=(kt == 7))
                        yield
                        P.op(act, lambda e, kt_i=kt_i: e.activation(
                            out=Vd[:, kt_i, :], in_=bgp[:, 0:128], func=AF.Copy),
                            reads=[bgb], writes=[B_V2[pb_]])
                        yield

            for _ in proj_gen(0):
                pass
            for hp in range(8):
                QT, KT, V = QT2[hp % 2], KT2[hp % 2], V2[hp % 2]
                B_QT, B_KT, B_V = B_QT2[hp % 2], B_KT2[hp % 2], B_V2[hp % 2]
                bg = proj_gen(hp + 1) if hp + 1 < 8 else iter(())
                tasks = []
                for h2 in range(2):
                    for qt in range(32):
                        hi = 128 * (qt + 1)
                        chunks = []
                        while hi > 0:
                            lo = ((hi - 1) // 512) * 512
                            chunks.append((lo, hi))
                            hi = lo
                        poh = {}
                        for ci_, (lo, hi) in enumerate(chunks):
                            tasks.append(dict(h2=h2, qt=qt, ci=ci_, lo=lo, hi=hi, n=len(chunks), poh=poh, bi=(h2 * 32 + qt) % NIPX))

                def S0(k, T):
                    hs = slice(64 * T["h2"], 64 * T["h2"] + 64)
                    W = T["hi"] - T["lo"]
                    pz, pzb = next_ps()
                    T["pz"], T["pzb"] = pz, pzb
                    qt, lo, hi = T["qt"], T["lo"], T["hi"]
                    if T["ci"] == 0:
                        P.op(pe, lambda e: e.matmul(out=pz[:, 0:W], lhsT=QT[hs, qt * 128:(qt + 1) * 128], rhs=KT[hs, lo:hi], start=True, stop=False),
                             reads=[B_QT, B_KT], writes=[pzb], inc=False)
                        P.op(pe, lambda e: e.matmul(out=pz[:, W - 128:W], lhsT=ident_b[:], rhs=negm[:], start=False, stop=True),
                             reads=[B_const], writes=[pzb])
                    else:
                        P.op(pe, lambda e: e.matmul(out=pz[:, 0:W], lhsT=QT[hs, qt * 128:(qt + 1) * 128], rhs=KT[hs, lo:hi], start=True, stop=True),
                             reads=[B_QT, B_KT], writes=[pzb])

                def S1(k, T):
                    i = k % NBUF
                    W = T["hi"] - T["lo"]
                    pz, pzb = T["pz"], T["pzb"]
                    P.op(act, lambda e: e.activation(out=sig[i][:, 0:W], in_=pz[:, 0:W], func=AF.Sigmoid, scale=-1.0),
                         reads=[pzb], writes=[B_sig[i]])

                def S2(k, T):
                    i = k % NBUF
                    b = T["bi"]
                    lo, hi = T["lo"], T["hi"]
                    W = hi - lo
                    tk = B_ipxc[b][lo // 512]
                    tc = B_ipxc[b][hi // 512]
                    if T["ci"] == 0:
                        P.op(dve, lambda e: e.memset(ipx[b][:, hi:hi + 1], 1.0), writes=[tc])
                    P.op(dve, lambda e: e.tensor_tensor_scan(
                        out=ipx[b][:, lo:hi][:, ::-1], data0=sig[i][:, 0:W][:, ::-1], data1=sig[i][:, 0:W][:, ::-1],
                        initial=ipx[b][:, hi:hi + 1], op0=ALU.mult, op1=ALU.bypass), reads=[B_sig[i], tc], writes=[tk])
                    P.op(pool if (k % 16) != 15 else dve, lambda e: e.tensor_tensor(out=wq[i][:, 0:W], in0=ipx[b][:, lo + 1:hi + 1], in1=ipx[b][:, lo:hi], op=ALU.subtract),
                         reads=[tk, tc], writes=[B_wq[i]])

                def S3(k, T):
                    i = k % NBUF
                    W = T["hi"] - T["lo"]
                    nb = W // 128
                    pw, pwb = next_ps()
                    T["pw"], T["pwb"] = pw, pwb
                    pwb16 = pw[:, :].bitcast(BF16)
                    for j in range(nb):
                        P.op(pe, lambda e, j=j: e.transpose(
                            out=pwb16[:, j * 128:(j + 1) * 128], in_=wq[i][:, j * 128:(j + 1) * 128], identity=ident_b[:]),
                            reads=[B_wq[i], B_const], writes=[pwb], inc=(j == nb - 1))

                def S4(k, T):
                    i = k % NBUF
                    W = T["hi"] - T["lo"]
                    pwb16 = T["pw"][:, :].bitcast(BF16)
                    P.op(act, lambda e: e.activation(out=wTs[i][:, 0:W], in_=pwb16[:, 0:W], func=AF.Copy),
                         reads=[T["pwb"]], writes=[B_wTs[i]])

                def S5(k, T):
                    i = k % NBUF
                    hs = slice(64 * T["h2"], 64 * T["h2"] + 64)
                    W = T["hi"] - T["lo"]
                    nb = W // 128
                    if T["ci"] == 0:
                        T["poh"]["po"] = next_ps(long=True)
                    po, pob = T["poh"]["po"]
                    for j in range(nb):
                        ktile = T["lo"] // 128 + j
                        first = (T["ci"] == 0 and j == 0)
                        last = (T["ci"] == T["n"] - 1 and j == nb - 1)
                        P.op(pe, lambda e, j=j, ktile=ktile, first=first, last=last: e.matmul(
                            out=po[hs, 0:128], lhsT=V[:, ktile, hs], rhs=wTs[i][:, j * 128:(j + 1) * 128],
                            start=first, stop=last), reads=[B_wTs[i], B_V], writes=[pob], inc=(j == nb - 1))
                    if T["ci"] == T["n"] - 1:
                        qt = T["qt"]
                        P.op(act, lambda e: e.activation(
                            out=attT[hp % 2][hs, qt * 128:(qt + 1) * 128], in_=po[hs, 0:128], func=AF.Copy),
                            reads=[pob], writes=[B_attT[hp % 2]])

                stages = [S0, S1, S2, S3, S4, S5]
                offs = [0, 1, 3, 5, 6, 8]
                for tick in range(len(tasks) + offs[-1]):
                    for si, fn in enumerate(stages):
                        k = tick - offs[si]
                        if 0 <= k < len(tasks):
                            fn(k, tasks[k])
                    if tick % 3 == 1:
                        next(bg, None)
                for _ in bg:
                    pass
                P.dma(pool, attTd[hp * 128:(hp + 1) * 128, :], attT[hp % 2][:, :], s_at, reads=[B_attT[hp % 2]], writes=[B_attTd])
            P.barrier()


        NROT[0] = 6
        dense_phase(1)
        sp.wait(s_o, s_o.v)
    return nc


_NC = None


def kernel(**inputs):
    global _NC
    if _NC is None:
        _NC = build_program()
    nc = _NC
    f = lambda a: np.ascontiguousarray(np.asarray(a, dtype=np.float32))
    shared = {
        "ssm_a_re": f(inputs["ssm_a_re"][0]), "ssm_a_im": f(inputs["ssm_a_im"][0]),
        "ssm_log_dt": f(inputs["ssm_log_dt"][0]),
        "ssm_b_re": f(inputs["ssm_b_re"][0]), "ssm_b_im": f(inputs["ssm_b_im"][0]),
        "ssm_c_re": f(inputs["ssm_c_re"][0]), "ssm_c_im": f(inputs["ssm_c_im"][0]),
        "ssm_d": f(inputs["ssm_d"][0]),
        "ssm_w_glu": f(inputs["ssm_w_glu"][0]), "ssm_w_out": f(inputs["ssm_w_out"][0]),
        "sb_w_qkv": f(inputs["sb_w_qkv"][0]), "sb_w_o": f(inputs["sb_w_o"][0]),
    }
    for i in range(2):
        shared["ffn_w_gu%d" % i] = f(inputs["ffn_w_gu"][i])
        shared["ffn_w_down%d" % i] = f(inputs["ffn_w_down"][i])
        for nm in ("ln_mix_g", "ln_mix_b", "ln_ffn_g", "ln_ffn_b"):
            shared["%s%d" % (nm, i)] = f(inputs[nm][i])
    xs = np.asarray(inputs["x"], dtype=np.float32)
    in_maps = []
    for b in range(8):
        m = dict(shared)
        m["x"] = np.ascontiguousarray(xs[b])
        in_maps.append(m)
    res = run_bass_kernel_spmd(nc, in_maps, core_ids=list(range(8)))
    return np.stack([np.asarray(r["y"], dtype=np.float32) for r in res.results], axis=0)
```
